# Optimizing a Trainium2 kernel written in Bass

```python
import jax, jax.numpy as jnp
from jax import lax
import numpy as np

D_MODEL = 1024
BATCH = 32
SEQ = 2048
DEPTH = 4

N_MIXERS = 2
N_ATTN_LAYERS = (DEPTH + 1) // 2
N_RNN_LAYERS = DEPTH // 2
ATTN_HEADS = 16
ATTN_HEAD_DIM = D_MODEL // ATTN_HEADS
Q_BLOCK = 128
RNN_WIDTH = 1280
RNN_HEADS = 16
RNN_BLOCK = RNN_WIDTH // RNN_HEADS
RNN_CONV = 4
LRU_C = 8.0
FFN_DIM = 2816
FFN_CONV = 3
PLE_DIM = 256
EPS = 1e-6

kernel_name = "hybrid_stickbreak_rglru_convffn"


def rms_norm(x, g):
    xf = x.astype(jnp.float32)
    y = xf * lax.rsqrt(jnp.mean(xf * xf, axis=-1, keepdims=True) + EPS)
    return (y * g.astype(jnp.float32)).astype(x.dtype)


def causal_depthwise_conv(x, w, b):
    k = w.shape[0]
    c = x.shape[-1]
    y = lax.conv_general_dilated(
        x, w[:, None, :].astype(x.dtype), window_strides=(1,), padding=[(k - 1, 0)],
        dimension_numbers=("NWC", "WIO", "NWC"), feature_group_count=c)
    return y + b.astype(x.dtype)


def stick_breaking_attention(h, w_qkv, w_o):
    b, s, _ = h.shape
    q, k, v = jnp.split(h @ w_qkv, 3, axis=-1)

    def heads(t):
        return t.reshape(b, s, ATTN_HEADS, ATTN_HEAD_DIM).transpose(0, 2, 1, 3).astype(jnp.float32)

    q, k, v = heads(q), heads(k), heads(v)
    scale = ATTN_HEAD_DIM ** -0.5
    outs = []
    for start in range(0, s, Q_BLOCK):
        end = start + Q_BLOCK
        qb = q[:, :, start:end]
        kb = k[:, :, :end]
        vb = v[:, :, :end]
        z = jnp.einsum("bhqd,bhkd->bhqk", qb, kb) * scale
        t_idx = jnp.arange(start, end)[:, None]
        s_idx = jnp.arange(end)[None, :]
        causal = s_idx < t_idx
        log_keep = jnp.where(causal, jax.nn.log_sigmoid(-z), 0.0)
        rest = lax.cumsum(log_keep, axis=3, reverse=True) - log_keep
        weights = jnp.where(causal, jnp.exp(jax.nn.log_sigmoid(z) + rest), 0.0)
        outs.append(jnp.einsum("bhqk,bhkd->bhqd", weights, vb))
    o = jnp.concatenate(outs, axis=2)
    o = o.transpose(0, 2, 1, 3).reshape(b, s, D_MODEL).astype(h.dtype)
    return o @ w_o


def rglru_block(h, w_in, conv_w, conv_b, w_gate_a, b_gate_a, w_gate_x, b_gate_x, lru_param, w_out):
    gate_branch, rec_branch = jnp.split(h @ w_in, 2, axis=-1)
    xr = causal_depthwise_conv(rec_branch, conv_w, conv_b)
    b, s, _ = xr.shape
    xb = xr.reshape(b, s, RNN_HEADS, RNN_BLOCK)
    r = jax.nn.sigmoid(jnp.einsum("bshi,hij->bshj", xb, w_gate_a).reshape(b, s, RNN_WIDTH) + b_gate_a)
    i = jax.nn.sigmoid(jnp.einsum("bshi,hij->bshj", xb, w_gate_x).reshape(b, s, RNN_WIDTH) + b_gate_x)
    log_a = LRU_C * r.astype(jnp.float32) * jax.nn.log_sigmoid(lru_param.astype(jnp.float32))
    a = jnp.exp(log_a)
    mult = jnp.sqrt(-jnp.expm1(2.0 * log_a))
    u = mult * (i * xr).astype(jnp.float32)

    def combine(c1, c2):
        a1, b1 = c1
        a2, b2 = c2
        return a1 * a2, a2 * b1 + b2

    _, hseq = lax.associative_scan(combine, (a, u), axis=1)
    y = jax.nn.gelu(gate_branch) * hseq.astype(h.dtype)
    return y @ w_out


def conv_ffn(h, w_up, conv_w, conv_b, w_down):
    u = causal_depthwise_conv(h @ w_up, conv_w, conv_b)
    gate, val = jnp.split(u, 2, axis=-1)
    return (jax.nn.gelu(gate) * val) @ w_down


def per_layer_embedding(h, p_i, norm_g, w_gate, w_proj):
    g = jax.nn.sigmoid(rms_norm(h, norm_g) @ w_gate)
    return g * (p_i @ w_proj)


def setup_inputs(seed: int = 0) -> dict:
    key = jax.random.key(seed)
    ks = jax.random.split(key, 32)
    f32 = jnp.float32

    def nrm(k, shape, fan_in):
        return jax.random.normal(k, shape, f32) * (fan_in ** -0.5)

    def gain(k, shape):
        return 1.0 + 0.02 * jax.random.normal(k, shape, f32)

    def bias(k, shape):
        return 0.01 * jax.random.normal(k, shape, f32)

    a_base = jax.random.uniform(ks[12], (N_RNN_LAYERS, RNN_WIDTH), f32, minval=0.9, maxval=0.999)
    lru_param = jnp.log(a_base) - jnp.log1p(-a_base)

    return {
        "x": jax.random.normal(ks[0], (BATCH, SEQ, D_MODEL), f32),
        "p": jax.random.normal(ks[1], (DEPTH, BATCH, SEQ, PLE_DIM), f32),
        "norm_mix": gain(ks[2], (DEPTH, D_MODEL)),
        "attn_w_qkv": nrm(ks[3], (N_ATTN_LAYERS, D_MODEL, 3 * D_MODEL), D_MODEL),
        "attn_w_o": nrm(ks[4], (N_ATTN_LAYERS, D_MODEL, D_MODEL), D_MODEL),
        "rnn_w_in": nrm(ks[5], (N_RNN_LAYERS, D_MODEL, 2 * RNN_WIDTH), D_MODEL),
        "rnn_conv_w": nrm(ks[6], (N_RNN_LAYERS, RNN_CONV, RNN_WIDTH), RNN_CONV),
        "rnn_conv_b": bias(ks[7], (N_RNN_LAYERS, RNN_WIDTH)),
        "rnn_w_gate_a": nrm(ks[8], (N_RNN_LAYERS, RNN_HEADS, RNN_BLOCK, RNN_BLOCK), RNN_BLOCK),
        "rnn_b_gate_a": bias(ks[9], (N_RNN_LAYERS, RNN_WIDTH)),
        "rnn_w_gate_x": nrm(ks[10], (N_RNN_LAYERS, RNN_HEADS, RNN_BLOCK, RNN_BLOCK), RNN_BLOCK),
        "rnn_b_gate_x": bias(ks[11], (N_RNN_LAYERS, RNN_WIDTH)),
        "rnn_lru_param": lru_param,
        "rnn_w_out": nrm(ks[13], (N_RNN_LAYERS, RNN_WIDTH, D_MODEL), RNN_WIDTH),
        "norm_ffn": gain(ks[14], (DEPTH, D_MODEL)),
        "ffn_w_up": nrm(ks[15], (DEPTH, D_MODEL, 2 * FFN_DIM), D_MODEL),
        "ffn_conv_w": nrm(ks[16], (DEPTH, FFN_CONV, 2 * FFN_DIM), FFN_CONV),
        "ffn_conv_b": bias(ks[17], (DEPTH, 2 * FFN_DIM)),
        "ffn_w_down": nrm(ks[18], (DEPTH, FFN_DIM, D_MODEL), FFN_DIM),
        "norm_ple": gain(ks[19], (DEPTH, D_MODEL)),
        "ple_w_gate": nrm(ks[20], (DEPTH, D_MODEL, D_MODEL), D_MODEL),
        "ple_w_proj": nrm(ks[21], (DEPTH, PLE_DIM, D_MODEL), PLE_DIM),
        "norm_final": gain(ks[22], (D_MODEL,)),
    }


def reference(x, p, norm_mix, attn_w_qkv, attn_w_o, rnn_w_in, rnn_conv_w, rnn_conv_b,
              rnn_w_gate_a, rnn_b_gate_a, rnn_w_gate_x, rnn_b_gate_x, rnn_lru_param, rnn_w_out,
              norm_ffn, ffn_w_up, ffn_conv_w, ffn_conv_b, ffn_w_down,
              norm_ple, ple_w_gate, ple_w_proj, norm_final):
    for i in range(DEPTH):
        slot = i // N_MIXERS
        hn = rms_norm(x, norm_mix[i])
        if i % N_MIXERS == 0:
            x = x + stick_breaking_attention(hn, attn_w_qkv[slot], attn_w_o[slot])
        else:
            x = x + rglru_block(hn, rnn_w_in[slot], rnn_conv_w[slot], rnn_conv_b[slot],
                                rnn_w_gate_a[slot], rnn_b_gate_a[slot],
                                rnn_w_gate_x[slot], rnn_b_gate_x[slot],
                                rnn_lru_param[slot], rnn_w_out[slot])
        x = x + conv_ffn(rms_norm(x, norm_ffn[i]), ffn_w_up[i], ffn_conv_w[i], ffn_conv_b[i], ffn_w_down[i])
        x = x + per_layer_embedding(x, p[i], norm_ple[i], ple_w_gate[i], ple_w_proj[i])
    return rms_norm(x, norm_final)
```

```python
import contextlib
import numpy as np
import concourse.bass as bass
import concourse.mybir as mybir
from concourse.bass_utils import run_bass_kernel_spmd

F32 = mybir.dt.float32
BF16 = mybir.dt.bfloat16
AF = mybir.ActivationFunctionType
ALU = mybir.AluOpType

D = 1024
NC8 = 8
HEADS = 16
DH = 64
RW = 1280
RCH = 10
RB = 80
FF = 2816
FCH = 22
PLE = 256
EPS = 1e-6
LRU_C = 8.0
N_CORES = 8


class SemObj:
    def __init__(self, sem, name):
        self.sem = sem
        self.name = name
        self.count = 0


class Buf:
    __slots__ = ("name", "w", "r")

    def __init__(self, name=""):
        self.name = name
        self.w = None
        self.r = {}


class Sched:
    def __init__(self, nc):
        self.nc = nc
        self.eng = {"pe": nc.tensor, "act": nc.scalar, "dve": nc.vector, "pool": nc.gpsimd, "sp": nc.sync}
        self.esem = {k: SemObj(nc.alloc_semaphore("es_" + k), k) for k in self.eng}
        self.waited = {k: {} for k in self.eng}
        self.ndsem = 0
        self.ninst = 0

    def dsem(self):
        self.ndsem += 1
        return SemObj(self.nc.alloc_semaphore("ds%d" % self.ndsem), "ds%d" % self.ndsem)

    def _deps(self, en, reads, writes):
        deps = {}

        def add(tok):
            if tok is None:
                return
            so, v = tok
            if en == "pe" and so is self.esem["pe"]:
                return
            if deps.get(so, 0) < v:
                deps[so] = v

        for b in reads:
            add(b.w)
        for b in writes:
            add(b.w)
            for so, v in b.r.items():
                add((so, v))
        w = self.waited[en]
        e = self.eng[en]
        for so, v in deps.items():
            if w.get(so, 0) >= v:
                continue
            assert so.count >= v, "waiting on un-issued increment %s %d>%d" % (so.name, v, so.count)
            e.wait_ge(so.sem, v)
            self.ninst += 1
            w[so] = v

    def _mark(self, tok, reads, writes):
        so, v = tok
        for b in reads:
            if b.r.get(so, 0) < v:
                b.r[so] = v
        for b in writes:
            b.w = tok
            b.r = {}

    def op(self, en, fn, reads=(), writes=(), inc=True):
        self._deps(en, reads, writes)
        ins = fn(self.eng[en])
        self.ninst += 1
        so = self.esem[en]
        if inc:
            so.count += 1
            ins.then_inc(so.sem, 1)
            tok = (so, so.count)
        else:
            tok = (so, so.count + 1)
        self._mark(tok, reads, writes)
        return ins

    def dma(self, q, out, in_, ds, reads=(), writes=(), **kw):
        self._deps(q, reads, writes)
        ins = self.eng[q].dma_start(out=out, in_=in_, **kw)
        self.ninst += 1
        ds.count += 16
        ins.then_inc(ds.sem, 16)
        self._mark((ds, ds.count), reads, writes)
        return ins

    def wait_all(self, en, bufs):
        self._deps(en, bufs, bufs)

    def sync_from(self, en, others=("pe", "act", "dve")):
        w = self.waited[en]
        for other in others:
            so = self.esem[other]
            if so.count > w.get(so, 0):
                self.eng[en].wait_ge(so.sem, so.count)
                self.ninst += 1
                w[so] = so.count

    def barrier(self, engines=("pe", "act", "dve", "pool")):
        for en in engines:
            w = self.waited[en]
            for other in engines:
                if other == en:
                    continue
                so = self.esem[other]
                if so.count > w.get(so, 0):
                    self.eng[en].wait_ge(so.sem, so.count)
                    self.ninst += 1
                    w[so] = so.count


class Ring:
    def __init__(self, tiles):
        self.tiles = tiles
        self.bufs = [Buf() for _ in tiles]
        self.i = 0

    def next(self):
        k = self.i % len(self.tiles)
        self.i += 1
        return self.tiles[k], self.bufs[k]


def _cols(v, nch):
    lead = int(np.prod(v.shape[:-1])) if v.ndim > 1 else 1
    a = np.asarray(v, np.float32).reshape(lead, nch, 128)
    return np.ascontiguousarray(a.transpose(2, 0, 1).reshape(128, lead * nch))


def band_range(fc):
    b_lo = (128 * fc) // RB
    b_hi = (128 * fc + 127) // RB
    k_lo = (RB * b_lo) // 128
    k_hi = (RB * b_hi + RB - 1) // 128
    return k_lo, min(k_hi, RCH - 1)


def _band(wg):
    L = wg.shape[0]
    dense = np.zeros((L, RW, RW), np.float32)
    for h in range(RW // RB):
        dense[:, h * RB:(h + 1) * RB, h * RB:(h + 1) * RB] = wg[:, h]
    out = np.zeros((L, 128, RCH, 3, 128), np.float32)
    for fc in range(RCH):
        k_lo, k_hi = band_range(fc)
        for kk, kc in enumerate(range(k_lo, k_hi + 1)):
            out[:, :, fc, kk, :] = dense[:, kc * 128:(kc + 1) * 128, fc * 128:(fc + 1) * 128]
    return out


class PP:
    def __init__(self, depth):
        self.off = {}
        self.n = 0
        nr = max(depth // 2, 1)
        for name, cnt in [("norm_mix", depth * 8), ("norm_ffn", depth * 8), ("norm_ple", depth * 8), ("norm_final", 8),
                          ("ffn_conv_w", depth * 3 * 44), ("ffn_conv_b", depth * 44),
                          ("rnn_conv_w", nr * 4 * 10), ("rnn_conv_b", nr * 10), ("b_gate_a", nr * 10),
                          ("b_gate_x", nr * 10), ("lru", nr * 10), ("lruc", nr * 10)]:
            self.off[name] = self.n
            self.n += cnt


def make_consts():
    j = np.arange(128)[:, None]
    s = np.arange(128)[None, :]
    ident = (j == s).astype(np.float32)
    negtri = -(j >= s).astype(np.float32)
    negones = -np.ones((128, 128), np.float32)
    masklt = (j < s).astype(np.float32)
    ones = np.ones((128, 128), np.float32)
    return np.ascontiguousarray(np.concatenate([ident, negtri, negones, masklt, ones], axis=1))


def build(NSEQ, S, DEPTH):
    assert S % 512 == 0
    NTT = S // 512
    NQB = S // 128
    TH = min(S, 1024)
    NH = S // TH
    NTH = TH // 512
    NATT = (DEPTH + 1) // 2
    NRNN = max(DEPTH // 2, 1)
    pp = PP(DEPTH)

    nc = bass.Bass("TRN2", target_bir_lowering=False)

    def dram(name, shape, kind="ExternalInput", dt=F32):
        return nc.dram_tensor(name, list(shape), dt, kind=kind).ap()

    x_d = dram("x", [NSEQ, S, D])
    p_d = dram("p", [DEPTH, NSEQ, S, PLE])
    out_d = dram("out", [NSEQ, S, D], kind="ExternalOutput")
    wqkv_d = dram("attn_w_qkv", [NATT, D, 3 * D])
    wo_d = dram("attn_w_o", [NATT, D, D])
    win_d = dram("rnn_w_in", [NRNN, D, 2 * RW])
    wout_d = dram("rnn_w_out", [NRNN, RW, D])
    wup_d = dram("ffn_w_up", [DEPTH, D, 2 * FF])
    wdown_d = dram("ffn_w_down", [DEPTH, FF, D])
    wpg_d = dram("ple_w_gate", [DEPTH, D, D])
    wpp_d = dram("ple_w_proj", [DEPTH, PLE, D])
    wba_d = dram("wband_a", [NRNN, 128, RCH, 3, 128])
    wbx_d = dram("wband_x", [NRNN, 128, RCH, 3, 128])
    pp_d = dram("pp", [128, pp.n])
    cst_d = dram("cst", [128, 5 * 128])

    S_ = Sched(nc)
    op = S_.op
    es = contextlib.ExitStack()

    uid = [0]

    def sb(name, shape, dt, stack=None):
        uid[0] += 1
        return (stack or es).enter_context(nc.sbuf_tensor("%s_u%d" % (name, uid[0]), list(shape), dt))

    xT = sb("xT", [128, NC8, S], F32)
    ppt = sb("ppt", [128, pp.n], F32)
    cst32 = sb("cst32", [128, 5 * 128], F32)
    cstb = sb("cstb", [128, 5 * 128], BF16)
    ident = cst32[:, 0:128]
    negtri = cstb[:, 128:256]
    negones = cstb[:, 256:384]
    masklt = cstb[:, 384:512]
    onesb = cstb[:, 512:640]
    NW = 4
    WSZ = FCH * 128
    wslots = [sb("w%d" % i, [128, WSZ], BF16) for i in range(NW)]
    wbufs = [Buf("w%d" % i) for i in range(NW)]
    wsems = [S_.dsem() for _ in range(NW)]
    wcnt = [0]
    stgS = [S_.dsem() for _ in range(4)]
    pstS = [S_.dsem() for _ in range(4)]
    pst = [sb("pst%d" % i, [128, PLE], F32) for i in range(4)]
    pstB = [Buf() for _ in range(4)]
    carry_f = sb("carry_f", [128, 2 * FCH, 2], F32)
    carry_r = sb("carry_r", [128, RCH, 3], F32)
    carry_h = sb("carry_h", [128, RCH], F32)
    Bcf = [Buf() for _ in range(2 * FCH)]
    Bcr = [Buf() for _ in range(RCH)]
    Bch = [Buf() for _ in range(RCH)]

    psum = [es.enter_context(nc.psum_tensor("ps%d" % i, [128, 512], F32)) for i in range(8)]
    PB = [Buf("ps%d" % i) for i in range(8)]
    bank_ring = [list(range(8))]
    bank_i = [0]

    def next_bank():
        r = bank_ring[0]
        b = r[bank_i[0] % len(r)]
        bank_i[0] += 1
        return b

    X = [[Buf() for _ in range(NTT)] for _ in range(NC8)]
    Bpp = Buf("pp")
    Bcst = Buf("cst")

    def ts(tt):
        return slice(tt * 512, (tt + 1) * 512)

    def ppc(name, idx):
        o = pp.off[name] + idx
        return ppt[:, o:o + 1]

    d0 = S_.dsem()
    S_.dma("sp", ppt[:], pp_d[:, :], d0, writes=[Bpp])
    d1 = S_.dsem()
    S_.dma("sp", cst32[:], cst_d[:, :], d1, writes=[Bcst])
    op("dve", lambda e: e.tensor_copy(out=cstb[:], in_=cst32[:]), reads=[Bcst], writes=[Bcst])
    nl = NRNN * RCH
    lo, lc = pp.off["lru"], pp.off["lruc"]
    op("act", lambda e: e.activation(out=ppt[:, lc:lc + nl], in_=ppt[:, lo:lo + nl], func=AF.Exp, scale=-1.0),
       reads=[Bpp], writes=[Bpp])
    op("act", lambda e: e.activation(out=ppt[:, lc:lc + nl], in_=ppt[:, lc:lc + nl], func=AF.Ln, bias=1.0),
       reads=[Bpp], writes=[Bpp])
    op("dve", lambda e: e.tensor_scalar(out=ppt[:, lc:lc + nl], in0=ppt[:, lc:lc + nl], scalar1=-LRU_C, scalar2=None,
                                        op0=ALU.mult), reads=[Bpp], writes=[Bpp])

    def load_w(view, K, n=128):
        i = wcnt[0] % NW
        wcnt[0] += 1
        t = wslots[i][:, 0:K * n].rearrange("p (k n) -> p k n", k=K)
        S_.dma("pool", t, view, wsems[i], writes=[wbufs[i]])
        return t, wbufs[i]

    pair_cache = {}

    def load_wp(name, w_ap, col0_chunk, idx, K, krange=None):
        if idx % 2 == 0:
            key = (name, col0_chunk + idx)
            if key in stash:
                pair_cache[name] = stash.pop(key)
            else:
                pair_cache[name] = slab_load(w_ap, col0_chunk + idx, K, krange)
        t, B = pair_cache[name]
        return t[:, :, (idx % 2) * 128:(idx % 2 + 1) * 128], B

    stash = {}

    def slab_load(w_ap, chunk, K, krange):
        f0 = chunk * 128
        v = w_ap.rearrange("(k p) f -> p k f", p=128)
        v = v[:, krange[0]:krange[1], f0:f0 + 256] if krange else v[:, :, f0:f0 + 256]
        return load_w(v, K, 256)

    def prefetch_wp(name, w_ap, col0_chunk, idx, K, krange=None):
        key = (name, col0_chunk + idx)
        if key not in stash:
            stash[key] = slab_load(w_ap, col0_chunk + idx, K, krange)

    def wview(w_ap, f0, n=128):
        return w_ap.rearrange("(k p) f -> p k f", p=128)[:, :, f0:f0 + n]

    def matmul_group(bank, cols, pairs, reads, inc_last=True):
        n = len(pairs)
        for k, (l, r) in enumerate(pairs):
            op("pe", lambda e, l=l, r=r, k=k: e.matmul(psum[bank][:, cols], lhsT=l, rhs=r, start=(k == 0), stop=(k == n - 1)),
               reads=reads, writes=[PB[bank]], inc=(inc_last and k == n - 1))

    def norm_tile(tt, gname, gidx, work, dst=None, out_f32=None):
        sq_ring, rs_ring = work
        if dst is not None:
            hn, HN, tl = dst
        bank = next_bank()
        for c in range(NC8):
            sq, Bsq = sq_ring.next()
            op("act", lambda e, c=c, sq=sq: e.activation(out=sq[:], in_=xT[:, c, ts(tt)], func=AF.Square),
               reads=[X[c][tt]], writes=[Bsq])
            op("pe", lambda e, c=c, sq=sq: e.matmul(psum[bank][:, :], lhsT=onesb, rhs=sq[:], start=(c == 0), stop=(c == NC8 - 1)),
               reads=[Bsq, Bcst], writes=[PB[bank]])
        rs, Brs = rs_ring.next()
        op("act", lambda e: e.activation(out=rs[:], in_=psum[bank][:, :], func=AF.Sqrt, scale=1.0 / D, bias=EPS),
           reads=[PB[bank]], writes=[Brs])
        op("dve", lambda e: e.reciprocal(out=rs[:], in_=rs[:]), reads=[Brs], writes=[Brs])
        for c in range(NC8):
            o = hn[:, c, ts(tl)] if out_f32 is None else out_f32[0][:, c, :]
            wb = [HN[c][tl]] if out_f32 is None else [out_f32[1]]
            op("dve", lambda e, c=c, o=o: e.scalar_tensor_tensor(out=o, in0=xT[:, c, ts(tt)], scalar=ppc(gname, gidx * 8 + c),
                                                                 in1=rs[:], op0=ALU.mult, op1=ALU.mult),
               reads=[X[c][tt], Brs, Bpp], writes=wb)

    def resid_add(bank, dc, tt):
        op("dve", lambda e: e.tensor_tensor(out=xT[:, dc, ts(tt)], in0=xT[:, dc, ts(tt)], in1=psum[bank][:, :], op=ALU.add),
           reads=[PB[bank], X[dc][tt]], writes=[X[dc][tt]])

    def hn_alloc(stack, ntl):
        t = sb("hn", [128, NC8, ntl * 512], BF16, stack)
        return t, [[Buf() for _ in range(ntl)] for _ in range(NC8)]

    def norm_work(stack):
        sq_ring = Ring([sb("sq%d" % i, [128, 512], BF16, stack) for i in range(2)])
        rs_ring = Ring([sb("rs%d" % i, [128, 512], F32, stack) for i in range(2)])
        return sq_ring, rs_ring

    def load_x(sq_i):
        bank_ring[0] = list(range(8))
        st = contextlib.ExitStack()
        stg = [sb("stg%d" % i, [128, D], F32, st) for i in range(4)]
        stgB = [Buf() for _ in range(4)]
        S_.sync_from("sp")
        for tt in range(NTT):
            for q in range(4):
                S_.dma("sp", stg[q][:], x_d[sq_i, tt * 512 + q * 128: tt * 512 + (q + 1) * 128, :], stgS[q], writes=[stgB[q]])
            for c in range(NC8):
                bank = next_bank()
                for q in range(4):
                    op("pe", lambda e, q=q, c=c: e.transpose(out=psum[bank][:, q * 128:(q + 1) * 128],
                                                             in_=stg[q][:, c * 128:(c + 1) * 128], identity=ident),
                       reads=[stgB[q], Bcst], writes=[PB[bank]])
                op("act", lambda e, c=c: e.activation(out=xT[:, c, ts(tt)], in_=psum[bank][:, :], func=AF.Identity),
                   reads=[PB[bank]], writes=[X[c][tt]])
        S_.barrier()
        st.close()

    def final_store(sq_i):
        with contextlib.ExitStack() as st:
            work = norm_work(st)
            yf = sb("yf", [128, NC8, 512], F32, st)
            Byf = Buf()
            stg = [sb("stg%d" % i, [128, D], F32, st) for i in range(4)]
            stgB = [Buf() for _ in range(4)]
            bank_ring[0] = list(range(8))
            for tt in range(NTT):
                norm_tile(tt, "norm_final", 0, work, out_f32=(yf, Byf))
                for q in range(4):
                    b0, b1 = next_bank(), next_bank()
                    for c in range(NC8):
                        bank = b0 if c < 4 else b1
                        op("pe", lambda e, c=c, bank=bank: e.transpose(out=psum[bank][:, (c % 4) * 128:(c % 4 + 1) * 128],
                                                                      in_=yf[:, c, q * 128:(q + 1) * 128], identity=ident),
                           reads=[Byf, Bcst], writes=[PB[bank]])
                    op("act", lambda e: e.activation(out=stg[q][:, 0:512], in_=psum[b0][:, :], func=AF.Identity),
                       reads=[PB[b0]], writes=[stgB[q]])
                    op("dve", lambda e: e.tensor_copy(out=stg[q][:, 512:1024], in_=psum[b1][:, :]),
                       reads=[PB[b1]], writes=[stgB[q]])
                    S_.dma("sp", out_d[sq_i, tt * 512 + q * 128: tt * 512 + (q + 1) * 128, :], stg[q][:], stgS[q], reads=[stgB[q]])
            for en in ("sp", "pe", "act", "dve"):
                S_.wait_all(en, stgB)
            S_.barrier()

    def attention(l):
        slot = l // 2
        with contextlib.ExitStack() as st:
            work = norm_work(st)
            OT = sb("OT", [128, NC8, S], BF16, st)
            BO = [[Buf() for _ in range(NTT)] for _ in range(NC8)]
            QT = sb("QT", [128, S], BF16, st)
            KT = sb("KT", [128, S], BF16, st)
            Vb = sb("Vb", [128, NQB, 128], BF16, st)
            BQ, BK, BV = Buf(), Buf(), Buf()
            NSTR = 4
            Er = [Ring([sb("E%d_%d" % (s, i), [128, 512], BF16, st) for i in range(1)]) for s in range(NSTR)]
            Lr = [Ring([sb("L%d_%d" % (s, i), [128, 512], BF16, st) for i in range(2)]) for s in range(NSTR)]
            Wr = [Ring([sb("W%d_%d" % (s, i), [128, 512], BF16, st) for i in range(2)]) for s in range(NSTR)]
            Lsuf = [sb("Ls%d" % s, [128, 512], BF16, st) for s in range(NSTR)]
            BLs = [Buf() for _ in range(NSTR)]

            hn, HN = hn_alloc(st, NTT)
            bank_ring[0] = list(range(8))
            for tt in range(NTT):
                norm_tile(tt, "norm_mix", l, work, (hn, HN, tt))
            bank_ring[0] = [6, 7]
            if NTT == 4:
                pairs = [[0, 3], [1, 2]]
            elif NTT == 2:
                pairs = [[0], [1]]
            else:
                pairs = [list(range(NTT))]
            for c in range(NC8):
                wq, Bwq = load_wp("q", wqkv_d[slot], 0, c, 8)
                wk, Bwk = load_wp("k", wqkv_d[slot], 8, c, 8)
                wv, Bwv = load_wp("v", wqkv_d[slot], 16, c, 8)
                for tt in range(NTT):
                    bank = next_bank()
                    matmul_group(bank, slice(0, 512), [(wq[:, k, :], hn[:, k, ts(tt)]) for k in range(8)],
                                 [Bwq] + [HN[k][tt] for k in range(8)])
                    op("dve", lambda e: e.tensor_scalar(out=QT[:, ts(tt)], in0=psum[bank][:, :], scalar1=DH ** -0.5, scalar2=None,
                                                        op0=ALU.mult), reads=[PB[bank]], writes=[BQ])
                    bank = next_bank()
                    matmul_group(bank, slice(0, 512), [(wk[:, k, :], hn[:, k, ts(tt)]) for k in range(8)],
                                 [Bwk] + [HN[k][tt] for k in range(8)])
                    op("dve", lambda e: e.tensor_copy(out=KT[:, ts(tt)], in_=psum[bank][:, :]), reads=[PB[bank]], writes=[BK])
                for tt in range(NTT):
                    bank = next_bank()
                    for q in range(4):
                        tb = tt * 4 + q
                        matmul_group(bank, slice(q * 128, (q + 1) * 128),
                                     [(hn[:, k, tb * 128:(tb + 1) * 128], wv[:, k, :]) for k in range(8)],
                                     [Bwv] + [HN[k][tt] for k in range(8)])
                    op("dve", lambda e: e.tensor_copy(out=Vb[:, tt * 4:(tt + 1) * 4, :],
                                                      in_=psum[bank][:, :].rearrange("p (q n) -> p q n", q=4)),
                       reads=[PB[bank]], writes=[BV])
                steps = []
                for ps_i, tcs in enumerate(pairs):
                    lst = []
                    for tc in tcs:
                        kbs = list(range(4 * tc + 3, -1, -1))
                        for idx, kb in enumerate(kbs):
                            lst.append((tc, kb, idx, idx == len(kbs) - 1))
                    steps.append(lst)
                nround = max(len(s) for s in steps)
                acts, infos = [], []
                for r in range(nround):
                    act = [(ps_i, hh) for ps_i in range(len(pairs)) if r < len(steps[ps_i]) for hh in range(2)]
                    info = {}
                    for (ps_i, hh) in act:
                        tc, kb, idx, last = steps[ps_i][r]
                        sid = ps_i * 2 + hh
                        off = max(0, kb - 4 * tc)
                        c0 = off * 128
                        diag = kb >= 4 * tc
                        info[(ps_i, hh)] = (tc, kb, idx, last, sid, c0, diag)
                    acts.append(act)
                    infos.append(info)

                def emit_z(r):
                    for key in acts[r]:
                        tc, kb, idx, last, sid, c0, diag = infos[r][key]
                        hh = key[1]
                        hp = slice(64 * hh, 64 * hh + 64)
                        if idx == 0:
                            op("dve", lambda e, sid=sid: e.memset(Lsuf[sid][:], 0.0), writes=[BLs[sid]])
                        op("pe", lambda e, sid=sid, hp=hp, kb=kb, tc=tc, c0=c0: e.matmul(
                            psum[sid][:, c0:512], lhsT=KT[hp, kb * 128:(kb + 1) * 128], rhs=QT[hp, tc * 512 + c0:(tc + 1) * 512],
                            start=True, stop=False, skip_group_check=True), reads=[BK, BQ], writes=[PB[sid]])

                emit_z(0)
                for r in range(nround):
                    act, info = acts[r], infos[r]
                    tiles = {}
                    for key in act:
                        tc, kb, idx, last, sid, c0, diag = info[key]
                        Et, BE = Er[sid].next()
                        op("act", lambda e, Et=Et, sid=sid, c0=c0: e.activation(out=Et[:, c0:512], in_=psum[sid][:, c0:512], func=AF.Exp),
                           reads=[PB[sid]], writes=[BE])
                        tiles[key] = [Et, BE]
                    for key in act:
                        tc, kb, idx, last, sid, c0, diag = info[key]
                        Et, BE = tiles[key]
                        Lt, BL = Lr[sid].next()
                        op("act", lambda e, Et=Et, Lt=Lt, c0=c0: e.activation(out=Lt[:, c0:512], in_=Et[:, c0:512], func=AF.Ln, bias=1.0),
                           reads=[BE], writes=[BL])
                        if diag:
                            op("dve", lambda e, Lt=Lt, c0=c0: e.tensor_tensor(out=Lt[:, c0:c0 + 128], in0=Lt[:, c0:c0 + 128], in1=masklt, op=ALU.mult),
                               reads=[BL, Bcst], writes=[BL])
                        tiles[key] += [Lt, BL]
                    for key in act:
                        tc, kb, idx, last, sid, c0, diag = info[key]
                        Et, BE, Lt, BL = tiles[key]
                        op("pe", lambda e, sid=sid, Lt=Lt, c0=c0, idx=idx: e.matmul(
                            psum[sid][:, c0:512], lhsT=negtri, rhs=Lt[:, c0:512], start=False, stop=(idx == 0), skip_group_check=True),
                           reads=[BL, Bcst], writes=[PB[sid]])
                        if idx > 0:
                            op("pe", lambda e, sid=sid, c0=c0: e.matmul(
                                psum[sid][:, c0:512], lhsT=negones, rhs=Lsuf[sid][:, c0:512], start=False, stop=True, skip_group_check=True),
                               reads=[BLs[sid], Bcst], writes=[PB[sid]])
                    for key in act:
                        tc, kb, idx, last, sid, c0, diag = info[key]
                        Wt, BW = Wr[sid].next()
                        op("act", lambda e, Wt=Wt, sid=sid, c0=c0: e.activation(out=Wt[:, c0:512], in_=psum[sid][:, c0:512], func=AF.Exp),
                           reads=[PB[sid]], writes=[BW])
                        if diag:
                            op("dve", lambda e, Wt=Wt, c0=c0: e.tensor_tensor(out=Wt[:, c0:c0 + 128], in0=Wt[:, c0:c0 + 128], in1=masklt, op=ALU.mult),
                               reads=[BW, Bcst], writes=[BW])
                        tiles[key] += [Wt, BW]
                    if r + 1 < nround:
                        emit_z(r + 1)
                    for key in act:
                        tc, kb, idx, last, sid, c0, diag = info[key]
                        ps_i, hh = key
                        Et, BE, Lt, BL, Wt, BW = tiles[key]
                        pob = 4 + ps_i
                        op("pe", lambda e, pob=pob, hh=hh, kb=kb, Wt=Wt, c0=c0, idx=idx, last=last: e.matmul(
                            psum[pob][64 * hh:64 * hh + 64, c0:512], lhsT=Vb[:, kb, 64 * hh:64 * hh + 64], rhs=Wt[:, c0:512],
                            start=(idx == 0), stop=last, skip_group_check=True), reads=[BW, BV], writes=[PB[pob]])
                        if not last:
                            op("dve", lambda e, sid=sid, Lt=Lt, c0=c0: e.tensor_tensor(out=Lsuf[sid][:, c0:512], in0=Lsuf[sid][:, c0:512],
                                                                                      in1=Lt[:, c0:512], op=ALU.add),
                               reads=[BL, BLs[sid]], writes=[BLs[sid]])
                    for ps_i in range(len(pairs)):
                        if r < len(steps[ps_i]) and steps[ps_i][r][3]:
                            tc = steps[ps_i][r][0]
                            pob = 4 + ps_i
                            op("dve", lambda e, pob=pob, tc=tc: e.tensor_copy(out=OT[:, c, ts(tc)], in_=psum[pob][:, :]),
                               reads=[PB[pob]], writes=[BO[c][tc]])
            bank_ring[0] = list(range(8))
            for dc in range(NC8):
                wo, Bwo = load_wp("o", wo_d[slot], 0, dc, 8)
                if dc % 2 == 0 and dc + 2 < NC8:
                    prefetch_wp("o", wo_d[slot], 0, dc + 2, 8)
                for tt in range(NTT):
                    bank = next_bank()
                    matmul_group(bank, slice(0, 512), [(wo[:, k, :], OT[:, k, ts(tt)]) for k in range(8)],
                                 [Bwo] + [BO[k][tt] for k in range(8)])
                    resid_add(bank, dc, tt)
            S_.barrier()

    def rglru(l):
        slot = l // 2
        with contextlib.ExitStack() as st:
            work = norm_work(st)
            xr = sb("xr", [128, RCH, TH], BF16, st)
            G = sb("G", [128, RCH, TH], BF16, st)
            Bxr = [Buf() for _ in range(RCH)]
            BG = [Buf() for _ in range(RCH)]
            Ur = Ring([sb("Ur%d" % i, [128, TH + 3], F32, st) for i in range(2)])
            Yr = Ring([sb("Yr%d" % i, [128, TH], F32, st) for i in range(2)])
            T1r = Ring([sb("T1_%d" % i, [128, TH], F32, st) for i in range(2)])
            T2r = Ring([sb("T2_%d" % i, [128, TH], F32, st) for i in range(2)])
            T3r = Ring([sb("T3_%d" % i, [128, TH], F32, st) for i in range(2)])
            hn, HN = hn_alloc(st, NTH)
            bank_ring[0] = list(range(8))
            for hf in range(NH):
                h0 = hf * TH
                tts = list(range(hf * NTH, (hf + 1) * NTH))
                for ti, tt in enumerate(tts):
                    norm_tile(tt, "norm_mix", l, work, (hn, HN, ti))
                for j in range(RCH):
                    wr, Bwr = load_wp("r", win_d[slot], RCH, j, 8)
                    wg, Bwg = load_wp("g", win_d[slot], 0, j, 8)
                    if j % 2 == 0 and j + 2 < RCH:
                        prefetch_wp("r", win_d[slot], RCH, j + 2, 8)
                        prefetch_wp("g", win_d[slot], 0, j + 2, 8)
                    U, BU = Ur.next()
                    if hf == 0:
                        op("pool", lambda e, U=U: e.memset(U[:, 0:3], 0.0), writes=[BU])
                    else:
                        op("pool", lambda e, U=U, j=j: e.tensor_copy(out=U[:, 0:3], in_=carry_r[:, j, :]), reads=[Bcr[j]], writes=[BU])
                    for ti, tt in enumerate(tts):
                        bank = next_bank()
                        matmul_group(bank, slice(0, 512), [(wr[:, k, :], hn[:, k, ts(ti)]) for k in range(8)],
                                     [Bwr] + [HN[k][ti] for k in range(8)])
                        op("act", lambda e, U=U, ti=ti: e.activation(out=U[:, 3 + ti * 512:3 + (ti + 1) * 512], in_=psum[bank][:, :], func=AF.Identity),
                           reads=[PB[bank]], writes=[BU])
                    if hf + 1 < NH:
                        op("pool", lambda e, U=U, j=j: e.tensor_copy(out=carry_r[:, j, :], in_=U[:, TH:TH + 3]), reads=[BU], writes=[Bcr[j]])
                    Y, BY = Yr.next()
                    cw = lambda k, j=j: ppc("rnn_conv_w", (slot * 4 + k) * RCH + j)
                    op("pool", lambda e, U=U, Y=Y, j=j: e.tensor_scalar(out=Y[:], in0=U[:, 3:3 + TH], scalar1=cw(3),
                                                                       scalar2=ppc("rnn_conv_b", slot * RCH + j), op0=ALU.mult, op1=ALU.add),
                       reads=[BU, Bpp], writes=[BY])
                    for k in (2, 1):
                        op("dve", lambda e, U=U, Y=Y, k=k: e.scalar_tensor_tensor(out=Y[:], in0=U[:, k:k + TH], scalar=cw(k), in1=Y[:],
                                                                                  op0=ALU.mult, op1=ALU.add),
                           reads=[BU, BY, Bpp], writes=[BY])
                    op("dve", lambda e, U=U, Y=Y, j=j: e.scalar_tensor_tensor(out=xr[:, j, :], in0=U[:, 0:TH], scalar=cw(0), in1=Y[:],
                                                                              op0=ALU.mult, op1=ALU.add),
                       reads=[BU, BY, Bpp], writes=[Bxr[j]])
                    for ti, tt in enumerate(tts):
                        bank = next_bank()
                        matmul_group(bank, slice(0, 512), [(wg[:, k, :], hn[:, k, ts(ti)]) for k in range(8)],
                                     [Bwg] + [HN[k][ti] for k in range(8)])
                        op("act", lambda e, j=j, ti=ti: e.activation(out=G[:, j, ti * 512:(ti + 1) * 512], in_=psum[bank][:, :], func=AF.Gelu_apprx_tanh),
                           reads=[PB[bank]], writes=[BG[j]])
                for fc in range(RCH):
                    k_lo, k_hi = band_range(fc)
                    nk = k_hi - k_lo + 1
                    wa, Bwa = load_w(wba_d[slot, :, fc, 0:nk, :], nk)
                    wx, Bwx = load_w(wbx_d[slot, :, fc, 0:nk, :], nk)
                    T1, B1 = T1r.next()
                    T2, B2 = T2r.next()
                    T3, B3 = T3r.next()
                    for (wt_, Bw_, T_, B_, bn) in ((wa, Bwa, T1, B1, "b_gate_a"), (wx, Bwx, T2, B2, "b_gate_x")):
                        for ti in range(NTH):
                            bank = next_bank()
                            matmul_group(bank, slice(0, 512),
                                         [(wt_[:, kk, :], xr[:, k_lo + kk, ti * 512:(ti + 1) * 512]) for kk in range(nk)],
                                         [Bw_] + [Bxr[k_lo + kk] for kk in range(nk)])
                            op("act", lambda e, T_=T_, ti=ti, bn=bn, bank=bank: e.activation(
                                out=T_[:, ti * 512:(ti + 1) * 512], in_=psum[bank][:, :], func=AF.Sigmoid, bias=ppc(bn, slot * RCH + fc)),
                               reads=[PB[bank], Bpp], writes=[B_])
                    op("act", lambda e, T1=T1: e.activation(out=T1[:], in_=T1[:], func=AF.Exp, scale=ppc("lruc", slot * RCH + fc)),
                       reads=[B1, Bpp], writes=[B1])
                    op("act", lambda e, T1=T1, T3=T3: e.activation(out=T3[:], in_=T1[:], func=AF.Square), reads=[B1], writes=[B3])
                    op("act", lambda e, T3=T3: e.activation(out=T3[:], in_=T3[:], func=AF.Sqrt, scale=-1.0, bias=1.0), reads=[B3], writes=[B3])
                    op("dve", lambda e, T2=T2: e.tensor_tensor(out=T2[:], in0=T2[:], in1=xr[:, fc, :], op=ALU.mult), reads=[B2, Bxr[fc]], writes=[B2])
                    op("dve", lambda e, T2=T2, T3=T3: e.tensor_tensor(out=T2[:], in0=T2[:], in1=T3[:], op=ALU.mult), reads=[B2, B3], writes=[B2])
                    init = 0.0 if hf == 0 else carry_h[:, fc:fc + 1]
                    op("dve", lambda e, T1=T1, T2=T2, T3=T3, init=init: e.tensor_tensor_scan(out=T3[:], data0=T1[:], data1=T2[:], initial=init,
                                                                                            op0=ALU.mult, op1=ALU.add),
                       reads=[B1, B2, B3, Bch[fc]], writes=[B3])
                    if hf + 1 < NH:
                        op("dve", lambda e, T3=T3: e.tensor_copy(out=carry_h[:, fc:fc + 1], in_=T3[:, TH - 1:TH]), reads=[B3], writes=[Bch[fc]])
                    op("dve", lambda e, T3=T3: e.tensor_tensor(out=G[:, fc, :], in0=G[:, fc, :], in1=T3[:], op=ALU.mult), reads=[B3, BG[fc]], writes=[BG[fc]])
                for dc in range(NC8):
                    wo, Bwo = load_wp("ro", wout_d[slot], 0, dc, RCH)
                    if dc % 2 == 0 and dc + 2 < NC8:
                        prefetch_wp("ro", wout_d[slot], 0, dc + 2, RCH)
                    for ti, tt in enumerate(tts):
                        bank = next_bank()
                        matmul_group(bank, slice(0, 512), [(wo[:, k, :], G[:, k, ti * 512:(ti + 1) * 512]) for k in range(RCH)],
                                     [Bwo] + BG)
                        resid_add(bank, dc, tt)
            S_.barrier()

    def ffn(l):
        with contextlib.ExitStack() as st:
            work = norm_work(st)
            aT = sb("aT", [128, FCH, TH], BF16, st)
            Ba = [Buf() for _ in range(FCH)]
            Ur = Ring([sb("Uf%d" % i, [128, TH + 2], F32, st) for i in range(4)])
            hn, HN = hn_alloc(st, NTH)
            Yr = Ring([sb("Yf%d" % i, [128, TH], F32, st) for i in range(2)])
            Ybr = Ring([sb("Yb%d" % i, [128, TH], BF16, st) for i in range(2)])
            Gr = Ring([sb("Gf%d" % i, [128, TH], BF16, st) for i in range(2)])
            bank_ring[0] = list(range(8))
            for hf in range(NH):
                tts = list(range(hf * NTH, (hf + 1) * NTH))
                for ti, tt in enumerate(tts):
                    norm_tile(tt, "norm_ffn", l, work, (hn, HN, ti))
                for j in range(FCH):
                    res = []
                    cur = [load_wp("u%d" % gv, wup_d[l], gv * FCH, j, 8) for gv in range(2)]
                    if j % 2 == 0:
                        if j + 2 < FCH:
                            for gv in range(2):
                                prefetch_wp("u%d" % gv, wup_d[l], gv * FCH, j + 2, 8)
                        else:
                            prefetch_wp("dA", wdown_d[l], 0, 0, 11, (0, 11))
                            prefetch_wp("dB", wdown_d[l], 0, 0, 11, (11, 22))
                    for gv in range(2):
                        fch = gv * FCH + j
                        w_, Bw_ = cur[gv]
                        U, BU = Ur.next()
                        if hf == 0:
                            op("pool", lambda e, U=U: e.memset(U[:, 0:2], 0.0), writes=[BU])
                        else:
                            op("pool", lambda e, U=U, fch=fch: e.tensor_copy(out=U[:, 0:2], in_=carry_f[:, fch, :]), reads=[Bcf[fch]], writes=[BU])
                        for ti, tt in enumerate(tts):
                            bank = next_bank()
                            matmul_group(bank, slice(0, 512), [(w_[:, k, :], hn[:, k, ts(ti)]) for k in range(8)],
                                         [Bw_] + [HN[k][ti] for k in range(8)])
                            op("act", lambda e, U=U, ti=ti, bank=bank: e.activation(out=U[:, 2 + ti * 512:2 + (ti + 1) * 512], in_=psum[bank][:, :], func=AF.Identity),
                               reads=[PB[bank]], writes=[BU])
                        if hf + 1 < NH:
                            op("pool", lambda e, U=U, fch=fch: e.tensor_copy(out=carry_f[:, fch, :], in_=U[:, TH:TH + 2]), reads=[BU], writes=[Bcf[fch]])
                        Y, BY = Yr.next()
                        cw = lambda k, fch=fch: ppc("ffn_conv_w", (l * 3 + k) * 44 + fch)
                        op("pool", lambda e, U=U, Y=Y, fch=fch: e.tensor_scalar(out=Y[:], in0=U[:, 2:2 + TH], scalar1=cw(2),
                                                                               scalar2=ppc("ffn_conv_b", l * 44 + fch), op0=ALU.mult, op1=ALU.add),
                           reads=[BU, Bpp], writes=[BY])
                        op("dve", lambda e, U=U, Y=Y: e.scalar_tensor_tensor(out=Y[:], in0=U[:, 1:1 + TH], scalar=cw(1), in1=Y[:],
                                                                             op0=ALU.mult, op1=ALU.add),
                           reads=[BU, BY, Bpp], writes=[BY])
                        if gv == 0:
                            op("dve", lambda e, U=U, Y=Y: e.scalar_tensor_tensor(out=Y[:], in0=U[:, 0:TH], scalar=cw(0), in1=Y[:],
                                                                                 op0=ALU.mult, op1=ALU.add),
                               reads=[BU, BY, Bpp], writes=[BY])
                            Gt, BGt = Gr.next()
                            op("act", lambda e, Y=Y, Gt=Gt: e.activation(out=Gt[:], in_=Y[:], func=AF.Gelu_apprx_tanh), reads=[BY], writes=[BGt])
                            res.append((Gt, BGt))
                        else:
                            Yb, BYb = Ybr.next()
                            op("dve", lambda e, U=U, Y=Y, Yb=Yb: e.scalar_tensor_tensor(out=Yb[:], in0=U[:, 0:TH], scalar=cw(0), in1=Y[:],
                                                                                       op0=ALU.mult, op1=ALU.add),
                               reads=[BU, BY, Bpp], writes=[BYb])
                            res.append((Yb, BYb))
                    (Gt, BGt), (Yb, BYb) = res
                    op("pool", lambda e, Gt=Gt, Yb=Yb, j=j: e.tensor_tensor(out=aT[:, j, :], in0=Gt[:], in1=Yb[:], op=ALU.mult),
                       reads=[BGt, BYb], writes=[Ba[j]])
                for dc in range(NC8):
                    wdA, BwdA = load_wp("dA", wdown_d[l], 0, dc, 11, (0, 11))
                    wdB, BwdB = load_wp("dB", wdown_d[l], 0, dc, 11, (11, 22))
                    if dc % 2 == 0:
                        if dc + 2 < NC8:
                            prefetch_wp("dA", wdown_d[l], 0, dc + 2, 11, (0, 11))
                            prefetch_wp("dB", wdown_d[l], 0, dc + 2, 11, (11, 22))
                        elif hf + 1 < NH:
                            for gv in range(2):
                                prefetch_wp("u%d" % gv, wup_d[l], gv * FCH, 0, 8)
                    for ti, tt in enumerate(tts):
                        bank = next_bank()
                        matmul_group(bank, slice(0, 512),
                                     [((wdA[:, k, :] if k < 11 else wdB[:, k - 11, :]), aT[:, k, ti * 512:(ti + 1) * 512]) for k in range(FCH)],
                                     [BwdA, BwdB] + Ba)
                        resid_add(bank, dc, tt)
            S_.barrier()

    def ple(l, sq_i):
        with contextlib.ExitStack() as st:
            work = norm_work(st)
            pT = Ring([sb("pT%d" % i, [128, 2, 512], BF16, st) for i in range(2)])
            Sg = Ring([sb("Sg%d" % i, [128, 512], F32, st) for i in range(2)])
            hn, HN = hn_alloc(st, 2)
            bank_ring[0] = list(range(8))
            for tt in range(NTT):
                tl = tt % 2
                norm_tile(tt, "norm_ple", l, work, (hn, HN, tl))
                for q in range(4):
                    S_.dma("sp", pst[q][:], p_d[l, sq_i, tt * 512 + q * 128: tt * 512 + (q + 1) * 128, :], pstS[q], writes=[pstB[q]])
                pt, Bpt = pT.next()
                for ec in range(2):
                    bank = next_bank()
                    for q in range(4):
                        op("pe", lambda e, q=q, ec=ec, bank=bank: e.transpose(out=psum[bank][:, q * 128:(q + 1) * 128],
                                                                             in_=pst[q][:, ec * 128:(ec + 1) * 128], identity=ident),
                           reads=[pstB[q], Bcst], writes=[PB[bank]])
                    op("act", lambda e, pt=pt, ec=ec, bank=bank: e.activation(out=pt[:, ec, :], in_=psum[bank][:, :], func=AF.Identity),
                       reads=[PB[bank]], writes=[Bpt])
                for dc in range(NC8):
                    wg, Bwg = load_wp("pg", wpg_d[l], 0, dc, 8)
                    wp, Bwp = load_wp("pp", wpp_d[l], 0, dc, 2)
                    if dc % 2 == 0:
                        ndc = dc + 2 if dc + 2 < NC8 else (0 if tt + 1 < NTT else None)
                        if ndc is not None:
                            prefetch_wp("pg", wpg_d[l], 0, ndc, 8)
                            prefetch_wp("pp", wpp_d[l], 0, ndc, 2)
                    bg = next_bank()
                    matmul_group(bg, slice(0, 512), [(wg[:, k, :], hn[:, k, ts(tl)]) for k in range(8)],
                                 [Bwg] + [HN[k][tl] for k in range(8)])
                    bp = next_bank()
                    matmul_group(bp, slice(0, 512), [(wp[:, k, :], pt[:, k, :]) for k in range(2)], [Bwp, Bpt])
                    sg, Bsg = Sg.next()
                    op("act", lambda e, sg=sg, bg=bg: e.activation(out=sg[:], in_=psum[bg][:, :], func=AF.Sigmoid), reads=[PB[bg]], writes=[Bsg])
                    op("dve", lambda e, sg=sg, bp=bp: e.tensor_tensor(out=sg[:], in0=sg[:], in1=psum[bp][:, :], op=ALU.mult),
                       reads=[Bsg, PB[bp]], writes=[Bsg])
                    op("dve", lambda e, sg=sg, dc=dc: e.tensor_tensor(out=xT[:, dc, ts(tt)], in0=xT[:, dc, ts(tt)], in1=sg[:], op=ALU.add),
                       reads=[Bsg, X[dc][tt]], writes=[X[dc][tt]])
            S_.barrier()

    for sq_i in range(NSEQ):
        load_x(sq_i)
        for l in range(DEPTH):
            if l % 2 == 0:
                attention(l)
            else:
                rglru(l)
            ffn(l)
            ple(l, sq_i)
        final_store(sq_i)
    es.close()
    return nc


def prep_shared(inp, depth):
    pp = PP(depth)
    nr = max(depth // 2, 1)
    tab = np.zeros((128, pp.n), np.float32)

    def put(name, arr):
        tab[:, pp.off[name]:pp.off[name] + arr.shape[1]] = arr

    put("norm_mix", _cols(inp["norm_mix"], 8))
    put("norm_ffn", _cols(inp["norm_ffn"], 8))
    put("norm_ple", _cols(inp["norm_ple"], 8))
    put("norm_final", _cols(inp["norm_final"], 8))
    put("ffn_conv_w", _cols(inp["ffn_conv_w"], 44))
    put("ffn_conv_b", _cols(inp["ffn_conv_b"], 44))
    put("rnn_conv_w", _cols(inp["rnn_conv_w"], 10))
    put("rnn_conv_b", _cols(inp["rnn_conv_b"], 10))
    put("b_gate_a", _cols(inp["rnn_b_gate_a"], 10))
    put("b_gate_x", _cols(inp["rnn_b_gate_x"], 10))
    put("lru", _cols(inp["rnn_lru_param"], 10))
    shared = {
        "pp": tab,
        "cst": make_consts(),
        "wband_a": _band(np.asarray(inp["rnn_w_gate_a"], np.float32)),
        "wband_x": _band(np.asarray(inp["rnn_w_gate_x"], np.float32)),
    }
    for k in ("attn_w_qkv", "attn_w_o", "rnn_w_in", "rnn_w_out", "ffn_w_up", "ffn_w_down", "ple_w_gate", "ple_w_proj"):
        shared[k] = np.ascontiguousarray(np.asarray(inp[k], np.float32))
    return shared


def run(inp, n_cores, depth):
    x = np.asarray(inp["x"], np.float32)
    p = np.asarray(inp["p"], np.float32)
    B, S, _ = x.shape
    assert B % n_cores == 0
    nseq = B // n_cores
    shared = prep_shared(inp, depth)
    nc = build(nseq, S, depth)
    in_maps = []
    for i in range(n_cores):
        m = dict(shared)
        m["x"] = np.ascontiguousarray(x[i * nseq:(i + 1) * nseq])
        m["p"] = np.ascontiguousarray(p[:, i * nseq:(i + 1) * nseq])
        in_maps.append(m)
    res = run_bass_kernel_spmd(nc, in_maps, core_ids=list(range(n_cores)))
    return np.concatenate([np.asarray(r["out"], np.float32) for r in res.results], axis=0)


def kernel(**inputs):
    return run(inputs, N_CORES, 4)
```

```python
import contextlib
import numpy as np
import concourse.bass as bass
import concourse.mybir as mybir
from concourse.bass_utils import run_bass_kernel_spmd

F32 = mybir.dt.float32
BF16 = mybir.dt.bfloat16
AF = mybir.ActivationFunctionType
ALU = mybir.AluOpType

D = 1024
NC8 = 8
HEADS = 16
DH = 64
RW = 1280
RCH = 10
RB = 80
FF = 2816
FCH = 22
PLE = 256
EPS = 1e-6
LRU_C = 8.0
N_CORES = 8


class SemObj:
    def __init__(self, sem, name):
        self.sem = sem
        self.name = name
        self.count = 0


class Buf:
    __slots__ = ("name", "w", "r")

    def __init__(self, name=""):
        self.name = name
        self.w = None
        self.r = {}


class Sched:
    def __init__(self, nc):
        self.nc = nc
        self.eng = {"pe": nc.tensor, "act": nc.scalar, "dve": nc.vector, "pool": nc.gpsimd, "sp": nc.sync}
        self.esem = {k: SemObj(nc.alloc_semaphore("es_" + k), k) for k in self.eng}
        self.waited = {k: {} for k in self.eng}
        self.ndsem = 0
        self.ninst = 0

    def dsem(self):
        self.ndsem += 1
        return SemObj(self.nc.alloc_semaphore("ds%d" % self.ndsem), "ds%d" % self.ndsem)

    def _deps(self, en, reads, writes):
        deps = {}

        def add(tok):
            if tok is None:
                return
            so, v = tok
            if en == "pe" and so is self.esem["pe"]:
                return
            if deps.get(so, 0) < v:
                deps[so] = v

        for b in reads:
            add(b.w)
        for b in writes:
            add(b.w)
            for so, v in b.r.items():
                add((so, v))
        w = self.waited[en]
        e = self.eng[en]
        for so, v in deps.items():
            if w.get(so, 0) >= v:
                continue
            assert so.count >= v, "waiting on un-issued increment %s %d>%d" % (so.name, v, so.count)
            e.wait_ge(so.sem, v)
            self.ninst += 1
            w[so] = v

    def _mark(self, tok, reads, writes):
        so, v = tok
        for b in reads:
            if b.r.get(so, 0) < v:
                b.r[so] = v
        for b in writes:
            b.w = tok
            b.r = {}

    def op(self, en, fn, reads=(), writes=(), inc=True):
        self._deps(en, reads, writes)
        ins = fn(self.eng[en])
        self.ninst += 1
        so = self.esem[en]
        if inc:
            so.count += 1
            ins.then_inc(so.sem, 1)
            tok = (so, so.count)
        else:
            tok = (so, so.count + 1)
        self._mark(tok, reads, writes)
        return ins

    def dma(self, q, out, in_, ds, reads=(), writes=(), **kw):
        self._deps(q, reads, writes)
        ins = self.eng[q].dma_start(out=out, in_=in_, **kw)
        self.ninst += 1
        ds.count += 16
        ins.then_inc(ds.sem, 16)
        self._mark((ds, ds.count), reads, writes)
        return ins

    def wait_all(self, en, bufs):
        self._deps(en, bufs, bufs)

    def sync_from(self, en, others=("pe", "act", "dve")):
        w = self.waited[en]
        for other in others:
            so = self.esem[other]
            if so.count > w.get(so, 0):
                self.eng[en].wait_ge(so.sem, so.count)
                self.ninst += 1
                w[so] = so.count

    def barrier(self, engines=("pe", "act", "dve", "pool")):
        for en in engines:
            w = self.waited[en]
            for other in engines:
                if other == en:
                    continue
                so = self.esem[other]
                if so.count > w.get(so, 0):
                    self.eng[en].wait_ge(so.sem, so.count)
                    self.ninst += 1
                    w[so] = so.count


class Ring:
    def __init__(self, tiles):
        self.tiles = tiles
        self.bufs = [Buf() for _ in tiles]
        self.i = 0

    def next(self):
        k = self.i % len(self.tiles)
        self.i += 1
        return self.tiles[k], self.bufs[k]


def _cols(v, nch):
    lead = int(np.prod(v.shape[:-1])) if v.ndim > 1 else 1
    a = np.asarray(v, np.float32).reshape(lead, nch, 128)
    return np.ascontiguousarray(a.transpose(2, 0, 1).reshape(128, lead * nch))


def band_range(fc):
    b_lo = (128 * fc) // RB
    b_hi = (128 * fc + 127) // RB
    k_lo = (RB * b_lo) // 128
    k_hi = (RB * b_hi + RB - 1) // 128
    return k_lo, min(k_hi, RCH - 1)


def _band(wg):
    L = wg.shape[0]
    dense = np.zeros((L, RW, RW), np.float32)
    for h in range(RW // RB):
        dense[:, h * RB:(h + 1) * RB, h * RB:(h + 1) * RB] = wg[:, h]
    out = np.zeros((L, 128, RCH, 3, 128), np.float32)
    for fc in range(RCH):
        k_lo, k_hi = band_range(fc)
        for kk, kc in enumerate(range(k_lo, k_hi + 1)):
            out[:, :, fc, kk, :] = dense[:, kc * 128:(kc + 1) * 128, fc * 128:(fc + 1) * 128]
    return out


class PP:
    def __init__(self, depth):
        self.off = {}
        self.n = 0
        nr = max(depth // 2, 1)
        for name, cnt in [("norm_mix", depth * 8), ("norm_ffn", depth * 8), ("norm_ple", depth * 8), ("norm_final", 8),
                          ("ffn_conv_w", depth * 3 * 44), ("ffn_conv_b", depth * 44),
                          ("rnn_conv_w", nr * 4 * 10), ("rnn_conv_b", nr * 10), ("b_gate_a", nr * 10),
                          ("b_gate_x", nr * 10), ("lru", nr * 10), ("lruc", nr * 10)]:
            self.off[name] = self.n
            self.n += cnt


def make_consts():
    j = np.arange(128)[:, None]
    s = np.arange(128)[None, :]
    ident = (j == s).astype(np.float32)
    negtri = -(j >= s).astype(np.float32)
    negones = -np.ones((128, 128), np.float32)
    masklt = (j < s).astype(np.float32)
    ones = np.ones((128, 128), np.float32)
    return np.ascontiguousarray(np.concatenate([ident, negtri, negones, masklt, ones], axis=1))


def build(NSEQ, S, DEPTH):
    assert S % 512 == 0
    NTT = S // 512
    NQB = S // 128
    TH = min(S, 1024)
    NH = S // TH
    NTH = TH // 512
    NATT = (DEPTH + 1) // 2
    NRNN = max(DEPTH // 2, 1)
    pp = PP(DEPTH)

    nc = bass.Bass("TRN2", target_bir_lowering=False)

    def dram(name, shape, kind="ExternalInput", dt=F32):
        return nc.dram_tensor(name, list(shape), dt, kind=kind).ap()

    x_d = dram("x", [NSEQ, S, D])
    p_d = dram("p", [DEPTH, NSEQ, S, PLE])
    out_d = dram("out", [NSEQ, S, D], kind="ExternalOutput")
    wqkv_d = dram("attn_w_qkv", [NATT, D, 3 * D])
    wo_d = dram("attn_w_o", [NATT, D, D])
    win_d = dram("rnn_w_in", [NRNN, D, 2 * RW])
    wout_d = dram("rnn_w_out", [NRNN, RW, D])
    wup_d = dram("ffn_w_up", [DEPTH, D, 2 * FF])
    wdown_d = dram("ffn_w_down", [DEPTH, FF, D])
    wpg_d = dram("ple_w_gate", [DEPTH, D, D])
    wpp_d = dram("ple_w_proj", [DEPTH, PLE, D])
    wba_d = dram("wband_a", [NRNN, 128, RCH, 3, 128])
    wbx_d = dram("wband_x", [NRNN, 128, RCH, 3, 128])
    pp_d = dram("pp", [128, pp.n])
    cst_d = dram("cst", [128, 5 * 128])

    S_ = Sched(nc)
    op = S_.op
    es = contextlib.ExitStack()

    uid = [0]

    def sb(name, shape, dt, stack=None):
        uid[0] += 1
        return (stack or es).enter_context(nc.sbuf_tensor("%s_u%d" % (name, uid[0]), list(shape), dt))

    xT = sb("xT", [128, NC8, S], F32)
    ppt = sb("ppt", [128, pp.n], F32)
    cst32 = sb("cst32", [128, 5 * 128], F32)
    cstb = sb("cstb", [128, 5 * 128], BF16)
    ident = cst32[:, 0:128]
    negtri = cstb[:, 128:256]
    negones = cstb[:, 256:384]
    masklt = cstb[:, 384:512]
    onesb = cstb[:, 512:640]
    NW = 4
    WSZ = FCH * 128
    wslots = [sb("w%d" % i, [128, WSZ], BF16) for i in range(NW)]
    wbufs = [Buf("w%d" % i) for i in range(NW)]
    wsems = [S_.dsem() for _ in range(NW)]
    wcnt = [0]
    stgS = [S_.dsem() for _ in range(4)]
    pstS = [S_.dsem() for _ in range(4)]
    pst = [sb("pst%d" % i, [128, PLE], F32) for i in range(4)]
    pstB = [Buf() for _ in range(4)]
    carry_f = sb("carry_f", [128, 2 * FCH, 2], F32)
    carry_r = sb("carry_r", [128, RCH, 3], F32)
    carry_h = sb("carry_h", [128, RCH], F32)
    Bcf = [Buf() for _ in range(2 * FCH)]
    Bcr = [Buf() for _ in range(RCH)]
    Bch = [Buf() for _ in range(RCH)]

    psum = [es.enter_context(nc.psum_tensor("ps%d" % i, [128, 512], F32)) for i in range(8)]
    PB = [Buf("ps%d" % i) for i in range(8)]
    bank_ring = [list(range(8))]
    bank_i = [0]

    def next_bank():
        r = bank_ring[0]
        b = r[bank_i[0] % len(r)]
        bank_i[0] += 1
        return b

    X = [[Buf() for _ in range(NTT)] for _ in range(NC8)]
    Bpp = Buf("pp")
    Bcst = Buf("cst")

    def ts(tt):
        return slice(tt * 512, (tt + 1) * 512)

    def ppc(name, idx):
        o = pp.off[name] + idx
        return ppt[:, o:o + 1]

    d0 = S_.dsem()
    S_.dma("sp", ppt[:], pp_d[:, :], d0, writes=[Bpp])
    d1 = S_.dsem()
    S_.dma("sp", cst32[:], cst_d[:, :], d1, writes=[Bcst])
    op("dve", lambda e: e.tensor_copy(out=cstb[:], in_=cst32[:]), reads=[Bcst], writes=[Bcst])
    nl = NRNN * RCH
    lo, lc = pp.off["lru"], pp.off["lruc"]
    op("act", lambda e: e.activation(out=ppt[:, lc:lc + nl], in_=ppt[:, lo:lo + nl], func=AF.Exp, scale=-1.0),
       reads=[Bpp], writes=[Bpp])
    op("act", lambda e: e.activation(out=ppt[:, lc:lc + nl], in_=ppt[:, lc:lc + nl], func=AF.Ln, bias=1.0),
       reads=[Bpp], writes=[Bpp])
    op("dve", lambda e: e.tensor_scalar(out=ppt[:, lc:lc + nl], in0=ppt[:, lc:lc + nl], scalar1=-LRU_C, scalar2=None,
                                        op0=ALU.mult), reads=[Bpp], writes=[Bpp])

    def load_w(view, K, n=128):
        i = wcnt[0] % NW
        wcnt[0] += 1
        t = wslots[i][:, 0:K * n].rearrange("p (k n) -> p k n", k=K)
        S_.dma("pool", t, view, wsems[i], writes=[wbufs[i]])
        return t, wbufs[i]

    pair_cache = {}

    def load_wp(name, w_ap, col0_chunk, idx, K, krange=None):
        if idx % 2 == 0:
            key = (name, col0_chunk + idx)
            if key in stash:
                pair_cache[name] = stash.pop(key)
            else:
                pair_cache[name] = slab_load(w_ap, col0_chunk + idx, K, krange)
        t, B = pair_cache[name]
        return t[:, :, (idx % 2) * 128:(idx % 2 + 1) * 128], B

    stash = {}

    def slab_load(w_ap, chunk, K, krange):
        f0 = chunk * 128
        v = w_ap.rearrange("(k p) f -> p k f", p=128)
        v = v[:, krange[0]:krange[1], f0:f0 + 256] if krange else v[:, :, f0:f0 + 256]
        return load_w(v, K, 256)

    def prefetch_wp(name, w_ap, col0_chunk, idx, K, krange=None):
        key = (name, col0_chunk + idx)
        if key not in stash:
            stash[key] = slab_load(w_ap, col0_chunk + idx, K, krange)

    def wview(w_ap, f0, n=128):
        return w_ap.rearrange("(k p) f -> p k f", p=128)[:, :, f0:f0 + n]

    def matmul_group(bank, cols, pairs, reads, inc_last=True):
        n = len(pairs)
        for k, (l, r) in enumerate(pairs):
            op("pe", lambda e, l=l, r=r, k=k: e.matmul(psum[bank][:, cols], lhsT=l, rhs=r, start=(k == 0), stop=(k == n - 1)),
               reads=reads, writes=[PB[bank]], inc=(inc_last and k == n - 1))

    def norm_tile(tt, gname, gidx, work, dst=None, out_f32=None):
        sq_ring, rs_ring = work
        if dst is not None:
            hn, HN, tl = dst
        bank = next_bank()
        for c in range(NC8):
            sq, Bsq = sq_ring.next()
            op("act", lambda e, c=c, sq=sq: e.activation(out=sq[:], in_=xT[:, c, ts(tt)], func=AF.Square),
               reads=[X[c][tt]], writes=[Bsq])
            op("pe", lambda e, c=c, sq=sq: e.matmul(psum[bank][:, :], lhsT=onesb, rhs=sq[:], start=(c == 0), stop=(c == NC8 - 1)),
               reads=[Bsq, Bcst], writes=[PB[bank]])
        rs, Brs = rs_ring.next()
        op("act", lambda e: e.activation(out=rs[:], in_=psum[bank][:, :], func=AF.Sqrt, scale=1.0 / D, bias=EPS),
           reads=[PB[bank]], writes=[Brs])
        op("dve", lambda e: e.reciprocal(out=rs[:], in_=rs[:]), reads=[Brs], writes=[Brs])
        for c in range(NC8):
            o = hn[:, c, ts(tl)] if out_f32 is None else out_f32[0][:, c, :]
            wb = [HN[c][tl]] if out_f32 is None else [out_f32[1]]
            op("dve", lambda e, c=c, o=o: e.scalar_tensor_tensor(out=o, in0=xT[:, c, ts(tt)], scalar=ppc(gname, gidx * 8 + c),
                                                                 in1=rs[:], op0=ALU.mult, op1=ALU.mult),
               reads=[X[c][tt], Brs, Bpp], writes=wb)

    def resid_add(bank, dc, tt):
        op("dve", lambda e: e.tensor_tensor(out=xT[:, dc, ts(tt)], in0=xT[:, dc, ts(tt)], in1=psum[bank][:, :], op=ALU.add),
           reads=[PB[bank], X[dc][tt]], writes=[X[dc][tt]])

    def hn_alloc(stack, ntl):
        t = sb("hn", [128, NC8, ntl * 512], BF16, stack)
        return t, [[Buf() for _ in range(ntl)] for _ in range(NC8)]

    def norm_work(stack):
        sq_ring = Ring([sb("sq%d" % i, [128, 512], BF16, stack) for i in range(2)])
        rs_ring = Ring([sb("rs%d" % i, [128, 512], F32, stack) for i in range(2)])
        return sq_ring, rs_ring

    def load_x(sq_i):
        bank_ring[0] = list(range(8))
        st = contextlib.ExitStack()
        stg = [sb("stg%d" % i, [128, D], F32, st) for i in range(4)]
        stgB = [Buf() for _ in range(4)]
        S_.sync_from("sp")
        for tt in range(NTT):
            for q in range(4):
                S_.dma("sp", stg[q][:], x_d[sq_i, tt * 512 + q * 128: tt * 512 + (q + 1) * 128, :], stgS[q], writes=[stgB[q]])
            for c in range(NC8):
                bank = next_bank()
                for q in range(4):
                    op("pe", lambda e, q=q, c=c: e.transpose(out=psum[bank][:, q * 128:(q + 1) * 128],
                                                             in_=stg[q][:, c * 128:(c + 1) * 128], identity=ident),
                       reads=[stgB[q], Bcst], writes=[PB[bank]])
                op("act", lambda e, c=c: e.activation(out=xT[:, c, ts(tt)], in_=psum[bank][:, :], func=AF.Identity),
                   reads=[PB[bank]], writes=[X[c][tt]])
        S_.barrier()
        st.close()

    def final_store(sq_i):
        with contextlib.ExitStack() as st:
            work = norm_work(st)
            yf = sb("yf", [128, NC8, 512], F32, st)
            Byf = Buf()
            stg = [sb("stg%d" % i, [128, D], F32, st) for i in range(4)]
            stgB = [Buf() for _ in range(4)]
            bank_ring[0] = list(range(8))
            for tt in range(NTT):
                norm_tile(tt, "norm_final", 0, work, out_f32=(yf, Byf))
                for q in range(4):
                    b0, b1 = next_bank(), next_bank()
                    for c in range(NC8):
                        bank = b0 if c < 4 else b1
                        op("pe", lambda e, c=c, bank=bank: e.transpose(out=psum[bank][:, (c % 4) * 128:(c % 4 + 1) * 128],
                                                                      in_=yf[:, c, q * 128:(q + 1) * 128], identity=ident),
                           reads=[Byf, Bcst], writes=[PB[bank]])
                    op("act", lambda e: e.activation(out=stg[q][:, 0:512], in_=psum[b0][:, :], func=AF.Identity),
                       reads=[PB[b0]], writes=[stgB[q]])
                    op("dve", lambda e: e.tensor_copy(out=stg[q][:, 512:1024], in_=psum[b1][:, :]),
                       reads=[PB[b1]], writes=[stgB[q]])
                    S_.dma("sp", out_d[sq_i, tt * 512 + q * 128: tt * 512 + (q + 1) * 128, :], stg[q][:], stgS[q], reads=[stgB[q]])
            for en in ("sp", "pe", "act", "dve"):
                S_.wait_all(en, stgB)
            S_.barrier()

    def attention(l):
        slot = l // 2
        with contextlib.ExitStack() as st:
            work = norm_work(st)
            OT = sb("OT", [128, NC8, S], BF16, st)
            BO = [[Buf() for _ in range(NTT)] for _ in range(NC8)]
            QT = sb("QT", [128, S], BF16, st)
            KT = sb("KT", [128, S], BF16, st)
            Vb = sb("Vb", [128, NQB, 128], BF16, st)
            BQ, BK, BV = Buf(), Buf(), Buf()
            NSTR = 4
            Er = [Ring([sb("E%d_%d" % (s, i), [128, 512], BF16, st) for i in range(1)]) for s in range(NSTR)]
            Lr = [Ring([sb("L%d_%d" % (s, i), [128, 512], BF16, st) for i in range(2)]) for s in range(NSTR)]
            Wr = [Ring([sb("W%d_%d" % (s, i), [128, 512], BF16, st) for i in range(2)]) for s in range(NSTR)]
            Lsuf = [sb("Ls%d" % s, [128, 512], BF16, st) for s in range(NSTR)]
            BLs = [Buf() for _ in range(NSTR)]

            hn, HN = hn_alloc(st, NTT)
            bank_ring[0] = list(range(8))
            for tt in range(NTT):
                norm_tile(tt, "norm_mix", l, work, (hn, HN, tt))
            bank_ring[0] = [6, 7]
            if NTT == 4:
                pairs = [[0, 3], [1, 2]]
            elif NTT == 2:
                pairs = [[0], [1]]
            else:
                pairs = [list(range(NTT))]
            for c in range(NC8):
                wq, Bwq = load_wp("q", wqkv_d[slot], 0, c, 8)
                wk, Bwk = load_wp("k", wqkv_d[slot], 8, c, 8)
                wv, Bwv = load_wp("v", wqkv_d[slot], 16, c, 8)
                for tt in range(NTT):
                    bank = next_bank()
                    matmul_group(bank, slice(0, 512), [(wq[:, k, :], hn[:, k, ts(tt)]) for k in range(8)],
                                 [Bwq] + [HN[k][tt] for k in range(8)])
                    op("dve", lambda e: e.tensor_scalar(out=QT[:, ts(tt)], in0=psum[bank][:, :], scalar1=DH ** -0.5, scalar2=None,
                                                        op0=ALU.mult), reads=[PB[bank]], writes=[BQ])
                    bank = next_bank()
                    matmul_group(bank, slice(0, 512), [(wk[:, k, :], hn[:, k, ts(tt)]) for k in range(8)],
                                 [Bwk] + [HN[k][tt] for k in range(8)])
                    op("dve", lambda e: e.tensor_copy(out=KT[:, ts(tt)], in_=psum[bank][:, :]), reads=[PB[bank]], writes=[BK])
                for tt in range(NTT):
                    bank = next_bank()
                    for q in range(4):
                        tb = tt * 4 + q
                        matmul_group(bank, slice(q * 128, (q + 1) * 128),
                                     [(hn[:, k, tb * 128:(tb + 1) * 128], wv[:, k, :]) for k in range(8)],
                                     [Bwv] + [HN[k][tt] for k in range(8)])
                    op("dve", lambda e: e.tensor_copy(out=Vb[:, tt * 4:(tt + 1) * 4, :],
                                                      in_=psum[bank][:, :].rearrange("p (q n) -> p q n", q=4)),
                       reads=[PB[bank]], writes=[BV])
                steps = []
                for ps_i, tcs in enumerate(pairs):
                    lst = []
                    for tc in tcs:
                        kbs = list(range(4 * tc + 3, -1, -1))
                        for idx, kb in enumerate(kbs):
                            lst.append((tc, kb, idx, idx == len(kbs) - 1))
                    steps.append(lst)
                nround = max(len(s) for s in steps)
                acts, infos = [], []
                for r in range(nround):
                    act = [(ps_i, hh) for ps_i in range(len(pairs)) if r < len(steps[ps_i]) for hh in range(2)]
                    info = {}
                    for (ps_i, hh) in act:
                        tc, kb, idx, last = steps[ps_i][r]
                        sid = ps_i * 2 + hh
                        off = max(0, kb - 4 * tc)
                        c0 = off * 128
                        diag = kb >= 4 * tc
                        info[(ps_i, hh)] = (tc, kb, idx, last, sid, c0, diag)
                    acts.append(act)
                    infos.append(info)

                def emit_z(r):
                    for key in acts[r]:
                        tc, kb, idx, last, sid, c0, diag = infos[r][key]
                        hh = key[1]
                        hp = slice(64 * hh, 64 * hh + 64)
                        if idx == 0:
                            op("dve", lambda e, sid=sid: e.memset(Lsuf[sid][:], 0.0), writes=[BLs[sid]])
                        op("pe", lambda e, sid=sid, hp=hp, kb=kb, tc=tc, c0=c0: e.matmul(
                            psum[sid][:, c0:512], lhsT=KT[hp, kb * 128:(kb + 1) * 128], rhs=QT[hp, tc * 512 + c0:(tc + 1) * 512],
                            start=True, stop=False, skip_group_check=True), reads=[BK, BQ], writes=[PB[sid]])

                emit_z(0)
                for r in range(nround):
                    act, info = acts[r], infos[r]
                    tiles = {}
                    for key in act:
                        tc, kb, idx, last, sid, c0, diag = info[key]
                        Et, BE = Er[sid].next()
                        op("act", lambda e, Et=Et, sid=sid, c0=c0: e.activation(out=Et[:, c0:512], in_=psum[sid][:, c0:512], func=AF.Exp),
                           reads=[PB[sid]], writes=[BE])
                        tiles[key] = [Et, BE]
                    for key in act:
                        tc, kb, idx, last, sid, c0, diag = info[key]
                        Et, BE = tiles[key]
                        Lt, BL = Lr[sid].next()
                        op("act", lambda e, Et=Et, Lt=Lt, c0=c0: e.activation(out=Lt[:, c0:512], in_=Et[:, c0:512], func=AF.Ln, bias=1.0),
                           reads=[BE], writes=[BL])
                        if diag:
                            op("dve", lambda e, Lt=Lt, c0=c0: e.tensor_tensor(out=Lt[:, c0:c0 + 128], in0=Lt[:, c0:c0 + 128], in1=masklt, op=ALU.mult),
                               reads=[BL, Bcst], writes=[BL])
                        tiles[key] += [Lt, BL]
                    for key in act:
                        tc, kb, idx, last, sid, c0, diag = info[key]
                        Et, BE, Lt, BL = tiles[key]
                        op("pe", lambda e, sid=sid, Lt=Lt, c0=c0, idx=idx: e.matmul(
                            psum[sid][:, c0:512], lhsT=negtri, rhs=Lt[:, c0:512], start=False, stop=(idx == 0), skip_group_check=True),
                           reads=[BL, Bcst], writes=[PB[sid]])
                        if idx > 0:
                            op("pe", lambda e, sid=sid, c0=c0: e.matmul(
                                psum[sid][:, c0:512], lhsT=negones, rhs=Lsuf[sid][:, c0:512], start=False, stop=True, skip_group_check=True),
                               reads=[BLs[sid], Bcst], writes=[PB[sid]])
                    for key in act:
                        tc, kb, idx, last, sid, c0, diag = info[key]
                        Wt, BW = Wr[sid].next()
                        op("act", lambda e, Wt=Wt, sid=sid, c0=c0: e.activation(out=Wt[:, c0:512], in_=psum[sid][:, c0:512], func=AF.Exp),
                           reads=[PB[sid]], writes=[BW])
                        if diag:
                            op("dve", lambda e, Wt=Wt, c0=c0: e.tensor_tensor(out=Wt[:, c0:c0 + 128], in0=Wt[:, c0:c0 + 128], in1=masklt, op=ALU.mult),
                               reads=[BW, Bcst], writes=[BW])
                        tiles[key] += [Wt, BW]
                    if r + 1 < nround:
                        emit_z(r + 1)
                    for key in act:
                        tc, kb, idx, last, sid, c0, diag = info[key]
                        ps_i, hh = key
                        Et, BE, Lt, BL, Wt, BW = tiles[key]
                        pob = 4 + ps_i
                        op("pe", lambda e, pob=pob, hh=hh, kb=kb, Wt=Wt, c0=c0, idx=idx, last=last: e.matmul(
                            psum[pob][64 * hh:64 * hh + 64, c0:512], lhsT=Vb[:, kb, 64 * hh:64 * hh + 64], rhs=Wt[:, c0:512],
                            start=(idx == 0), stop=last, skip_group_check=True), reads=[BW, BV], writes=[PB[pob]])
                        if not last:
                            op("dve", lambda e, sid=sid, Lt=Lt, c0=c0: e.tensor_tensor(out=Lsuf[sid][:, c0:512], in0=Lsuf[sid][:, c0:512],
                                                                                      in1=Lt[:, c0:512], op=ALU.add),
                               reads=[BL, BLs[sid]], writes=[BLs[sid]])
                    for ps_i in range(len(pairs)):
                        if r < len(steps[ps_i]) and steps[ps_i][r][3]:
                            tc = steps[ps_i][r][0]
                            pob = 4 + ps_i
                            op("dve", lambda e, pob=pob, tc=tc: e.tensor_copy(out=OT[:, c, ts(tc)], in_=psum[pob][:, :]),
                               reads=[PB[pob]], writes=[BO[c][tc]])
            bank_ring[0] = list(range(8))
            for dc in range(NC8):
                wo, Bwo = load_wp("o", wo_d[slot], 0, dc, 8)
                if dc % 2 == 0 and dc + 2 < NC8:
                    prefetch_wp("o", wo_d[slot], 0, dc + 2, 8)
                for tt in range(NTT):
                    bank = next_bank()
                    matmul_group(bank, slice(0, 512), [(wo[:, k, :], OT[:, k, ts(tt)]) for k in range(8)],
                                 [Bwo] + [BO[k][tt] for k in range(8)])
                    resid_add(bank, dc, tt)
            S_.barrier()

    def rglru(l):
        slot = l // 2
        with contextlib.ExitStack() as st:
            work = norm_work(st)
            xr = sb("xr", [128, RCH, TH], BF16, st)
            G = sb("G", [128, RCH, TH], BF16, st)
            Bxr = [Buf() for _ in range(RCH)]
            BG = [Buf() for _ in range(RCH)]
            Ur = Ring([sb("Ur%d" % i, [128, TH + 3], F32, st) for i in range(2)])
            Yr = Ring([sb("Yr%d" % i, [128, TH], F32, st) for i in range(2)])
            T1r = Ring([sb("T1_%d" % i, [128, TH], F32, st) for i in range(2)])
            T2r = Ring([sb("T2_%d" % i, [128, TH], F32, st) for i in range(2)])
            T3r = Ring([sb("T3_%d" % i, [128, TH], F32, st) for i in range(2)])
            hn, HN = hn_alloc(st, NTH)
            bank_ring[0] = list(range(8))
            for hf in range(NH):
                h0 = hf * TH
                tts = list(range(hf * NTH, (hf + 1) * NTH))
                for ti, tt in enumerate(tts):
                    norm_tile(tt, "norm_mix", l, work, (hn, HN, ti))
                for j in range(RCH):
                    wr, Bwr = load_wp("r", win_d[slot], RCH, j, 8)
                    wg, Bwg = load_wp("g", win_d[slot], 0, j, 8)
                    if j % 2 == 0 and j + 2 < RCH:
                        prefetch_wp("r", win_d[slot], RCH, j + 2, 8)
                        prefetch_wp("g", win_d[slot], 0, j + 2, 8)
                    U, BU = Ur.next()
                    if hf == 0:
                        op("pool", lambda e, U=U: e.memset(U[:, 0:3], 0.0), writes=[BU])
                    else:
                        op("pool", lambda e, U=U, j=j: e.tensor_copy(out=U[:, 0:3], in_=carry_r[:, j, :]), reads=[Bcr[j]], writes=[BU])
                    for ti, tt in enumerate(tts):
                        bank = next_bank()
                        matmul_group(bank, slice(0, 512), [(wr[:, k, :], hn[:, k, ts(ti)]) for k in range(8)],
                                     [Bwr] + [HN[k][ti] for k in range(8)])
                        op("act", lambda e, U=U, ti=ti: e.activation(out=U[:, 3 + ti * 512:3 + (ti + 1) * 512], in_=psum[bank][:, :], func=AF.Identity),
                           reads=[PB[bank]], writes=[BU])
                    if hf + 1 < NH:
                        op("pool", lambda e, U=U, j=j: e.tensor_copy(out=carry_r[:, j, :], in_=U[:, TH:TH + 3]), reads=[BU], writes=[Bcr[j]])
                    Y, BY = Yr.next()
                    cw = lambda k, j=j: ppc("rnn_conv_w", (slot * 4 + k) * RCH + j)
                    op("act", lambda e, U=U, Y=Y, j=j: e.activation(out=Y[:], in_=U[:, 3:3 + TH], func=AF.Identity,
                                                                   scale=cw(3), bias=ppc("rnn_conv_b", slot * RCH + j)),
                       reads=[BU, Bpp], writes=[BY])
                    for k in (2, 1):
                        op("dve", lambda e, U=U, Y=Y, k=k: e.scalar_tensor_tensor(out=Y[:], in0=U[:, k:k + TH], scalar=cw(k), in1=Y[:],
                                                                                  op0=ALU.mult, op1=ALU.add),
                           reads=[BU, BY, Bpp], writes=[BY])
                    op("dve", lambda e, U=U, Y=Y, j=j: e.scalar_tensor_tensor(out=xr[:, j, :], in0=U[:, 0:TH], scalar=cw(0), in1=Y[:],
                                                                              op0=ALU.mult, op1=ALU.add),
                       reads=[BU, BY, Bpp], writes=[Bxr[j]])
                    for ti, tt in enumerate(tts):
                        bank = next_bank()
                        matmul_group(bank, slice(0, 512), [(wg[:, k, :], hn[:, k, ts(ti)]) for k in range(8)],
                                     [Bwg] + [HN[k][ti] for k in range(8)])
                        op("act", lambda e, j=j, ti=ti: e.activation(out=G[:, j, ti * 512:(ti + 1) * 512], in_=psum[bank][:, :], func=AF.Gelu_apprx_tanh),
                           reads=[PB[bank]], writes=[BG[j]])
                for fc in range(RCH):
                    k_lo, k_hi = band_range(fc)
                    nk = k_hi - k_lo + 1
                    wa, Bwa = load_w(wba_d[slot, :, fc, 0:nk, :], nk)
                    wx, Bwx = load_w(wbx_d[slot, :, fc, 0:nk, :], nk)
                    T1, B1 = T1r.next()
                    T2, B2 = T2r.next()
                    T3, B3 = T3r.next()
                    for (wt_, Bw_, T_, B_, bn) in ((wa, Bwa, T1, B1, "b_gate_a"), (wx, Bwx, T2, B2, "b_gate_x")):
                        for ti in range(NTH):
                            bank = next_bank()
                            matmul_group(bank, slice(0, 512),
                                         [(wt_[:, kk, :], xr[:, k_lo + kk, ti * 512:(ti + 1) * 512]) for kk in range(nk)],
                                         [Bw_] + [Bxr[k_lo + kk] for kk in range(nk)])
                            op("act", lambda e, T_=T_, ti=ti, bn=bn, bank=bank: e.activation(
                                out=T_[:, ti * 512:(ti + 1) * 512], in_=psum[bank][:, :], func=AF.Sigmoid, bias=ppc(bn, slot * RCH + fc)),
                               reads=[PB[bank], Bpp], writes=[B_])
                    op("act", lambda e, T1=T1: e.activation(out=T1[:], in_=T1[:], func=AF.Exp, scale=ppc("lruc", slot * RCH + fc)),
                       reads=[B1, Bpp], writes=[B1])
                    op("act", lambda e, T1=T1, T3=T3: e.activation(out=T3[:], in_=T1[:], func=AF.Square), reads=[B1], writes=[B3])
                    op("act", lambda e, T3=T3: e.activation(out=T3[:], in_=T3[:], func=AF.Sqrt, scale=-1.0, bias=1.0), reads=[B3], writes=[B3])
                    op("dve", lambda e, T2=T2: e.tensor_tensor(out=T2[:], in0=T2[:], in1=xr[:, fc, :], op=ALU.mult), reads=[B2, Bxr[fc]], writes=[B2])
                    op("dve", lambda e, T2=T2, T3=T3: e.tensor_tensor(out=T2[:], in0=T2[:], in1=T3[:], op=ALU.mult), reads=[B2, B3], writes=[B2])
                    init = 0.0 if hf == 0 else carry_h[:, fc:fc + 1]
                    op("dve", lambda e, T1=T1, T2=T2, T3=T3, init=init: e.tensor_tensor_scan(out=T3[:], data0=T1[:], data1=T2[:], initial=init,
                                                                                            op0=ALU.mult, op1=ALU.add),
                       reads=[B1, B2, B3, Bch[fc]], writes=[B3])
                    if hf + 1 < NH:
                        op("dve", lambda e, T3=T3: e.tensor_copy(out=carry_h[:, fc:fc + 1], in_=T3[:, TH - 1:TH]), reads=[B3], writes=[Bch[fc]])
                    op("dve", lambda e, T3=T3: e.tensor_tensor(out=G[:, fc, :], in0=G[:, fc, :], in1=T3[:], op=ALU.mult), reads=[B3, BG[fc]], writes=[BG[fc]])
                for dc in range(NC8):
                    wo, Bwo = load_wp("ro", wout_d[slot], 0, dc, RCH)
                    if dc % 2 == 0 and dc + 2 < NC8:
                        prefetch_wp("ro", wout_d[slot], 0, dc + 2, RCH)
                    for ti, tt in enumerate(tts):
                        bank = next_bank()
                        matmul_group(bank, slice(0, 512), [(wo[:, k, :], G[:, k, ti * 512:(ti + 1) * 512]) for k in range(RCH)],
                                     [Bwo] + BG)
                        resid_add(bank, dc, tt)
            S_.barrier()

    def ffn(l):
        with contextlib.ExitStack() as st:
            work = norm_work(st)
            aT = sb("aT", [128, FCH, TH], BF16, st)
            Ba = [Buf() for _ in range(FCH)]
            Ur = Ring([sb("Uf%d" % i, [128, TH + 2], F32, st) for i in range(4)])
            hn, HN = hn_alloc(st, NTH)
            Yr = Ring([sb("Yf%d" % i, [128, TH], F32, st) for i in range(4)])
            Ybr = Ring([sb("Yb%d" % i, [128, TH], BF16, st) for i in range(2)])
            Gr = Ring([sb("Gf%d" % i, [128, TH], BF16, st) for i in range(2)])
            bank_ring[0] = list(range(8))
            for hf in range(NH):
                tts = list(range(hf * NTH, (hf + 1) * NTH))
                for ti, tt in enumerate(tts):
                    norm_tile(tt, "norm_ffn", l, work, (hn, HN, ti))
                for j in range(FCH):
                    res = []
                    cur = [load_wp("u%d" % gv, wup_d[l], gv * FCH, j, 8) for gv in range(2)]
                    if j % 2 == 0:
                        if j + 2 < FCH:
                            for gv in range(2):
                                prefetch_wp("u%d" % gv, wup_d[l], gv * FCH, j + 2, 8)
                        else:
                            prefetch_wp("dA", wdown_d[l], 0, 0, 11, (0, 11))
                            prefetch_wp("dB", wdown_d[l], 0, 0, 11, (11, 22))
                    for gv in range(2):
                        fch = gv * FCH + j
                        w_, Bw_ = cur[gv]
                        U, BU = Ur.next()
                        if hf == 0:
                            op("pool", lambda e, U=U: e.memset(U[:, 0:2], 0.0), writes=[BU])
                        else:
                            op("pool", lambda e, U=U, fch=fch: e.tensor_copy(out=U[:, 0:2], in_=carry_f[:, fch, :]), reads=[Bcf[fch]], writes=[BU])
                        for ti, tt in enumerate(tts):
                            bank = next_bank()
                            matmul_group(bank, slice(0, 512), [(w_[:, k, :], hn[:, k, ts(ti)]) for k in range(8)],
                                         [Bw_] + [HN[k][ti] for k in range(8)])
                            op("act", lambda e, U=U, ti=ti, bank=bank: e.activation(out=U[:, 2 + ti * 512:2 + (ti + 1) * 512], in_=psum[bank][:, :], func=AF.Identity),
                               reads=[PB[bank]], writes=[BU])
                        if hf + 1 < NH:
                            op("pool", lambda e, U=U, fch=fch: e.tensor_copy(out=carry_f[:, fch, :], in_=U[:, TH:TH + 2]), reads=[BU], writes=[Bcf[fch]])
                        Y, BY = Yr.next()
                        cw = lambda k, fch=fch: ppc("ffn_conv_w", (l * 3 + k) * 44 + fch)
                        op("act", lambda e, U=U, Y=Y, fch=fch: e.activation(out=Y[:], in_=U[:, 2:2 + TH], func=AF.Identity,
                                                                           scale=cw(2), bias=ppc("ffn_conv_b", l * 44 + fch)),
                           reads=[BU, Bpp], writes=[BY])
                        op("dve", lambda e, U=U, Y=Y: e.scalar_tensor_tensor(out=Y[:], in0=U[:, 1:1 + TH], scalar=cw(1), in1=Y[:],
                                                                             op0=ALU.mult, op1=ALU.add),
                           reads=[BU, BY, Bpp], writes=[BY])
                        if gv == 0:
                            op("dve", lambda e, U=U, Y=Y: e.scalar_tensor_tensor(out=Y[:], in0=U[:, 0:TH], scalar=cw(0), in1=Y[:],
                                                                                 op0=ALU.mult, op1=ALU.add),
                               reads=[BU, BY, Bpp], writes=[BY])
                            Gt, BGt = Gr.next()
                            op("act", lambda e, Y=Y, Gt=Gt: e.activation(out=Gt[:], in_=Y[:], func=AF.Gelu_apprx_tanh), reads=[BY], writes=[BGt])
                            res.append((Gt, BGt))
                        else:
                            Yb, BYb = Ybr.next()
                            op("dve", lambda e, U=U, Y=Y, Yb=Yb: e.scalar_tensor_tensor(out=Yb[:], in0=U[:, 0:TH], scalar=cw(0), in1=Y[:],
                                                                                       op0=ALU.mult, op1=ALU.add),
                               reads=[BU, BY, Bpp], writes=[BYb])
                            res.append((Yb, BYb))
                    (Gt, BGt), (Yb, BYb) = res
                    op("dve", lambda e, Gt=Gt, Yb=Yb, j=j: e.tensor_tensor(out=aT[:, j, :], in0=Gt[:], in1=Yb[:], op=ALU.mult),
                       reads=[BGt, BYb], writes=[Ba[j]])
                for dc in range(NC8):
                    wdA, BwdA = load_wp("dA", wdown_d[l], 0, dc, 11, (0, 11))
                    wdB, BwdB = load_wp("dB", wdown_d[l], 0, dc, 11, (11, 22))
                    if dc % 2 == 0:
                        if dc + 2 < NC8:
                            prefetch_wp("dA", wdown_d[l], 0, dc + 2, 11, (0, 11))
                            prefetch_wp("dB", wdown_d[l], 0, dc + 2, 11, (11, 22))
                        elif hf + 1 < NH:
                            for gv in range(2):
                                prefetch_wp("u%d" % gv, wup_d[l], gv * FCH, 0, 8)
                    for ti, tt in enumerate(tts):
                        bank = next_bank()
                        matmul_group(bank, slice(0, 512),
                                     [((wdA[:, k, :] if k < 11 else wdB[:, k - 11, :]), aT[:, k, ti * 512:(ti + 1) * 512]) for k in range(FCH)],
                                     [BwdA, BwdB] + Ba)
                        resid_add(bank, dc, tt)
            S_.barrier()

    def ple(l, sq_i):
        with contextlib.ExitStack() as st:
            work = norm_work(st)
            pT = Ring([sb("pT%d" % i, [128, 2, 512], BF16, st) for i in range(2)])
            Sg = Ring([sb("Sg%d" % i, [128, 512], F32, st) for i in range(2)])
            hn, HN = hn_alloc(st, 2)
            bank_ring[0] = list(range(8))
            for tt in range(NTT):
                tl = tt % 2
                norm_tile(tt, "norm_ple", l, work, (hn, HN, tl))
                for q in range(4):
                    S_.dma("sp", pst[q][:], p_d[l, sq_i, tt * 512 + q * 128: tt * 512 + (q + 1) * 128, :], pstS[q], writes=[pstB[q]])
                pt, Bpt = pT.next()
                for ec in range(2):
                    bank = next_bank()
                    for q in range(4):
                        op("pe", lambda e, q=q, ec=ec, bank=bank: e.transpose(out=psum[bank][:, q * 128:(q + 1) * 128],
                                                                             in_=pst[q][:, ec * 128:(ec + 1) * 128], identity=ident),
                           reads=[pstB[q], Bcst], writes=[PB[bank]])
                    op("act", lambda e, pt=pt, ec=ec, bank=bank: e.activation(out=pt[:, ec, :], in_=psum[bank][:, :], func=AF.Identity),
                       reads=[PB[bank]], writes=[Bpt])
                for dc in range(NC8):
                    wg, Bwg = load_wp("pg", wpg_d[l], 0, dc, 8)
                    wp, Bwp = load_wp("pp", wpp_d[l], 0, dc, 2)
                    if dc % 2 == 0:
                        ndc = dc + 2 if dc + 2 < NC8 else (0 if tt + 1 < NTT else None)
                        if ndc is not None:
                            prefetch_wp("pg", wpg_d[l], 0, ndc, 8)
                            prefetch_wp("pp", wpp_d[l], 0, ndc, 2)
                    bg = next_bank()
                    matmul_group(bg, slice(0, 512), [(wg[:, k, :], hn[:, k, ts(tl)]) for k in range(8)],
                                 [Bwg] + [HN[k][tl] for k in range(8)])
                    bp = next_bank()
                    matmul_group(bp, slice(0, 512), [(wp[:, k, :], pt[:, k, :]) for k in range(2)], [Bwp, Bpt])
                    sg, Bsg = Sg.next()
                    op("act", lambda e, sg=sg, bg=bg: e.activation(out=sg[:], in_=psum[bg][:, :], func=AF.Sigmoid), reads=[PB[bg]], writes=[Bsg])
                    op("dve", lambda e, sg=sg, bp=bp: e.tensor_tensor(out=sg[:], in0=sg[:], in1=psum[bp][:, :], op=ALU.mult),
                       reads=[Bsg, PB[bp]], writes=[Bsg])
                    op("dve", lambda e, sg=sg, dc=dc: e.tensor_tensor(out=xT[:, dc, ts(tt)], in0=xT[:, dc, ts(tt)], in1=sg[:], op=ALU.add),
                       reads=[Bsg, X[dc][tt]], writes=[X[dc][tt]])
            S_.barrier()

    for sq_i in range(NSEQ):
        load_x(sq_i)
        for l in range(DEPTH):
            if l % 2 == 0:
                attention(l)
            else:
                rglru(l)
            ffn(l)
            ple(l, sq_i)
        final_store(sq_i)
    es.close()
    return nc


def prep_shared(inp, depth):
    pp = PP(depth)
    nr = max(depth // 2, 1)
    tab = np.zeros((128, pp.n), np.float32)

    def put(name, arr):
        tab[:, pp.off[name]:pp.off[name] + arr.shape[1]] = arr

    put("norm_mix", _cols(inp["norm_mix"], 8))
    put("norm_ffn", _cols(inp["norm_ffn"], 8))
    put("norm_ple", _cols(inp["norm_ple"], 8))
    put("norm_final", _cols(inp["norm_final"], 8))
    put("ffn_conv_w", _cols(inp["ffn_conv_w"], 44))
    put("ffn_conv_b", _cols(inp["ffn_conv_b"], 44))
    put("rnn_conv_w", _cols(inp["rnn_conv_w"], 10))
    put("rnn_conv_b", _cols(inp["rnn_conv_b"], 10))
    put("b_gate_a", _cols(inp["rnn_b_gate_a"], 10))
    put("b_gate_x", _cols(inp["rnn_b_gate_x"], 10))
    put("lru", _cols(inp["rnn_lru_param"], 10))
    shared = {
        "pp": tab,
        "cst": make_consts(),
        "wband_a": _band(np.asarray(inp["rnn_w_gate_a"], np.float32)),
        "wband_x": _band(np.asarray(inp["rnn_w_gate_x"], np.float32)),
    }
    for k in ("attn_w_qkv", "attn_w_o", "rnn_w_in", "rnn_w_out", "ffn_w_up", "ffn_w_down", "ple_w_gate", "ple_w_proj"):
        shared[k] = np.ascontiguousarray(np.asarray(inp[k], np.float32))
    return shared


def run(inp, n_cores, depth):
    x = np.asarray(inp["x"], np.float32)
    p = np.asarray(inp["p"], np.float32)
    B, S, _ = x.shape
    assert B % n_cores == 0
    nseq = B // n_cores
    shared = prep_shared(inp, depth)
    nc = build(nseq, S, depth)
    in_maps = []
    for i in range(n_cores):
        m = dict(shared)
        m["x"] = np.ascontiguousarray(x[i * nseq:(i + 1) * nseq])
        m["p"] = np.ascontiguousarray(p[:, i * nseq:(i + 1) * nseq])
        in_maps.append(m)
    res = run_bass_kernel_spmd(nc, in_maps, core_ids=list(range(n_cores)))
    return np.concatenate([np.asarray(r["out"], np.float32) for r in res.results], axis=0)


def kernel(**inputs):
    return run(inputs, N_CORES, 4)
```

```python
import contextlib
import numpy as np
import concourse.bass as bass
import concourse.mybir as mybir
from concourse.bass_utils import run_bass_kernel_spmd

F32 = mybir.dt.float32
BF16 = mybir.dt.bfloat16
AF = mybir.ActivationFunctionType
ALU = mybir.AluOpType

D = 1024
NC8 = 8
HEADS = 16
DH = 64
RW = 1280
RCH = 10
RB = 80
FF = 2816
FCH = 22
PLE = 256
EPS = 1e-6
LRU_C = 8.0
N_CORES = 8


class SemObj:
    def __init__(self, sem, name):
        self.sem = sem
        self.name = name
        self.count = 0


class Buf:
    __slots__ = ("name", "w", "r")

    def __init__(self, name=""):
        self.name = name
        self.w = None
        self.r = {}


class Sched:
    def __init__(self, nc):
        self.nc = nc
        self.eng = {"pe": nc.tensor, "act": nc.scalar, "dve": nc.vector, "pool": nc.gpsimd, "sp": nc.sync}
        self.esem = {k: SemObj(nc.alloc_semaphore("es_" + k), k) for k in self.eng}
        self.waited = {k: {} for k in self.eng}
        self.ndsem = 0
        self.ninst = 0

    def dsem(self):
        self.ndsem += 1
        return SemObj(self.nc.alloc_semaphore("ds%d" % self.ndsem), "ds%d" % self.ndsem)

    def _deps(self, en, reads, writes):
        deps = {}

        def add(tok):
            if tok is None:
                return
            so, v = tok
            if en == "pe" and so is self.esem["pe"]:
                return
            if deps.get(so, 0) < v:
                deps[so] = v

        for b in reads:
            add(b.w)
        for b in writes:
            add(b.w)
            for so, v in b.r.items():
                add((so, v))
        w = self.waited[en]
        e = self.eng[en]
        for so, v in deps.items():
            if w.get(so, 0) >= v:
                continue
            assert so.count >= v, "waiting on un-issued increment %s %d>%d" % (so.name, v, so.count)
            e.wait_ge(so.sem, v)
            self.ninst += 1
            w[so] = v

    def _mark(self, tok, reads, writes):
        so, v = tok
        for b in reads:
            if b.r.get(so, 0) < v:
                b.r[so] = v
        for b in writes:
            b.w = tok
            b.r = {}

    def op(self, en, fn, reads=(), writes=(), inc=True):
        self._deps(en, reads, writes)
        ins = fn(self.eng[en])
        self.ninst += 1
        so = self.esem[en]
        if inc:
            so.count += 1
            ins.then_inc(so.sem, 1)
            tok = (so, so.count)
        else:
            tok = (so, so.count + 1)
        self._mark(tok, reads, writes)
        return ins

    def dma(self, q, out, in_, ds, reads=(), writes=(), **kw):
        self._deps(q, reads, writes)
        ins = self.eng[q].dma_start(out=out, in_=in_, **kw)
        self.ninst += 1
        ds.count += 16
        ins.then_inc(ds.sem, 16)
        self._mark((ds, ds.count), reads, writes)
        return ins

    def wait_all(self, en, bufs):
        self._deps(en, bufs, bufs)

    def sync_from(self, en, others=("pe", "act", "dve")):
        w = self.waited[en]
        for other in others:
            so = self.esem[other]
            if so.count > w.get(so, 0):
                self.eng[en].wait_ge(so.sem, so.count)
                self.ninst += 1
                w[so] = so.count

    def barrier(self, engines=("pe", "act", "dve", "pool")):
        for en in engines:
            w = self.waited[en]
            for other in engines:
                if other == en:
                    continue
                so = self.esem[other]
                if so.count > w.get(so, 0):
                    self.eng[en].wait_ge(so.sem, so.count)
                    self.ninst += 1
                    w[so] = so.count


class Ring:
    def __init__(self, tiles):
        self.tiles = tiles
        self.bufs = [Buf() for _ in tiles]
        self.i = 0

    def next(self):
        k = self.i % len(self.tiles)
        self.i += 1
        return self.tiles[k], self.bufs[k]


def _cols(v, nch):
    lead = int(np.prod(v.shape[:-1])) if v.ndim > 1 else 1
    a = np.asarray(v, np.float32).reshape(lead, nch, 128)
    return np.ascontiguousarray(a.transpose(2, 0, 1).reshape(128, lead * nch))


def band_range(fc):
    b_lo = (128 * fc) // RB
    b_hi = (128 * fc + 127) // RB
    k_lo = (RB * b_lo) // 128
    k_hi = (RB * b_hi + RB - 1) // 128
    return k_lo, min(k_hi, RCH - 1)


def _band(wg):
    L = wg.shape[0]
    dense = np.zeros((L, RW, RW), np.float32)
    for h in range(RW // RB):
        dense[:, h * RB:(h + 1) * RB, h * RB:(h + 1) * RB] = wg[:, h]
    out = np.zeros((L, 128, RCH, 3, 128), np.float32)
    for fc in range(RCH):
        k_lo, k_hi = band_range(fc)
        for kk, kc in enumerate(range(k_lo, k_hi + 1)):
            out[:, :, fc, kk, :] = dense[:, kc * 128:(kc + 1) * 128, fc * 128:(fc + 1) * 128]
    return out


class PP:
    def __init__(self, depth):
        self.off = {}
        self.n = 0
        nr = max(depth // 2, 1)
        for name, cnt in [("norm_mix", depth * 8), ("norm_ffn", depth * 8), ("norm_ple", depth * 8), ("norm_final", 8),
                          ("ffn_conv_w", depth * 3 * 44), ("ffn_conv_b", depth * 44),
                          ("rnn_conv_w", nr * 4 * 10), ("rnn_conv_b", nr * 10), ("b_gate_a", nr * 10),
                          ("b_gate_x", nr * 10), ("lru", nr * 10), ("lruc", nr * 10)]:
            self.off[name] = self.n
            self.n += cnt


def make_consts():
    j = np.arange(128)[:, None]
    s = np.arange(128)[None, :]
    ident = (j == s).astype(np.float32)
    negtri = -(j >= s).astype(np.float32)
    negones = -np.ones((128, 128), np.float32)
    masklt = (j < s).astype(np.float32)
    ones = np.ones((128, 128), np.float32)
    return np.ascontiguousarray(np.concatenate([ident, negtri, negones, masklt, ones], axis=1))


def build(NSEQ, S, DEPTH):
    assert S % 512 == 0
    NTT = S // 512
    NQB = S // 128
    TH = min(S, 1024)
    NH = S // TH
    NTH = TH // 512
    NATT = (DEPTH + 1) // 2
    NRNN = max(DEPTH // 2, 1)
    pp = PP(DEPTH)

    nc = bass.Bass("TRN2", target_bir_lowering=False)

    def dram(name, shape, kind="ExternalInput", dt=F32):
        return nc.dram_tensor(name, list(shape), dt, kind=kind).ap()

    x_d = dram("x", [NSEQ, S, D])
    p_d = dram("p", [DEPTH, NSEQ, S, PLE])
    out_d = dram("out", [NSEQ, S, D], kind="ExternalOutput")
    wqkv_d = dram("attn_w_qkv", [NATT, D, 3 * D])
    wo_d = dram("attn_w_o", [NATT, D, D])
    win_d = dram("rnn_w_in", [NRNN, D, 2 * RW])
    wout_d = dram("rnn_w_out", [NRNN, RW, D])
    wup_d = dram("ffn_w_up", [DEPTH, D, 2 * FF])
    wdown_d = dram("ffn_w_down", [DEPTH, FF, D])
    wpg_d = dram("ple_w_gate", [DEPTH, D, D])
    wpp_d = dram("ple_w_proj", [DEPTH, PLE, D])
    wba_d = dram("wband_a", [NRNN, 128, RCH, 3, 128])
    wbx_d = dram("wband_x", [NRNN, 128, RCH, 3, 128])
    pp_d = dram("pp", [128, pp.n])
    cst_d = dram("cst", [128, 5 * 128])

    S_ = Sched(nc)
    op = S_.op
    es = contextlib.ExitStack()

    uid = [0]

    def sb(name, shape, dt, stack=None):
        uid[0] += 1
        return (stack or es).enter_context(nc.sbuf_tensor("%s_u%d" % (name, uid[0]), list(shape), dt))

    xT = sb("xT", [128, NC8, S], F32)
    ppt = sb("ppt", [128, pp.n], F32)
    cst32 = sb("cst32", [128, 5 * 128], F32)
    cstb = sb("cstb", [128, 5 * 128], BF16)
    ident = cst32[:, 0:128]
    negtri = cstb[:, 128:256]
    negones = cstb[:, 256:384]
    masklt = cstb[:, 384:512]
    onesb = cstb[:, 512:640]
    NW = 4
    WSZ = FCH * 128
    wslots = [sb("w%d" % i, [128, WSZ], BF16) for i in range(NW)]
    wbufs = [Buf("w%d" % i) for i in range(NW)]
    wsems = [S_.dsem() for _ in range(NW)]
    wcnt = [0]
    stgS = [S_.dsem() for _ in range(4)]
    pstS = [S_.dsem() for _ in range(4)]
    carry_f = sb("carry_f", [128, 2 * FCH, 2], F32)
    carry_r = sb("carry_r", [128, RCH, 3], F32)
    carry_h = sb("carry_h", [128, RCH], F32)
    Bcf = [Buf() for _ in range(2 * FCH)]
    Bcr = [Buf() for _ in range(RCH)]
    Bch = [Buf() for _ in range(RCH)]

    psum = [es.enter_context(nc.psum_tensor("ps%d" % i, [128, 512], F32)) for i in range(8)]
    PB = [Buf("ps%d" % i) for i in range(8)]
    bank_ring = [list(range(8))]
    bank_i = [0]

    def next_bank():
        r = bank_ring[0]
        b = r[bank_i[0] % len(r)]
        bank_i[0] += 1
        return b

    X = [[Buf() for _ in range(NTT)] for _ in range(NC8)]
    Bpp = Buf("pp")
    Bcst = Buf("cst")

    def ts(tt):
        return slice(tt * 512, (tt + 1) * 512)

    def ppc(name, idx):
        o = pp.off[name] + idx
        return ppt[:, o:o + 1]

    d0 = S_.dsem()
    S_.dma("sp", ppt[:], pp_d[:, :], d0, writes=[Bpp])
    d1 = S_.dsem()
    S_.dma("sp", cst32[:], cst_d[:, :], d1, writes=[Bcst])
    op("dve", lambda e: e.tensor_copy(out=cstb[:], in_=cst32[:]), reads=[Bcst], writes=[Bcst])
    nl = NRNN * RCH
    lo, lc = pp.off["lru"], pp.off["lruc"]
    op("act", lambda e: e.activation(out=ppt[:, lc:lc + nl], in_=ppt[:, lo:lo + nl], func=AF.Exp, scale=-1.0),
       reads=[Bpp], writes=[Bpp])
    op("act", lambda e: e.activation(out=ppt[:, lc:lc + nl], in_=ppt[:, lc:lc + nl], func=AF.Ln, bias=1.0),
       reads=[Bpp], writes=[Bpp])
    op("dve", lambda e: e.tensor_scalar(out=ppt[:, lc:lc + nl], in0=ppt[:, lc:lc + nl], scalar1=-LRU_C, scalar2=None,
                                        op0=ALU.mult), reads=[Bpp], writes=[Bpp])

    def load_w(view, K, n=128):
        i = wcnt[0] % NW
        wcnt[0] += 1
        t = wslots[i][:, 0:K * n].rearrange("p (k n) -> p k n", k=K)
        S_.dma("pool", t, view, wsems[i], writes=[wbufs[i]])
        return t, wbufs[i]

    pair_cache = {}

    def load_wp(name, w_ap, col0_chunk, idx, K, krange=None):
        if idx % 2 == 0:
            key = (name, col0_chunk + idx)
            if key in stash:
                pair_cache[name] = stash.pop(key)
            else:
                pair_cache[name] = slab_load(w_ap, col0_chunk + idx, K, krange)
        t, B = pair_cache[name]
        return t[:, :, (idx % 2) * 128:(idx % 2 + 1) * 128], B

    stash = {}

    def slab_load(w_ap, chunk, K, krange):
        f0 = chunk * 128
        v = w_ap.rearrange("(k p) f -> p k f", p=128)
        v = v[:, krange[0]:krange[1], f0:f0 + 256] if krange else v[:, :, f0:f0 + 256]
        return load_w(v, K, 256)

    def prefetch_wp(name, w_ap, col0_chunk, idx, K, krange=None):
        key = (name, col0_chunk + idx)
        if key not in stash:
            stash[key] = slab_load(w_ap, col0_chunk + idx, K, krange)

    def wview(w_ap, f0, n=128):
        return w_ap.rearrange("(k p) f -> p k f", p=128)[:, :, f0:f0 + n]

    def matmul_group(bank, cols, pairs, reads, inc_last=True):
        n = len(pairs)
        for k, (l, r) in enumerate(pairs):
            op("pe", lambda e, l=l, r=r, k=k: e.matmul(psum[bank][:, cols], lhsT=l, rhs=r, start=(k == 0), stop=(k == n - 1)),
               reads=reads, writes=[PB[bank]], inc=(inc_last and k == n - 1))

    def norm_tile(tt, gname, gidx, work, dst=None, out_f32=None):
        sq_ring, rs_ring = work
        if dst is not None:
            hn, HN, tl = dst
        bank = next_bank()
        for c in range(NC8):
            sq, Bsq = sq_ring.next()
            op("act", lambda e, c=c, sq=sq: e.activation(out=sq[:], in_=xT[:, c, ts(tt)], func=AF.Square),
               reads=[X[c][tt]], writes=[Bsq])
            op("pe", lambda e, c=c, sq=sq: e.matmul(psum[bank][:, :], lhsT=onesb, rhs=sq[:], start=(c == 0), stop=(c == NC8 - 1)),
               reads=[Bsq, Bcst], writes=[PB[bank]])
        rs, Brs = rs_ring.next()
        op("act", lambda e: e.activation(out=rs[:], in_=psum[bank][:, :], func=AF.Sqrt, scale=1.0 / D, bias=EPS),
           reads=[PB[bank]], writes=[Brs])
        op("dve", lambda e: e.reciprocal(out=rs[:], in_=rs[:]), reads=[Brs], writes=[Brs])
        for c in range(NC8):
            o = hn[:, c, ts(tl)] if out_f32 is None else out_f32[0][:, c, :]
            wb = [HN[c][tl]] if out_f32 is None else [out_f32[1]]
            op("dve", lambda e, c=c, o=o: e.scalar_tensor_tensor(out=o, in0=xT[:, c, ts(tt)], scalar=ppc(gname, gidx * 8 + c),
                                                                 in1=rs[:], op0=ALU.mult, op1=ALU.mult),
               reads=[X[c][tt], Brs, Bpp], writes=wb)

    def resid_add(bank, dc, tt):
        op("dve", lambda e: e.tensor_tensor(out=xT[:, dc, ts(tt)], in0=xT[:, dc, ts(tt)], in1=psum[bank][:, :], op=ALU.add),
           reads=[PB[bank], X[dc][tt]], writes=[X[dc][tt]])

    def hn_alloc(stack, ntl):
        t = sb("hn", [128, NC8, ntl * 512], BF16, stack)
        return t, [[Buf() for _ in range(ntl)] for _ in range(NC8)]

    def norm_work(stack):
        sq_ring = Ring([sb("sq%d" % i, [128, 512], BF16, stack) for i in range(2)])
        rs_ring = Ring([sb("rs%d" % i, [128, 512], F32, stack) for i in range(2)])
        return sq_ring, rs_ring

    def load_x(sq_i):
        bank_ring[0] = list(range(8))
        st = contextlib.ExitStack()
        stg = [sb("stg%d" % i, [128, D], F32, st) for i in range(4)]
        stgB = [Buf() for _ in range(4)]
        S_.sync_from("sp")
        for tt in range(NTT):
            for q in range(4):
                S_.dma("sp", stg[q][:], x_d[sq_i, tt * 512 + q * 128: tt * 512 + (q + 1) * 128, :], stgS[q], writes=[stgB[q]])
            for c in range(NC8):
                bank = next_bank()
                for q in range(4):
                    op("pe", lambda e, q=q, c=c: e.transpose(out=psum[bank][:, q * 128:(q + 1) * 128],
                                                             in_=stg[q][:, c * 128:(c + 1) * 128], identity=ident),
                       reads=[stgB[q], Bcst], writes=[PB[bank]])
                op("act", lambda e, c=c: e.activation(out=xT[:, c, ts(tt)], in_=psum[bank][:, :], func=AF.Identity),
                   reads=[PB[bank]], writes=[X[c][tt]])
        S_.barrier()
        st.close()

    def final_store(sq_i):
        with contextlib.ExitStack() as st:
            work = norm_work(st)
            yf = sb("yf", [128, NC8, 512], F32, st)
            Byf = Buf()
            stg = [sb("stg%d" % i, [128, D], F32, st) for i in range(4)]
            stgB = [Buf() for _ in range(4)]
            bank_ring[0] = list(range(8))
            for tt in range(NTT):
                norm_tile(tt, "norm_final", 0, work, out_f32=(yf, Byf))
                for q in range(4):
                    b0, b1 = next_bank(), next_bank()
                    for c in range(NC8):
                        bank = b0 if c < 4 else b1
                        op("pe", lambda e, c=c, bank=bank: e.transpose(out=psum[bank][:, (c % 4) * 128:(c % 4 + 1) * 128],
                                                                      in_=yf[:, c, q * 128:(q + 1) * 128], identity=ident),
                           reads=[Byf, Bcst], writes=[PB[bank]])
                    op("act", lambda e: e.activation(out=stg[q][:, 0:512], in_=psum[b0][:, :], func=AF.Identity),
                       reads=[PB[b0]], writes=[stgB[q]])
                    op("dve", lambda e: e.tensor_copy(out=stg[q][:, 512:1024], in_=psum[b1][:, :]),
                       reads=[PB[b1]], writes=[stgB[q]])
                    S_.dma("sp", out_d[sq_i, tt * 512 + q * 128: tt * 512 + (q + 1) * 128, :], stg[q][:], stgS[q], reads=[stgB[q]])
            for en in ("sp", "pe", "act", "dve"):
                S_.wait_all(en, stgB)
            S_.barrier()

    def attention(l):
        slot = l // 2
        with contextlib.ExitStack() as st:
            OT = sb("OT", [128, NC8, S], BF16, st)
            BO = [[Buf() for _ in range(NTT)] for _ in range(NC8)]
            QTs = [sb("QT0", [128, S], BF16, st)]
            KTs = [sb("KT0", [128, S], BF16, st)]
            Vbs = [sb("Vb0", [128, NQB, 128], BF16, st)]
            BQs, BKs, BVs = [Buf(), Buf()], [Buf(), Buf()], [Buf(), Buf()]
            NSTR = 4
            Er = [Ring([sb("E%d_%d" % (s, i), [128, 512], BF16, st) for i in range(1)]) for s in range(NSTR)]
            Lr = [Ring([sb("L%d_%d" % (s, i), [128, 512], BF16, st) for i in range(2)]) for s in range(NSTR)]
            Wr = [Ring([sb("W%d_%d" % (s, i), [128, 512], BF16, st) for i in range(2)]) for s in range(NSTR)]
            Lsuf = [sb("Ls%d" % s, [128, 512], BF16, st) for s in range(NSTR)]
            BLs = [Buf() for _ in range(NSTR)]

            hn, HN = hn_alloc(st, NTT)
            bank_ring[0] = list(range(8))
            st2 = contextlib.ExitStack()
            work = norm_work(st2)
            for tt in range(NTT):
                norm_tile(tt, "norm_mix", l, work, (hn, HN, tt))
            S_.barrier()
            st2.close()
            QTs.append(sb("QT1", [128, S], BF16, st))
            KTs.append(sb("KT1", [128, S], BF16, st))
            Vbs.append(sb("Vb1", [128, NQB, 128], BF16, st))
            bank_ring[0] = [6, 7]
            if NTT == 4:
                pairs = [[0, 3], [1, 2]]
            elif NTT == 2:
                pairs = [[0], [1]]
            else:
                pairs = [list(range(NTT))]
            def proj_units(c):
                cb = c % 2
                QT, KT, Vb, BQ, BK, BV = QTs[cb], KTs[cb], Vbs[cb], BQs[cb], BKs[cb], BVs[cb]
                wq, Bwq = load_wp("q", wqkv_d[slot], 0, c, 8)
                wk, Bwk = load_wp("k", wqkv_d[slot], 8, c, 8)
                wv, Bwv = load_wp("v", wqkv_d[slot], 16, c, 8)
                for tt in range(NTT):
                    bank = next_bank()
                    matmul_group(bank, slice(0, 512), [(wq[:, k, :], hn[:, k, ts(tt)]) for k in range(8)],
                                 [Bwq] + [HN[k][tt] for k in range(8)])
                    op("dve", lambda e: e.tensor_scalar(out=QT[:, ts(tt)], in0=psum[bank][:, :], scalar1=DH ** -0.5, scalar2=None,
                                                        op0=ALU.mult), reads=[PB[bank]], writes=[BQ])
                    yield
                    bank = next_bank()
                    matmul_group(bank, slice(0, 512), [(wk[:, k, :], hn[:, k, ts(tt)]) for k in range(8)],
                                 [Bwk] + [HN[k][tt] for k in range(8)])
                    op("dve", lambda e: e.tensor_copy(out=KT[:, ts(tt)], in_=psum[bank][:, :]), reads=[PB[bank]], writes=[BK])
                    yield
                for tt in range(NTT):
                    bank = next_bank()
                    for q in range(4):
                        tb = tt * 4 + q
                        matmul_group(bank, slice(q * 128, (q + 1) * 128),
                                     [(hn[:, k, tb * 128:(tb + 1) * 128], wv[:, k, :]) for k in range(8)],
                                     [Bwv] + [HN[k][tt] for k in range(8)])
                    op("dve", lambda e: e.tensor_copy(out=Vb[:, tt * 4:(tt + 1) * 4, :],
                                                      in_=psum[bank][:, :].rearrange("p (q n) -> p q n", q=4)),
                       reads=[PB[bank]], writes=[BV])
                    yield

            for _ in proj_units(0):
                pass
            for c in range(NC8):
                cb = c % 2
                QT, KT, Vb, BQ, BK, BV = QTs[cb], KTs[cb], Vbs[cb], BQs[cb], BKs[cb], BVs[cb]
                if c + 1 < NC8 and (c + 1) % 2 == 0:
                    prefetch_wp("q", wqkv_d[slot], 0, c + 1, 8)
                    prefetch_wp("k", wqkv_d[slot], 8, c + 1, 8)
                    prefetch_wp("v", wqkv_d[slot], 16, c + 1, 8)
                nxt = proj_units(c + 1) if c + 1 < NC8 else iter(())
                steps = []
                for ps_i, tcs in enumerate(pairs):
                    lst = []
                    for tc in tcs:
                        kbs = list(range(4 * tc + 3, -1, -1))
                        for idx, kb in enumerate(kbs):
                            lst.append((tc, kb, idx, idx == len(kbs) - 1))
                    steps.append(lst)
                nround = max(len(s) for s in steps)
                acts, infos = [], []
                for r in range(nround):
                    act = [(ps_i, hh) for ps_i in range(len(pairs)) if r < len(steps[ps_i]) for hh in range(2)]
                    info = {}
                    for (ps_i, hh) in act:
                        tc, kb, idx, last = steps[ps_i][r]
                        sid = ps_i * 2 + hh
                        off = max(0, kb - 4 * tc)
                        c0 = off * 128
                        diag = kb >= 4 * tc
                        info[(ps_i, hh)] = (tc, kb, idx, last, sid, c0, diag)
                    acts.append(act)
                    infos.append(info)

                def emit_z(r):
                    for key in acts[r]:
                        tc, kb, idx, last, sid, c0, diag = infos[r][key]
                        hh = key[1]
                        hp = slice(64 * hh, 64 * hh + 64)
                        if idx == 0:
                            op("dve", lambda e, sid=sid: e.memset(Lsuf[sid][:], 0.0), writes=[BLs[sid]])
                        op("pe", lambda e, sid=sid, hp=hp, kb=kb, tc=tc, c0=c0: e.matmul(
                            psum[sid][:, c0:512], lhsT=KT[hp, kb * 128:(kb + 1) * 128], rhs=QT[hp, tc * 512 + c0:(tc + 1) * 512],
                            start=True, stop=False, skip_group_check=True), reads=[BK, BQ], writes=[PB[sid]])

                emit_z(0)
                for r in range(nround):
                    act, info = acts[r], infos[r]
                    tiles = {}
                    for key in act:
                        tc, kb, idx, last, sid, c0, diag = info[key]
                        Et, BE = Er[sid].next()
                        op("act", lambda e, Et=Et, sid=sid, c0=c0: e.activation(out=Et[:, c0:512], in_=psum[sid][:, c0:512], func=AF.Exp),
                           reads=[PB[sid]], writes=[BE])
                        tiles[key] = [Et, BE]
                    for key in act:
                        tc, kb, idx, last, sid, c0, diag = info[key]
                        Et, BE = tiles[key]
                        Lt, BL = Lr[sid].next()
                        op("act", lambda e, Et=Et, Lt=Lt, c0=c0: e.activation(out=Lt[:, c0:512], in_=Et[:, c0:512], func=AF.Ln, bias=1.0),
                           reads=[BE], writes=[BL])
                        if diag:
                            op("dve", lambda e, Lt=Lt, c0=c0: e.tensor_tensor(out=Lt[:, c0:c0 + 128], in0=Lt[:, c0:c0 + 128], in1=masklt, op=ALU.mult),
                               reads=[BL, Bcst], writes=[BL])
                        tiles[key] += [Lt, BL]
                    for key in act:
                        tc, kb, idx, last, sid, c0, diag = info[key]
                        Et, BE, Lt, BL = tiles[key]
                        op("pe", lambda e, sid=sid, Lt=Lt, c0=c0, idx=idx: e.matmul(
                            psum[sid][:, c0:512], lhsT=negtri, rhs=Lt[:, c0:512], start=False, stop=(idx == 0), skip_group_check=True),
                           reads=[BL, Bcst], writes=[PB[sid]])
                        if idx > 0:
                            op("pe", lambda e, sid=sid, c0=c0: e.matmul(
                                psum[sid][:, c0:512], lhsT=negones, rhs=Lsuf[sid][:, c0:512], start=False, stop=True, skip_group_check=True),
                               reads=[BLs[sid], Bcst], writes=[PB[sid]])
                    for key in act:
                        tc, kb, idx, last, sid, c0, diag = info[key]
                        Wt, BW = Wr[sid].next()
                        op("act", lambda e, Wt=Wt, sid=sid, c0=c0: e.activation(out=Wt[:, c0:512], in_=psum[sid][:, c0:512], func=AF.Exp),
                           reads=[PB[sid]], writes=[BW])
                        if diag:
                            op("dve", lambda e, Wt=Wt, c0=c0: e.tensor_tensor(out=Wt[:, c0:c0 + 128], in0=Wt[:, c0:c0 + 128], in1=masklt, op=ALU.mult),
                               reads=[BW, Bcst], writes=[BW])
                        tiles[key] += [Wt, BW]
                    if r + 1 < nround:
                        emit_z(r + 1)
                    for key in act:
                        tc, kb, idx, last, sid, c0, diag = info[key]
                        ps_i, hh = key
                        Et, BE, Lt, BL, Wt, BW = tiles[key]
                        pob = 4 + ps_i
                        op("pe", lambda e, pob=pob, hh=hh, kb=kb, Wt=Wt, c0=c0, idx=idx, last=last: e.matmul(
                            psum[pob][64 * hh:64 * hh + 64, c0:512], lhsT=Vb[:, kb, 64 * hh:64 * hh + 64], rhs=Wt[:, c0:512],
                            start=(idx == 0), stop=last, skip_group_check=True), reads=[BW, BV], writes=[PB[pob]])
                        if not last:
                            op("dve", lambda e, sid=sid, Lt=Lt, c0=c0: e.tensor_tensor(out=Lsuf[sid][:, c0:512], in0=Lsuf[sid][:, c0:512],
                                                                                      in1=Lt[:, c0:512], op=ALU.add),
                               reads=[BL, BLs[sid]], writes=[BLs[sid]])
                    if r >= 1:
                        next(nxt, None)
                    for ps_i in range(len(pairs)):
                        if r < len(steps[ps_i]) and steps[ps_i][r][3]:
                            tc = steps[ps_i][r][0]
                            pob = 4 + ps_i
                            op("dve", lambda e, pob=pob, tc=tc: e.tensor_copy(out=OT[:, c, ts(tc)], in_=psum[pob][:, :]),
                               reads=[PB[pob]], writes=[BO[c][tc]])
                for _ in nxt:
                    pass
            bank_ring[0] = list(range(8))
            for dc in range(NC8):
                wo, Bwo = load_wp("o", wo_d[slot], 0, dc, 8)
                if dc % 2 == 0 and dc + 2 < NC8:
                    prefetch_wp("o", wo_d[slot], 0, dc + 2, 8)
                for tt in range(NTT):
                    bank = next_bank()
                    matmul_group(bank, slice(0, 512), [(wo[:, k, :], OT[:, k, ts(tt)]) for k in range(8)],
                                 [Bwo] + [BO[k][tt] for k in range(8)])
                    resid_add(bank, dc, tt)
            S_.barrier()

    def rglru(l):
        slot = l // 2
        with contextlib.ExitStack() as st:
            work = norm_work(st)
            xr = sb("xr", [128, RCH, TH], BF16, st)
            G = sb("G", [128, RCH, TH], BF16, st)
            Bxr = [Buf() for _ in range(RCH)]
            BG = [Buf() for _ in range(RCH)]
            Ur = Ring([sb("Ur%d" % i, [128, TH + 3], F32, st) for i in range(2)])
            Yr = Ring([sb("Yr%d" % i, [128, TH], F32, st) for i in range(2)])
            T1r = Ring([sb("T1_%d" % i, [128, TH], F32, st) for i in range(2)])
            T2r = Ring([sb("T2_%d" % i, [128, TH], F32, st) for i in range(2)])
            T3r = Ring([sb("T3_%d" % i, [128, TH], F32, st) for i in range(2)])
            hn, HN = hn_alloc(st, NTH)
            bank_ring[0] = list(range(8))
            for hf in range(NH):
                h0 = hf * TH
                tts = list(range(hf * NTH, (hf + 1) * NTH))
                for ti, tt in enumerate(tts):
                    norm_tile(tt, "norm_mix", l, work, (hn, HN, ti))
                for j in range(RCH):
                    wr, Bwr = load_wp("r", win_d[slot], RCH, j, 8)
                    wg, Bwg = load_wp("g", win_d[slot], 0, j, 8)
                    if j % 2 == 0 and j + 2 < RCH:
                        prefetch_wp("r", win_d[slot], RCH, j + 2, 8)
                        prefetch_wp("g", win_d[slot], 0, j + 2, 8)
                    U, BU = Ur.next()
                    if hf == 0:
                        op("pool", lambda e, U=U: e.memset(U[:, 0:3], 0.0), writes=[BU])
                    else:
                        op("pool", lambda e, U=U, j=j: e.tensor_copy(out=U[:, 0:3], in_=carry_r[:, j, :]), reads=[Bcr[j]], writes=[BU])
                    for ti, tt in enumerate(tts):
                        bank = next_bank()
                        matmul_group(bank, slice(0, 512), [(wr[:, k, :], hn[:, k, ts(ti)]) for k in range(8)],
                                     [Bwr] + [HN[k][ti] for k in range(8)])
                        op("act", lambda e, U=U, ti=ti: e.activation(out=U[:, 3 + ti * 512:3 + (ti + 1) * 512], in_=psum[bank][:, :], func=AF.Identity),
                           reads=[PB[bank]], writes=[BU])
                    if hf + 1 < NH:
                        op("pool", lambda e, U=U, j=j: e.tensor_copy(out=carry_r[:, j, :], in_=U[:, TH:TH + 3]), reads=[BU], writes=[Bcr[j]])
                    Y, BY = Yr.next()
                    cw = lambda k, j=j: ppc("rnn_conv_w", (slot * 4 + k) * RCH + j)
                    op("act", lambda e, U=U, Y=Y, j=j: e.activation(out=Y[:], in_=U[:, 3:3 + TH], func=AF.Identity,
                                                                   scale=cw(3), bias=ppc("rnn_conv_b", slot * RCH + j)),
                       reads=[BU, Bpp], writes=[BY])
                    for k in (2, 1):
                        op("dve", lambda e, U=U, Y=Y, k=k: e.scalar_tensor_tensor(out=Y[:], in0=U[:, k:k + TH], scalar=cw(k), in1=Y[:],
                                                                                  op0=ALU.mult, op1=ALU.add),
                           reads=[BU, BY, Bpp], writes=[BY])
                    op("dve", lambda e, U=U, Y=Y, j=j: e.scalar_tensor_tensor(out=xr[:, j, :], in0=U[:, 0:TH], scalar=cw(0), in1=Y[:],
                                                                              op0=ALU.mult, op1=ALU.add),
                       reads=[BU, BY, Bpp], writes=[Bxr[j]])
                    for ti, tt in enumerate(tts):
                        bank = next_bank()
                        matmul_group(bank, slice(0, 512), [(wg[:, k, :], hn[:, k, ts(ti)]) for k in range(8)],
                                     [Bwg] + [HN[k][ti] for k in range(8)])
                        op("act", lambda e, j=j, ti=ti: e.activation(out=G[:, j, ti * 512:(ti + 1) * 512], in_=psum[bank][:, :], func=AF.Gelu_apprx_tanh),
                           reads=[PB[bank]], writes=[BG[j]])
                for fc in range(RCH):
                    k_lo, k_hi = band_range(fc)
                    nk = k_hi - k_lo + 1
                    wa, Bwa = load_w(wba_d[slot, :, fc, 0:nk, :], nk)
                    wx, Bwx = load_w(wbx_d[slot, :, fc, 0:nk, :], nk)
                    T1, B1 = T1r.next()
                    T2, B2 = T2r.next()
                    T3, B3 = T3r.next()
                    for (wt_, Bw_, T_, B_, bn) in ((wa, Bwa, T1, B1, "b_gate_a"), (wx, Bwx, T2, B2, "b_gate_x")):
                        for ti in range(NTH):
                            bank = next_bank()
                            matmul_group(bank, slice(0, 512),
                                         [(wt_[:, kk, :], xr[:, k_lo + kk, ti * 512:(ti + 1) * 512]) for kk in range(nk)],
                                         [Bw_] + [Bxr[k_lo + kk] for kk in range(nk)])
                            op("act", lambda e, T_=T_, ti=ti, bn=bn, bank=bank: e.activation(
                                out=T_[:, ti * 512:(ti + 1) * 512], in_=psum[bank][:, :], func=AF.Sigmoid, bias=ppc(bn, slot * RCH + fc)),
                               reads=[PB[bank], Bpp], writes=[B_])
                    op("act", lambda e, T1=T1: e.activation(out=T1[:], in_=T1[:], func=AF.Exp, scale=ppc("lruc", slot * RCH + fc)),
                       reads=[B1, Bpp], writes=[B1])
                    op("act", lambda e, T1=T1, T3=T3: e.activation(out=T3[:], in_=T1[:], func=AF.Square), reads=[B1], writes=[B3])
                    op("act", lambda e, T3=T3: e.activation(out=T3[:], in_=T3[:], func=AF.Sqrt, scale=-1.0, bias=1.0), reads=[B3], writes=[B3])
                    op("dve", lambda e, T2=T2: e.tensor_tensor(out=T2[:], in0=T2[:], in1=xr[:, fc, :], op=ALU.mult), reads=[B2, Bxr[fc]], writes=[B2])
                    op("dve", lambda e, T2=T2, T3=T3: e.tensor_tensor(out=T2[:], in0=T2[:], in1=T3[:], op=ALU.mult), reads=[B2, B3], writes=[B2])
                    init = 0.0 if hf == 0 else carry_h[:, fc:fc + 1]
                    op("dve", lambda e, T1=T1, T2=T2, T3=T3, init=init: e.tensor_tensor_scan(out=T3[:], data0=T1[:], data1=T2[:], initial=init,
                                                                                            op0=ALU.mult, op1=ALU.add),
                       reads=[B1, B2, B3, Bch[fc]], writes=[B3])
                    if hf + 1 < NH:
                        op("dve", lambda e, T3=T3: e.tensor_copy(out=carry_h[:, fc:fc + 1], in_=T3[:, TH - 1:TH]), reads=[B3], writes=[Bch[fc]])
                    op("dve", lambda e, T3=T3: e.tensor_tensor(out=G[:, fc, :], in0=G[:, fc, :], in1=T3[:], op=ALU.mult), reads=[B3, BG[fc]], writes=[BG[fc]])
                for dc in range(NC8):
                    wo, Bwo = load_wp("ro", wout_d[slot], 0, dc, RCH)
                    if dc % 2 == 0 and dc + 2 < NC8:
                        prefetch_wp("ro", wout_d[slot], 0, dc + 2, RCH)
                    for ti, tt in enumerate(tts):
                        bank = next_bank()
                        matmul_group(bank, slice(0, 512), [(wo[:, k, :], G[:, k, ti * 512:(ti + 1) * 512]) for k in range(RCH)],
                                     [Bwo] + BG)
                        resid_add(bank, dc, tt)
            S_.barrier()

    def ffn(l):
        with contextlib.ExitStack() as st:
            work = norm_work(st)
            aT = sb("aT", [128, FCH, TH], BF16, st)
            Ba = [Buf() for _ in range(FCH)]
            Ur = Ring([sb("Uf%d" % i, [128, TH + 2], F32, st) for i in range(4)])
            hn, HN = hn_alloc(st, NTH)
            Yr = Ring([sb("Yf%d" % i, [128, TH], F32, st) for i in range(4)])
            Ybr = Ring([sb("Yb%d" % i, [128, TH], BF16, st) for i in range(2)])
            Gr = Ring([sb("Gf%d" % i, [128, TH], BF16, st) for i in range(2)])
            bank_ring[0] = list(range(8))
            for hf in range(NH):
                tts = list(range(hf * NTH, (hf + 1) * NTH))
                for ti, tt in enumerate(tts):
                    norm_tile(tt, "norm_ffn", l, work, (hn, HN, ti))
                for j in range(FCH):
                    res = []
                    cur = [load_wp("u%d" % gv, wup_d[l], gv * FCH, j, 8) for gv in range(2)]
                    if j % 2 == 0:
                        if j + 2 < FCH:
                            for gv in range(2):
                                prefetch_wp("u%d" % gv, wup_d[l], gv * FCH, j + 2, 8)
                        else:
                            prefetch_wp("dA", wdown_d[l], 0, 0, 11, (0, 11))
                            prefetch_wp("dB", wdown_d[l], 0, 0, 11, (11, 22))
                    for gv in range(2):
                        fch = gv * FCH + j
                        w_, Bw_ = cur[gv]
                        U, BU = Ur.next()
                        if hf == 0:
                            op("pool", lambda e, U=U: e.memset(U[:, 0:2], 0.0), writes=[BU])
                        else:
                            op("pool", lambda e, U=U, fch=fch: e.tensor_copy(out=U[:, 0:2], in_=carry_f[:, fch, :]), reads=[Bcf[fch]], writes=[BU])
                        for ti, tt in enumerate(tts):
                            bank = next_bank()
                            matmul_group(bank, slice(0, 512), [(w_[:, k, :], hn[:, k, ts(ti)]) for k in range(8)],
                                         [Bw_] + [HN[k][ti] for k in range(8)])
                            op("act", lambda e, U=U, ti=ti, bank=bank: e.activation(out=U[:, 2 + ti * 512:2 + (ti + 1) * 512], in_=psum[bank][:, :], func=AF.Identity),
                               reads=[PB[bank]], writes=[BU])
                        if hf + 1 < NH:
                            op("pool", lambda e, U=U, fch=fch: e.tensor_copy(out=carry_f[:, fch, :], in_=U[:, TH:TH + 2]), reads=[BU], writes=[Bcf[fch]])
                        Y, BY = Yr.next()
                        cw = lambda k, fch=fch: ppc("ffn_conv_w", (l * 3 + k) * 44 + fch)
                        op("act", lambda e, U=U, Y=Y, fch=fch: e.activation(out=Y[:], in_=U[:, 2:2 + TH], func=AF.Identity,
                                                                           scale=cw(2), bias=ppc("ffn_conv_b", l * 44 + fch)),
                           reads=[BU, Bpp], writes=[BY])
                        op("dve", lambda e, U=U, Y=Y: e.scalar_tensor_tensor(out=Y[:], in0=U[:, 1:1 + TH], scalar=cw(1), in1=Y[:],
                                                                             op0=ALU.mult, op1=ALU.add),
                           reads=[BU, BY, Bpp], writes=[BY])
                        if gv == 0:
                            op("dve", lambda e, U=U, Y=Y: e.scalar_tensor_tensor(out=Y[:], in0=U[:, 0:TH], scalar=cw(0), in1=Y[:],
                                                                                 op0=ALU.mult, op1=ALU.add),
                               reads=[BU, BY, Bpp], writes=[BY])
                            Gt, BGt = Gr.next()
                            op("act", lambda e, Y=Y, Gt=Gt: e.activation(out=Gt[:], in_=Y[:], func=AF.Gelu_apprx_tanh), reads=[BY], writes=[BGt])
                            res.append((Gt, BGt))
                        else:
                            Yb, BYb = Ybr.next()
                            op("dve", lambda e, U=U, Y=Y, Yb=Yb: e.scalar_tensor_tensor(out=Yb[:], in0=U[:, 0:TH], scalar=cw(0), in1=Y[:],
                                                                                       op0=ALU.mult, op1=ALU.add),
                               reads=[BU, BY, Bpp], writes=[BYb])
                            res.append((Yb, BYb))
                    (Gt, BGt), (Yb, BYb) = res
                    op("dve", lambda e, Gt=Gt, Yb=Yb, j=j: e.tensor_tensor(out=aT[:, j, :], in0=Gt[:], in1=Yb[:], op=ALU.mult),
                       reads=[BGt, BYb], writes=[Ba[j]])
                for dc in range(NC8):
                    wdA, BwdA = load_wp("dA", wdown_d[l], 0, dc, 11, (0, 11))
                    wdB, BwdB = load_wp("dB", wdown_d[l], 0, dc, 11, (11, 22))
                    if dc % 2 == 0:
                        if dc + 2 < NC8:
                            prefetch_wp("dA", wdown_d[l], 0, dc + 2, 11, (0, 11))
                            prefetch_wp("dB", wdown_d[l], 0, dc + 2, 11, (11, 22))
                        elif hf + 1 < NH:
                            for gv in range(2):
                                prefetch_wp("u%d" % gv, wup_d[l], gv * FCH, 0, 8)
                    for ti, tt in enumerate(tts):
                        bank = next_bank()
                        matmul_group(bank, slice(0, 512),
                                     [((wdA[:, k, :] if k < 11 else wdB[:, k - 11, :]), aT[:, k, ti * 512:(ti + 1) * 512]) for k in range(FCH)],
                                     [BwdA, BwdB] + Ba)
                        resid_add(bank, dc, tt)
            S_.barrier()

    def ple(l, sq_i):
        with contextlib.ExitStack() as st:
            work = norm_work(st)
            pT = Ring([sb("pT%d" % i, [128, 2, 512], BF16, st) for i in range(2)])
            Sg = Ring([sb("Sg%d" % i, [128, 512], F32, st) for i in range(2)])
            hn, HN = hn_alloc(st, 2)
            pst = [sb("pst%d" % i, [128, PLE], F32, st) for i in range(4)]
            pstB = [Buf() for _ in range(4)]
            S_.sync_from("sp")
            bank_ring[0] = list(range(8))
            for tt in range(NTT):
                tl = tt % 2
                norm_tile(tt, "norm_ple", l, work, (hn, HN, tl))
                for q in range(4):
                    S_.dma("sp", pst[q][:], p_d[l, sq_i, tt * 512 + q * 128: tt * 512 + (q + 1) * 128, :], pstS[q], writes=[pstB[q]])
                pt, Bpt = pT.next()
                for ec in range(2):
                    bank = next_bank()
                    for q in range(4):
                        op("pe", lambda e, q=q, ec=ec, bank=bank: e.transpose(out=psum[bank][:, q * 128:(q + 1) * 128],
                                                                             in_=pst[q][:, ec * 128:(ec + 1) * 128], identity=ident),
                           reads=[pstB[q], Bcst], writes=[PB[bank]])
                    op("act", lambda e, pt=pt, ec=ec, bank=bank: e.activation(out=pt[:, ec, :], in_=psum[bank][:, :], func=AF.Identity),
                       reads=[PB[bank]], writes=[Bpt])
                for dc in range(NC8):
                    wg, Bwg = load_wp("pg", wpg_d[l], 0, dc, 8)
                    wp, Bwp = load_wp("pp", wpp_d[l], 0, dc, 2)
                    if dc % 2 == 0:
                        ndc = dc + 2 if dc + 2 < NC8 else (0 if tt + 1 < NTT else None)
                        if ndc is not None:
                            prefetch_wp("pg", wpg_d[l], 0, ndc, 8)
                            prefetch_wp("pp", wpp_d[l], 0, ndc, 2)
                    bg = next_bank()
                    matmul_group(bg, slice(0, 512), [(wg[:, k, :], hn[:, k, ts(tl)]) for k in range(8)],
                                 [Bwg] + [HN[k][tl] for k in range(8)])
                    bp = next_bank()
                    matmul_group(bp, slice(0, 512), [(wp[:, k, :], pt[:, k, :]) for k in range(2)], [Bwp, Bpt])
                    sg, Bsg = Sg.next()
                    op("act", lambda e, sg=sg, bg=bg: e.activation(out=sg[:], in_=psum[bg][:, :], func=AF.Sigmoid), reads=[PB[bg]], writes=[Bsg])
                    op("dve", lambda e, sg=sg, bp=bp: e.tensor_tensor(out=sg[:], in0=sg[:], in1=psum[bp][:, :], op=ALU.mult),
                       reads=[Bsg, PB[bp]], writes=[Bsg])
                    op("dve", lambda e, sg=sg, dc=dc: e.tensor_tensor(out=xT[:, dc, ts(tt)], in0=xT[:, dc, ts(tt)], in1=sg[:], op=ALU.add),
                       reads=[Bsg, X[dc][tt]], writes=[X[dc][tt]])
            S_.barrier()

    for sq_i in range(NSEQ):
        load_x(sq_i)
        for l in range(DEPTH):
            if l % 2 == 0:
                attention(l)
            else:
                rglru(l)
            ffn(l)
            ple(l, sq_i)
        final_store(sq_i)
    es.close()
    return nc


def prep_shared(inp, depth):
    pp = PP(depth)
    nr = max(depth // 2, 1)
    tab = np.zeros((128, pp.n), np.float32)

    def put(name, arr):
        tab[:, pp.off[name]:pp.off[name] + arr.shape[1]] = arr

    put("norm_mix", _cols(inp["norm_mix"], 8))
    put("norm_ffn", _cols(inp["norm_ffn"], 8))
    put("norm_ple", _cols(inp["norm_ple"], 8))
    put("norm_final", _cols(inp["norm_final"], 8))
    put("ffn_conv_w", _cols(inp["ffn_conv_w"], 44))
    put("ffn_conv_b", _cols(inp["ffn_conv_b"], 44))
    put("rnn_conv_w", _cols(inp["rnn_conv_w"], 10))
    put("rnn_conv_b", _cols(inp["rnn_conv_b"], 10))
    put("b_gate_a", _cols(inp["rnn_b_gate_a"], 10))
    put("b_gate_x", _cols(inp["rnn_b_gate_x"], 10))
    put("lru", _cols(inp["rnn_lru_param"], 10))
    shared = {
        "pp": tab,
        "cst": make_consts(),
        "wband_a": _band(np.asarray(inp["rnn_w_gate_a"], np.float32)),
        "wband_x": _band(np.asarray(inp["rnn_w_gate_x"], np.float32)),
    }
    for k in ("attn_w_qkv", "attn_w_o", "rnn_w_in", "rnn_w_out", "ffn_w_up", "ffn_w_down", "ple_w_gate", "ple_w_proj"):
        shared[k] = np.ascontiguousarray(np.asarray(inp[k], np.float32))
    return shared


def run(inp, n_cores, depth):
    x = np.asarray(inp["x"], np.float32)
    p = np.asarray(inp["p"], np.float32)
    B, S, _ = x.shape
    assert B % n_cores == 0
    nseq = B // n_cores
    shared = prep_shared(inp, depth)
    nc = build(nseq, S, depth)
    in_maps = []
    for i in range(n_cores):
        m = dict(shared)
        m["x"] = np.ascontiguousarray(x[i * nseq:(i + 1) * nseq])
        m["p"] = np.ascontiguousarray(p[:, i * nseq:(i + 1) * nseq])
        in_maps.append(m)
    res = run_bass_kernel_spmd(nc, in_maps, core_ids=list(range(n_cores)))
    return np.concatenate([np.asarray(r["out"], np.float32) for r in res.results], axis=0)


def kernel(**inputs):
    return run(inputs, N_CORES, 4)
```

```python
import contextlib
import numpy as np
import concourse.bass as bass
import concourse.mybir as mybir
from concourse.bass_utils import run_bass_kernel_spmd

F32 = mybir.dt.float32
BF16 = mybir.dt.bfloat16
AF = mybir.ActivationFunctionType
ALU = mybir.AluOpType

D = 1024
NC8 = 8
HEADS = 16
DH = 64
RW = 1280
RCH = 10
RB = 80
FF = 2816
FCH = 22
PLE = 256
EPS = 1e-6
LRU_C = 8.0
N_CORES = 8


class SemObj:
    def __init__(self, sem, name):
        self.sem = sem
        self.name = name
        self.count = 0


class Buf:
    __slots__ = ("name", "w", "r")

    def __init__(self, name=""):
        self.name = name
        self.w = None
        self.r = {}


class Sched:
    def __init__(self, nc):
        self.nc = nc
        self.eng = {"pe": nc.tensor, "act": nc.scalar, "dve": nc.vector, "pool": nc.gpsimd, "sp": nc.sync}
        self.esem = {k: SemObj(nc.alloc_semaphore("es_" + k), k) for k in self.eng}
        self.waited = {k: {} for k in self.eng}
        self.ndsem = 0
        self.ninst = 0

    def dsem(self):
        self.ndsem += 1
        return SemObj(self.nc.alloc_semaphore("ds%d" % self.ndsem), "ds%d" % self.ndsem)

    def _deps(self, en, reads, writes):
        deps = {}

        def add(tok):
            if tok is None:
                return
            so, v = tok
            if en == "pe" and so is self.esem["pe"]:
                return
            if deps.get(so, 0) < v:
                deps[so] = v

        for b in reads:
            add(b.w)
        for b in writes:
            add(b.w)
            for so, v in b.r.items():
                add((so, v))
        w = self.waited[en]
        e = self.eng[en]
        for so, v in deps.items():
            if w.get(so, 0) >= v:
                continue
            assert so.count >= v, "waiting on un-issued increment %s %d>%d" % (so.name, v, so.count)
            e.wait_ge(so.sem, v)
            self.ninst += 1
            w[so] = v

    def _mark(self, tok, reads, writes):
        so, v = tok
        for b in reads:
            if b.r.get(so, 0) < v:
                b.r[so] = v
        for b in writes:
            b.w = tok
            b.r = {}

    def op(self, en, fn, reads=(), writes=(), inc=True):
        self._deps(en, reads, writes)
        ins = fn(self.eng[en])
        self.ninst += 1
        so = self.esem[en]
        if inc:
            so.count += 1
            ins.then_inc(so.sem, 1)
            tok = (so, so.count)
        else:
            tok = (so, so.count + 1)
        self._mark(tok, reads, writes)
        return ins

    def dma(self, q, out, in_, ds, reads=(), writes=(), **kw):
        self._deps(q, reads, writes)
        ins = self.eng[q].dma_start(out=out, in_=in_, **kw)
        self.ninst += 1
        ds.count += 16
        ins.then_inc(ds.sem, 16)
        self._mark((ds, ds.count), reads, writes)
        return ins

    def wait_all(self, en, bufs):
        self._deps(en, bufs, bufs)

    def sync_from(self, en, others=("pe", "act", "dve")):
        w = self.waited[en]
        for other in others:
            so = self.esem[other]
            if so.count > w.get(so, 0):
                self.eng[en].wait_ge(so.sem, so.count)
                self.ninst += 1
                w[so] = so.count

    def barrier(self, engines=("pe", "act", "dve", "pool")):
        for en in engines:
            w = self.waited[en]
            for other in engines:
                if other == en:
                    continue
                so = self.esem[other]
                if so.count > w.get(so, 0):
                    self.eng[en].wait_ge(so.sem, so.count)
                    self.ninst += 1
                    w[so] = so.count


class Ring:
    def __init__(self, tiles):
        self.tiles = tiles
        self.bufs = [Buf() for _ in tiles]
        self.i = 0

    def next(self):
        k = self.i % len(self.tiles)
        self.i += 1
        return self.tiles[k], self.bufs[k]


def _cols(v, nch):
    lead = int(np.prod(v.shape[:-1])) if v.ndim > 1 else 1
    a = np.asarray(v, np.float32).reshape(lead, nch, 128)
    return np.ascontiguousarray(a.transpose(2, 0, 1).reshape(128, lead * nch))


def band_range(fc):
    b_lo = (128 * fc) // RB
    b_hi = (128 * fc + 127) // RB
    k_lo = (RB * b_lo) // 128
    k_hi = (RB * b_hi + RB - 1) // 128
    return k_lo, min(k_hi, RCH - 1)


def _band(wg):
    L = wg.shape[0]
    dense = np.zeros((L, RW, RW), np.float32)
    for h in range(RW // RB):
        dense[:, h * RB:(h + 1) * RB, h * RB:(h + 1) * RB] = wg[:, h]
    out = np.zeros((L, 128, RCH, 3, 128), np.float32)
    for fc in range(RCH):
        k_lo, k_hi = band_range(fc)
        for kk, kc in enumerate(range(k_lo, k_hi + 1)):
            out[:, :, fc, kk, :] = dense[:, kc * 128:(kc + 1) * 128, fc * 128:(fc + 1) * 128]
    return out


class PP:
    def __init__(self, depth):
        self.off = {}
        self.n = 0
        nr = max(depth // 2, 1)
        for name, cnt in [("norm_mix", depth * 8), ("norm_ffn", depth * 8), ("norm_ple", depth * 8), ("norm_final", 8),
                          ("ffn_conv_w", depth * 3 * 44), ("ffn_conv_b", depth * 44),
                          ("rnn_conv_w", nr * 4 * 10), ("rnn_conv_b", nr * 10), ("b_gate_a", nr * 10),
                          ("b_gate_x", nr * 10), ("lru", nr * 10), ("lruc", nr * 10)]:
            self.off[name] = self.n
            self.n += cnt


def make_consts():
    j = np.arange(128)[:, None]
    s = np.arange(128)[None, :]
    ident = (j == s).astype(np.float32)
    negtri = -(j >= s).astype(np.float32)
    negones = -np.ones((128, 128), np.float32)
    masklt = (j < s).astype(np.float32)
    ones = np.ones((128, 128), np.float32)
    return np.ascontiguousarray(np.concatenate([ident, negtri, negones, masklt, ones], axis=1))


def build(NSEQ, S, DEPTH):
    assert S % 512 == 0
    NTT = S // 512
    NQB = S // 128
    TH = min(S, 1024)
    NH = S // TH
    NTH = TH // 512
    NATT = (DEPTH + 1) // 2
    NRNN = max(DEPTH // 2, 1)
    pp = PP(DEPTH)

    nc = bass.Bass("TRN2", target_bir_lowering=False)

    def dram(name, shape, kind="ExternalInput", dt=F32):
        return nc.dram_tensor(name, list(shape), dt, kind=kind).ap()

    x_d = dram("x", [NSEQ, S, D])
    p_d = dram("p", [DEPTH, NSEQ, S, PLE])
    out_d = dram("out", [NSEQ, S, D], kind="ExternalOutput")
    wqkv_d = dram("attn_w_qkv", [NATT, D, 3 * D])
    wo_d = dram("attn_w_o", [NATT, D, D])
    win_d = dram("rnn_w_in", [NRNN, D, 2 * RW])
    wout_d = dram("rnn_w_out", [NRNN, RW, D])
    wup_d = dram("ffn_w_up", [DEPTH, D, 2 * FF])
    wdown_d = dram("ffn_w_down", [DEPTH, FF, D])
    wpg_d = dram("ple_w_gate", [DEPTH, D, D])
    wpp_d = dram("ple_w_proj", [DEPTH, PLE, D])
    wba_d = dram("wband_a", [NRNN, 128, RCH, 3, 128])
    wbx_d = dram("wband_x", [NRNN, 128, RCH, 3, 128])
    pp_d = dram("pp", [128, pp.n])
    cst_d = dram("cst", [128, 5 * 128])

    S_ = Sched(nc)
    op = S_.op
    es = contextlib.ExitStack()

    uid = [0]

    def sb(name, shape, dt, stack=None):
        uid[0] += 1
        return (stack or es).enter_context(nc.sbuf_tensor("%s_u%d" % (name, uid[0]), list(shape), dt))

    xT = sb("xT", [128, NC8, S], F32)
    ppt = sb("ppt", [128, pp.n], F32)
    cst32 = sb("cst32", [128, 5 * 128], F32)
    cstb = sb("cstb", [128, 5 * 128], BF16)
    ident = cst32[:, 0:128]
    negtri = cstb[:, 128:256]
    negones = cstb[:, 256:384]
    masklt = cstb[:, 384:512]
    onesb = cstb[:, 512:640]
    NW = 4
    WSZ = FCH * 128
    wslots = [sb("w%d" % i, [128, WSZ], BF16) for i in range(NW)]
    wbufs = [Buf("w%d" % i) for i in range(NW)]
    wsems = [S_.dsem() for _ in range(NW)]
    wcnt = [0]
    stgS = [S_.dsem() for _ in range(4)]
    pstS = [S_.dsem() for _ in range(4)]
    carry_f = sb("carry_f", [128, 2 * FCH, 2], F32)
    carry_r = sb("carry_r", [128, RCH, 3], F32)
    carry_h = sb("carry_h", [128, RCH], F32)
    Bcf = [Buf() for _ in range(2 * FCH)]
    Bcr = [Buf() for _ in range(RCH)]
    Bch = [Buf() for _ in range(RCH)]

    psum = [es.enter_context(nc.psum_tensor("ps%d" % i, [128, 512], F32)) for i in range(8)]
    PB = [Buf("ps%d" % i) for i in range(8)]
    bank_ring = [list(range(8))]
    bank_i = [0]

    def next_bank():
        r = bank_ring[0]
        b = r[bank_i[0] % len(r)]
        bank_i[0] += 1
        return b

    X = [[Buf() for _ in range(NTT)] for _ in range(NC8)]
    Bpp = Buf("pp")
    Bcst = Buf("cst")

    def ts(tt):
        return slice(tt * 512, (tt + 1) * 512)

    def ppc(name, idx):
        o = pp.off[name] + idx
        return ppt[:, o:o + 1]

    d0 = S_.dsem()
    S_.dma("sp", ppt[:], pp_d[:, :], d0, writes=[Bpp])
    d1 = S_.dsem()
    S_.dma("sp", cst32[:], cst_d[:, :], d1, writes=[Bcst])
    op("dve", lambda e: e.tensor_copy(out=cstb[:], in_=cst32[:]), reads=[Bcst], writes=[Bcst])
    nl = NRNN * RCH
    lo, lc = pp.off["lru"], pp.off["lruc"]
    op("act", lambda e: e.activation(out=ppt[:, lc:lc + nl], in_=ppt[:, lo:lo + nl], func=AF.Exp, scale=-1.0),
       reads=[Bpp], writes=[Bpp])
    op("act", lambda e: e.activation(out=ppt[:, lc:lc + nl], in_=ppt[:, lc:lc + nl], func=AF.Ln, bias=1.0),
       reads=[Bpp], writes=[Bpp])
    op("dve", lambda e: e.tensor_scalar(out=ppt[:, lc:lc + nl], in0=ppt[:, lc:lc + nl], scalar1=-LRU_C, scalar2=None,
                                        op0=ALU.mult), reads=[Bpp], writes=[Bpp])

    def load_w(view, K, n=128):
        i = wcnt[0] % NW
        wcnt[0] += 1
        t = wslots[i][:, 0:K * n].rearrange("p (k n) -> p k n", k=K)
        S_.dma("pool", t, view, wsems[i], writes=[wbufs[i]])
        return t, wbufs[i]

    pair_cache = {}

    def load_wp(name, w_ap, col0_chunk, idx, K, krange=None):
        if idx % 2 == 0:
            key = (name, col0_chunk + idx)
            if key in stash:
                pair_cache[name] = stash.pop(key)
            else:
                pair_cache[name] = slab_load(w_ap, col0_chunk + idx, K, krange)
        t, B = pair_cache[name]
        return t[:, :, (idx % 2) * 128:(idx % 2 + 1) * 128], B

    stash = {}

    def slab_load(w_ap, chunk, K, krange):
        f0 = chunk * 128
        v = w_ap.rearrange("(k p) f -> p k f", p=128)
        v = v[:, krange[0]:krange[1], f0:f0 + 256] if krange else v[:, :, f0:f0 + 256]
        return load_w(v, K, 256)

    def prefetch_wp(name, w_ap, col0_chunk, idx, K, krange=None):
        key = (name, col0_chunk + idx)
        if key not in stash:
            stash[key] = slab_load(w_ap, col0_chunk + idx, K, krange)

    def wview(w_ap, f0, n=128):
        return w_ap.rearrange("(k p) f -> p k f", p=128)[:, :, f0:f0 + n]

    def matmul_group(bank, cols, pairs, reads, inc_last=True):
        n = len(pairs)
        for k, (l, r) in enumerate(pairs):
            op("pe", lambda e, l=l, r=r, k=k: e.matmul(psum[bank][:, cols], lhsT=l, rhs=r, start=(k == 0), stop=(k == n - 1)),
               reads=reads, writes=[PB[bank]], inc=(inc_last and k == n - 1))

    def norm_tile(tt, gname, gidx, work, dst=None, out_f32=None):
        sq_ring, rs_ring = work
        if dst is not None:
            hn, HN, tl = dst
        bank = next_bank()
        for c in range(NC8):
            sq, Bsq = sq_ring.next()
            op("act", lambda e, c=c, sq=sq: e.activation(out=sq[:], in_=xT[:, c, ts(tt)], func=AF.Square),
               reads=[X[c][tt]], writes=[Bsq])
            op("pe", lambda e, c=c, sq=sq: e.matmul(psum[bank][:, :], lhsT=onesb, rhs=sq[:], start=(c == 0), stop=(c == NC8 - 1)),
               reads=[Bsq, Bcst], writes=[PB[bank]])
        rs, Brs = rs_ring.next()
        op("act", lambda e: e.activation(out=rs[:], in_=psum[bank][:, :], func=AF.Sqrt, scale=1.0 / D, bias=EPS),
           reads=[PB[bank]], writes=[Brs])
        op("dve", lambda e: e.reciprocal(out=rs[:], in_=rs[:]), reads=[Brs], writes=[Brs])
        for c in range(NC8):
            o = hn[:, c, ts(tl)] if out_f32 is None else out_f32[0][:, c, :]
            wb = [HN[c][tl]] if out_f32 is None else [out_f32[1]]
            op("dve", lambda e, c=c, o=o: e.scalar_tensor_tensor(out=o, in0=xT[:, c, ts(tt)], scalar=ppc(gname, gidx * 8 + c),
                                                                 in1=rs[:], op0=ALU.mult, op1=ALU.mult),
               reads=[X[c][tt], Brs, Bpp], writes=wb)

    def resid_add(bank, dc, tt):
        op("dve", lambda e: e.tensor_tensor(out=xT[:, dc, ts(tt)], in0=xT[:, dc, ts(tt)], in1=psum[bank][:, :], op=ALU.add),
           reads=[PB[bank], X[dc][tt]], writes=[X[dc][tt]])

    def hn_alloc(stack, ntl):
        t = sb("hn", [128, NC8, ntl * 512], BF16, stack)
        return t, [[Buf() for _ in range(ntl)] for _ in range(NC8)]

    def norm_work(stack):
        sq_ring = Ring([sb("sq%d" % i, [128, 512], BF16, stack) for i in range(2)])
        rs_ring = Ring([sb("rs%d" % i, [128, 512], F32, stack) for i in range(2)])
        return sq_ring, rs_ring

    def load_x(sq_i):
        bank_ring[0] = list(range(8))
        st = contextlib.ExitStack()
        stg = [sb("stg%d" % i, [128, D], F32, st) for i in range(4)]
        stgB = [Buf() for _ in range(4)]
        S_.sync_from("sp")
        for tt in range(NTT):
            for q in range(4):
                S_.dma("sp", stg[q][:], x_d[sq_i, tt * 512 + q * 128: tt * 512 + (q + 1) * 128, :], stgS[q], writes=[stgB[q]])
            for c in range(NC8):
                bank = next_bank()
                for q in range(4):
                    op("pe", lambda e, q=q, c=c: e.transpose(out=psum[bank][:, q * 128:(q + 1) * 128],
                                                             in_=stg[q][:, c * 128:(c + 1) * 128], identity=ident),
                       reads=[stgB[q], Bcst], writes=[PB[bank]])
                op("act", lambda e, c=c: e.activation(out=xT[:, c, ts(tt)], in_=psum[bank][:, :], func=AF.Identity),
                   reads=[PB[bank]], writes=[X[c][tt]])
        S_.barrier()
        st.close()

    def final_store(sq_i):
        with contextlib.ExitStack() as st:
            work = norm_work(st)
            yf = sb("yf", [128, NC8, 512], F32, st)
            Byf = Buf()
            stg = [sb("stg%d" % i, [128, D], F32, st) for i in range(4)]
            stgB = [Buf() for _ in range(4)]
            bank_ring[0] = list(range(8))
            for tt in range(NTT):
                norm_tile(tt, "norm_final", 0, work, out_f32=(yf, Byf))
                for q in range(4):
                    b0, b1 = next_bank(), next_bank()
                    for c in range(NC8):
                        bank = b0 if c < 4 else b1
                        op("pe", lambda e, c=c, bank=bank: e.transpose(out=psum[bank][:, (c % 4) * 128:(c % 4 + 1) * 128],
                                                                      in_=yf[:, c, q * 128:(q + 1) * 128], identity=ident),
                           reads=[Byf, Bcst], writes=[PB[bank]])
                    op("act", lambda e: e.activation(out=stg[q][:, 0:512], in_=psum[b0][:, :], func=AF.Identity),
                       reads=[PB[b0]], writes=[stgB[q]])
                    op("dve", lambda e: e.tensor_copy(out=stg[q][:, 512:1024], in_=psum[b1][:, :]),
                       reads=[PB[b1]], writes=[stgB[q]])
                    S_.dma("sp", out_d[sq_i, tt * 512 + q * 128: tt * 512 + (q + 1) * 128, :], stg[q][:], stgS[q], reads=[stgB[q]])
            for en in ("sp", "pe", "act", "dve"):
                S_.wait_all(en, stgB)
            S_.barrier()

    def attention(l):
        slot = l // 2
        with contextlib.ExitStack() as st:
            OT = sb("OT", [128, NC8, S], BF16, st)
            BO = [[Buf() for _ in range(NTT)] for _ in range(NC8)]
            QTs = [sb("QT0", [128, S], BF16, st)]
            KTs = [sb("KT0", [128, S], BF16, st)]
            Vbs = [sb("Vb0", [128, NQB, 128], BF16, st)]
            BQs, BKs, BVs = [Buf(), Buf()], [Buf(), Buf()], [Buf(), Buf()]
            NSTR = 4
            Er = [Ring([sb("E%d_%d" % (s, i), [128, 512], BF16, st) for i in range(1)]) for s in range(NSTR)]
            Lr = [Ring([sb("L%d_%d" % (s, i), [128, 512], BF16, st) for i in range(2)]) for s in range(NSTR)]
            Wr = [Ring([sb("W%d_%d" % (s, i), [128, 512], BF16, st) for i in range(2)]) for s in range(NSTR)]
            Lsuf = [sb("Ls%d" % s, [128, 512], BF16, st) for s in range(NSTR)]
            BLs = [Buf() for _ in range(NSTR)]

            hn, HN = hn_alloc(st, NTT)
            bank_ring[0] = list(range(8))
            st2 = contextlib.ExitStack()
            work = norm_work(st2)
            for tt in range(NTT):
                norm_tile(tt, "norm_mix", l, work, (hn, HN, tt))
            S_.barrier()
            st2.close()
            QTs.append(sb("QT1", [128, S], BF16, st))
            KTs.append(sb("KT1", [128, S], BF16, st))
            Vbs.append(sb("Vb1", [128, NQB, 128], BF16, st))
            bank_ring[0] = [6, 7]
            if NTT == 4:
                pairs = [[0, 3], [1, 2]]
            elif NTT == 2:
                pairs = [[0], [1]]
            else:
                pairs = [list(range(NTT))]
            def proj_units(c):
                cb = c % 2
                QT, KT, Vb, BQ, BK, BV = QTs[cb], KTs[cb], Vbs[cb], BQs[cb], BKs[cb], BVs[cb]
                wq, Bwq = load_wp("q", wqkv_d[slot], 0, c, 8)
                wk, Bwk = load_wp("k", wqkv_d[slot], 8, c, 8)
                wv, Bwv = load_wp("v", wqkv_d[slot], 16, c, 8)
                for tt in range(NTT):
                    bank = next_bank()
                    matmul_group(bank, slice(0, 512), [(wq[:, k, :], hn[:, k, ts(tt)]) for k in range(8)],
                                 [Bwq] + [HN[k][tt] for k in range(8)])
                    op("dve", lambda e: e.tensor_scalar(out=QT[:, ts(tt)], in0=psum[bank][:, :], scalar1=DH ** -0.5, scalar2=None,
                                                        op0=ALU.mult), reads=[PB[bank]], writes=[BQ])
                    yield
                    bank = next_bank()
                    matmul_group(bank, slice(0, 512), [(wk[:, k, :], hn[:, k, ts(tt)]) for k in range(8)],
                                 [Bwk] + [HN[k][tt] for k in range(8)])
                    op("dve", lambda e: e.tensor_copy(out=KT[:, ts(tt)], in_=psum[bank][:, :]), reads=[PB[bank]], writes=[BK])
                    yield
                for tt in range(NTT):
                    bank = next_bank()
                    for q in range(4):
                        tb = tt * 4 + q
                        matmul_group(bank, slice(q * 128, (q + 1) * 128),
                                     [(hn[:, k, tb * 128:(tb + 1) * 128], wv[:, k, :]) for k in range(8)],
                                     [Bwv] + [HN[k][tt] for k in range(8)])
                    op("dve", lambda e: e.tensor_copy(out=Vb[:, tt * 4:(tt + 1) * 4, :],
                                                      in_=psum[bank][:, :].rearrange("p (q n) -> p q n", q=4)),
                       reads=[PB[bank]], writes=[BV])
                    yield

            for _ in proj_units(0):
                pass
            for c in range(NC8):
                cb = c % 2
                QT, KT, Vb, BQ, BK, BV = QTs[cb], KTs[cb], Vbs[cb], BQs[cb], BKs[cb], BVs[cb]
                if c + 1 < NC8 and (c + 1) % 2 == 0:
                    prefetch_wp("q", wqkv_d[slot], 0, c + 1, 8)
                    prefetch_wp("k", wqkv_d[slot], 8, c + 1, 8)
                    prefetch_wp("v", wqkv_d[slot], 16, c + 1, 8)
                nxt = proj_units(c + 1) if c + 1 < NC8 else iter(())
                steps = []
                for ps_i, tcs in enumerate(pairs):
                    lst = []
                    for tc in tcs:
                        kbs = list(range(4 * tc + 3, -1, -1))
                        for idx, kb in enumerate(kbs):
                            lst.append((tc, kb, idx, idx == len(kbs) - 1))
                    steps.append(lst)
                nround = max(len(s) for s in steps)
                acts, infos = [], []
                for r in range(nround):
                    act = [(ps_i, hh) for ps_i in range(len(pairs)) if r < len(steps[ps_i]) for hh in range(2)]
                    info = {}
                    for (ps_i, hh) in act:
                        tc, kb, idx, last = steps[ps_i][r]
                        sid = ps_i * 2 + hh
                        off = max(0, kb - 4 * tc)
                        c0 = off * 128
                        diag = kb >= 4 * tc
                        info[(ps_i, hh)] = (tc, kb, idx, last, sid, c0, diag)
                    acts.append(act)
                    infos.append(info)

                def emit_z(r):
                    for key in acts[r]:
                        tc, kb, idx, last, sid, c0, diag = infos[r][key]
                        hh = key[1]
                        hp = slice(64 * hh, 64 * hh + 64)
                        if idx == 0:
                            op("dve", lambda e, sid=sid: e.memset(Lsuf[sid][:], 0.0), writes=[BLs[sid]])
                        op("pe", lambda e, sid=sid, hp=hp, kb=kb, tc=tc, c0=c0: e.matmul(
                            psum[sid][:, c0:512], lhsT=KT[hp, kb * 128:(kb + 1) * 128], rhs=QT[hp, tc * 512 + c0:(tc + 1) * 512],
                            start=True, stop=False, skip_group_check=True), reads=[BK, BQ], writes=[PB[sid]])

                emit_z(0)
                for r in range(nround):
                    act, info = acts[r], infos[r]
                    tiles = {}
                    for key in act:
                        tc, kb, idx, last, sid, c0, diag = info[key]
                        Et, BE = Er[sid].next()
                        op("act", lambda e, Et=Et, sid=sid, c0=c0: e.activation(out=Et[:, c0:512], in_=psum[sid][:, c0:512], func=AF.Exp),
                           reads=[PB[sid]], writes=[BE])
                        tiles[key] = [Et, BE]
                    for key in act:
                        tc, kb, idx, last, sid, c0, diag = info[key]
                        Et, BE = tiles[key]
                        Lt, BL = Lr[sid].next()
                        op("act", lambda e, Et=Et, Lt=Lt, c0=c0: e.activation(out=Lt[:, c0:512], in_=Et[:, c0:512], func=AF.Ln, bias=1.0),
                           reads=[BE], writes=[BL])
                        if diag:
                            op("dve", lambda e, Lt=Lt, c0=c0: e.tensor_tensor(out=Lt[:, c0:c0 + 128], in0=Lt[:, c0:c0 + 128], in1=masklt, op=ALU.mult),
                               reads=[BL, Bcst], writes=[BL])
                        tiles[key] += [Lt, BL]
                    for key in act:
                        tc, kb, idx, last, sid, c0, diag = info[key]
                        Et, BE, Lt, BL = tiles[key]
                        op("pe", lambda e, sid=sid, Lt=Lt, c0=c0, idx=idx: e.matmul(
                            psum[sid][:, c0:512], lhsT=negtri, rhs=Lt[:, c0:512], start=False, stop=(idx == 0), skip_group_check=True),
                           reads=[BL, Bcst], writes=[PB[sid]])
                        if idx > 0:
                            op("pe", lambda e, sid=sid, c0=c0: e.matmul(
                                psum[sid][:, c0:512], lhsT=negones, rhs=Lsuf[sid][:, c0:512], start=False, stop=True, skip_group_check=True),
                               reads=[BLs[sid], Bcst], writes=[PB[sid]])
                    for key in act:
                        tc, kb, idx, last, sid, c0, diag = info[key]
                        Wt, BW = Wr[sid].next()
                        op("act", lambda e, Wt=Wt, sid=sid, c0=c0: e.activation(out=Wt[:, c0:512], in_=psum[sid][:, c0:512], func=AF.Exp),
                           reads=[PB[sid]], writes=[BW])
                        if diag:
                            op("dve", lambda e, Wt=Wt, c0=c0: e.tensor_tensor(out=Wt[:, c0:c0 + 128], in0=Wt[:, c0:c0 + 128], in1=masklt, op=ALU.mult),
                               reads=[BW, Bcst], writes=[BW])
                        tiles[key] += [Wt, BW]
                    if r + 1 < nround:
                        emit_z(r + 1)
                    for key in act:
                        tc, kb, idx, last, sid, c0, diag = info[key]
                        ps_i, hh = key
                        Et, BE, Lt, BL, Wt, BW = tiles[key]
                        pob = 4 + ps_i
                        op("pe", lambda e, pob=pob, hh=hh, kb=kb, Wt=Wt, c0=c0, idx=idx, last=last: e.matmul(
                            psum[pob][64 * hh:64 * hh + 64, c0:512], lhsT=Vb[:, kb, 64 * hh:64 * hh + 64], rhs=Wt[:, c0:512],
                            start=(idx == 0), stop=last, skip_group_check=True), reads=[BW, BV], writes=[PB[pob]])
                        if not last:
                            op("dve", lambda e, sid=sid, Lt=Lt, c0=c0: e.tensor_tensor(out=Lsuf[sid][:, c0:512], in0=Lsuf[sid][:, c0:512],
                                                                                      in1=Lt[:, c0:512], op=ALU.add),
                               reads=[BL, BLs[sid]], writes=[BLs[sid]])
                    if r >= 1:
                        next(nxt, None)
                    for ps_i in range(len(pairs)):
                        if r < len(steps[ps_i]) and steps[ps_i][r][3]:
                            tc = steps[ps_i][r][0]
                            pob = 4 + ps_i
                            op("dve", lambda e, pob=pob, tc=tc: e.tensor_copy(out=OT[:, c, ts(tc)], in_=psum[pob][:, :]),
                               reads=[PB[pob]], writes=[BO[c][tc]])
                for _ in nxt:
                    pass
            bank_ring[0] = list(range(8))
            for dc in range(NC8):
                wo, Bwo = load_wp("o", wo_d[slot], 0, dc, 8)
                if dc % 2 == 0 and dc + 2 < NC8:
                    prefetch_wp("o", wo_d[slot], 0, dc + 2, 8)
                for tt in range(NTT):
                    bank = next_bank()
                    matmul_group(bank, slice(0, 512), [(wo[:, k, :], OT[:, k, ts(tt)]) for k in range(8)],
                                 [Bwo] + [BO[k][tt] for k in range(8)])
                    resid_add(bank, dc, tt)
            S_.barrier()

    def rglru(l):
        slot = l // 2
        with contextlib.ExitStack() as st:
            work = norm_work(st)
            xr = sb("xr", [128, RCH, TH], BF16, st)
            G = sb("G", [128, RCH, TH], BF16, st)
            Bxr = [Buf() for _ in range(RCH)]
            BG = [Buf() for _ in range(RCH)]
            Ur = Ring([sb("Ur%d" % i, [128, TH + 3], F32, st) for i in range(2)])
            Yr = Ring([sb("Yr%d" % i, [128, TH], F32, st) for i in range(2)])
            T1r = Ring([sb("T1_%d" % i, [128, TH], F32, st) for i in range(2)])
            T2r = Ring([sb("T2_%d" % i, [128, TH], F32, st) for i in range(2)])
            T3r = Ring([sb("T3_%d" % i, [128, TH], F32, st) for i in range(2)])
            hn, HN = hn_alloc(st, NTH)
            bank_ring[0] = list(range(8))
            for hf in range(NH):
                h0 = hf * TH
                tts = list(range(hf * NTH, (hf + 1) * NTH))
                for ti, tt in enumerate(tts):
                    norm_tile(tt, "norm_mix", l, work, (hn, HN, ti))
                for j in range(RCH):
                    wr, Bwr = load_wp("r", win_d[slot], RCH, j, 8)
                    wg, Bwg = load_wp("g", win_d[slot], 0, j, 8)
                    if j % 2 == 0 and j + 2 < RCH:
                        prefetch_wp("r", win_d[slot], RCH, j + 2, 8)
                        prefetch_wp("g", win_d[slot], 0, j + 2, 8)
                    U, BU = Ur.next()
                    if hf == 0:
                        op("pool", lambda e, U=U: e.memset(U[:, 0:3], 0.0), writes=[BU])
                    else:
                        op("pool", lambda e, U=U, j=j: e.tensor_copy(out=U[:, 0:3], in_=carry_r[:, j, :]), reads=[Bcr[j]], writes=[BU])
                    for ti, tt in enumerate(tts):
                        bank = next_bank()
                        matmul_group(bank, slice(0, 512), [(wr[:, k, :], hn[:, k, ts(ti)]) for k in range(8)],
                                     [Bwr] + [HN[k][ti] for k in range(8)])
                        op("act", lambda e, U=U, ti=ti: e.activation(out=U[:, 3 + ti * 512:3 + (ti + 1) * 512], in_=psum[bank][:, :], func=AF.Identity),
                           reads=[PB[bank]], writes=[BU])
                    if hf + 1 < NH:
                        op("pool", lambda e, U=U, j=j: e.tensor_copy(out=carry_r[:, j, :], in_=U[:, TH:TH + 3]), reads=[BU], writes=[Bcr[j]])
                    Y, BY = Yr.next()
                    cw = lambda k, j=j: ppc("rnn_conv_w", (slot * 4 + k) * RCH + j)
                    op("act", lambda e, U=U, Y=Y, j=j: e.activation(out=Y[:], in_=U[:, 3:3 + TH], func=AF.Identity,
                                                                   scale=cw(3), bias=ppc("rnn_conv_b", slot * RCH + j)),
                       reads=[BU, Bpp], writes=[BY])
                    for k in (2, 1):
                        op("dve", lambda e, U=U, Y=Y, k=k: e.scalar_tensor_tensor(out=Y[:], in0=U[:, k:k + TH], scalar=cw(k), in1=Y[:],
                                                                                  op0=ALU.mult, op1=ALU.add),
                           reads=[BU, BY, Bpp], writes=[BY])
                    op("dve", lambda e, U=U, Y=Y, j=j: e.scalar_tensor_tensor(out=xr[:, j, :], in0=U[:, 0:TH], scalar=cw(0), in1=Y[:],
                                                                              op0=ALU.mult, op1=ALU.add),
                       reads=[BU, BY, Bpp], writes=[Bxr[j]])
                    for ti, tt in enumerate(tts):
                        bank = next_bank()
                        matmul_group(bank, slice(0, 512), [(wg[:, k, :], hn[:, k, ts(ti)]) for k in range(8)],
                                     [Bwg] + [HN[k][ti] for k in range(8)])
                        op("act", lambda e, j=j, ti=ti: e.activation(out=G[:, j, ti * 512:(ti + 1) * 512], in_=psum[bank][:, :], func=AF.Gelu_apprx_tanh),
                           reads=[PB[bank]], writes=[BG[j]])
                for fc in range(RCH):
                    k_lo, k_hi = band_range(fc)
                    nk = k_hi - k_lo + 1
                    wa, Bwa = load_w(wba_d[slot, :, fc, 0:nk, :], nk)
                    wx, Bwx = load_w(wbx_d[slot, :, fc, 0:nk, :], nk)
                    T1, B1 = T1r.next()
                    T2, B2 = T2r.next()
                    T3, B3 = T3r.next()
                    for (wt_, Bw_, T_, B_, bn) in ((wa, Bwa, T1, B1, "b_gate_a"), (wx, Bwx, T2, B2, "b_gate_x")):
                        for ti in range(NTH):
                            bank = next_bank()
                            matmul_group(bank, slice(0, 512),
                                         [(wt_[:, kk, :], xr[:, k_lo + kk, ti * 512:(ti + 1) * 512]) for kk in range(nk)],
                                         [Bw_] + [Bxr[k_lo + kk] for kk in range(nk)])
                            op("act", lambda e, T_=T_, ti=ti, bn=bn, bank=bank: e.activation(
                                out=T_[:, ti * 512:(ti + 1) * 512], in_=psum[bank][:, :], func=AF.Sigmoid, bias=ppc(bn, slot * RCH + fc)),
                               reads=[PB[bank], Bpp], writes=[B_])
                    op("act", lambda e, T1=T1: e.activation(out=T1[:], in_=T1[:], func=AF.Exp, scale=ppc("lruc", slot * RCH + fc)),
                       reads=[B1, Bpp], writes=[B1])
                    op("act", lambda e, T1=T1, T3=T3: e.activation(out=T3[:], in_=T1[:], func=AF.Square), reads=[B1], writes=[B3])
                    op("act", lambda e, T3=T3: e.activation(out=T3[:], in_=T3[:], func=AF.Sqrt, scale=-1.0, bias=1.0), reads=[B3], writes=[B3])
                    op("dve", lambda e, T2=T2: e.tensor_tensor(out=T2[:], in0=T2[:], in1=xr[:, fc, :], op=ALU.mult), reads=[B2, Bxr[fc]], writes=[B2])
                    op("dve", lambda e, T2=T2, T3=T3: e.tensor_tensor(out=T2[:], in0=T2[:], in1=T3[:], op=ALU.mult), reads=[B2, B3], writes=[B2])
                    init = 0.0 if hf == 0 else carry_h[:, fc:fc + 1]
                    op("dve", lambda e, T1=T1, T2=T2, T3=T3, init=init: e.tensor_tensor_scan(out=T3[:], data0=T1[:], data1=T2[:], initial=init,
                                                                                            op0=ALU.mult, op1=ALU.add),
                       reads=[B1, B2, B3, Bch[fc]], writes=[B3])
                    if hf + 1 < NH:
                        op("dve", lambda e, T3=T3: e.tensor_copy(out=carry_h[:, fc:fc + 1], in_=T3[:, TH - 1:TH]), reads=[B3], writes=[Bch[fc]])
                    op("dve", lambda e, T3=T3: e.tensor_tensor(out=G[:, fc, :], in0=G[:, fc, :], in1=T3[:], op=ALU.mult), reads=[B3, BG[fc]], writes=[BG[fc]])
                for dc in range(NC8):
                    wo, Bwo = load_wp("ro", wout_d[slot], 0, dc, RCH)
                    if dc % 2 == 0 and dc + 2 < NC8:
                        prefetch_wp("ro", wout_d[slot], 0, dc + 2, RCH)
                    for ti, tt in enumerate(tts):
                        bank = next_bank()
                        matmul_group(bank, slice(0, 512), [(wo[:, k, :], G[:, k, ti * 512:(ti + 1) * 512]) for k in range(RCH)],
                                     [Bwo] + BG)
                        resid_add(bank, dc, tt)
            S_.barrier()

    def ffn(l):
        with contextlib.ExitStack() as st:
            work = norm_work(st)
            aT = sb("aT", [128, FCH, TH], BF16, st)
            Ba = [Buf() for _ in range(FCH)]
            Ur = Ring([sb("Uf%d" % i, [128, TH + 2], F32, st) for i in range(4)])
            hn, HN = hn_alloc(st, NTH)
            Yr = Ring([sb("Yf%d" % i, [128, TH], F32, st) for i in range(4)])
            Ybr = Ring([sb("Yb%d" % i, [128, TH], BF16, st) for i in range(2)])
            Gr = Ring([sb("Gf%d" % i, [128, TH], BF16, st) for i in range(2)])
            bank_ring[0] = list(range(8))
            for hf in range(NH):
                tts = list(range(hf * NTH, (hf + 1) * NTH))
                for ti, tt in enumerate(tts):
                    norm_tile(tt, "norm_ffn", l, work, (hn, HN, ti))
                def flush(pend):
                    jj, Y0, BY0, Yb_, BYb_ = pend
                    Gt, BGt = Gr.next()
                    op("act", lambda e: e.activation(out=Gt[:], in_=Y0[:], func=AF.Gelu_apprx_tanh), reads=[BY0], writes=[BGt])
                    op("dve", lambda e: e.tensor_tensor(out=aT[:, jj, :], in0=Gt[:], in1=Yb_[:], op=ALU.mult),
                       reads=[BGt, BYb_], writes=[Ba[jj]])

                pend = None
                for j in range(FCH):
                    cur = [load_wp("u%d" % gv, wup_d[l], gv * FCH, j, 8) for gv in range(2)]
                    if j % 2 == 0:
                        if j + 2 < FCH:
                            for gv in range(2):
                                prefetch_wp("u%d" % gv, wup_d[l], gv * FCH, j + 2, 8)
                        else:
                            prefetch_wp("dA", wdown_d[l], 0, 0, 11, (0, 11))
                            prefetch_wp("dB", wdown_d[l], 0, 0, 11, (11, 22))
                    for gv in range(2):
                        fch = gv * FCH + j
                        w_, Bw_ = cur[gv]
                        U, BU = Ur.next()
                        if hf == 0:
                            op("pool", lambda e, U=U: e.memset(U[:, 0:2], 0.0), writes=[BU])
                        else:
                            op("pool", lambda e, U=U, fch=fch: e.tensor_copy(out=U[:, 0:2], in_=carry_f[:, fch, :]), reads=[Bcf[fch]], writes=[BU])
                        for ti, tt in enumerate(tts):
                            bank = next_bank()
                            matmul_group(bank, slice(0, 512), [(w_[:, k, :], hn[:, k, ts(ti)]) for k in range(8)],
                                         [Bw_] + [HN[k][ti] for k in range(8)])
                            op("act", lambda e, U=U, ti=ti, bank=bank: e.activation(out=U[:, 2 + ti * 512:2 + (ti + 1) * 512], in_=psum[bank][:, :], func=AF.Identity),
                               reads=[PB[bank]], writes=[BU])
                        if hf + 1 < NH:
                            op("pool", lambda e, U=U, fch=fch: e.tensor_copy(out=carry_f[:, fch, :], in_=U[:, TH:TH + 2]), reads=[BU], writes=[Bcf[fch]])
                        Y, BY = Yr.next()
                        cw = lambda k, fch=fch: ppc("ffn_conv_w", (l * 3 + k) * 44 + fch)
                        op("act", lambda e, U=U, Y=Y, fch=fch: e.activation(out=Y[:], in_=U[:, 2:2 + TH], func=AF.Identity,
                                                                           scale=cw(2), bias=ppc("ffn_conv_b", l * 44 + fch)),
                           reads=[BU, Bpp], writes=[BY])
                        op("dve", lambda e, U=U, Y=Y: e.scalar_tensor_tensor(out=Y[:], in0=U[:, 1:1 + TH], scalar=cw(1), in1=Y[:],
                                                                             op0=ALU.mult, op1=ALU.add),
                           reads=[BU, BY, Bpp], writes=[BY])
                        if gv == 0:
                            op("dve", lambda e, U=U, Y=Y: e.scalar_tensor_tensor(out=Y[:], in0=U[:, 0:TH], scalar=cw(0), in1=Y[:],
                                                                                 op0=ALU.mult, op1=ALU.add),
                               reads=[BU, BY, Bpp], writes=[BY])
                            y0 = (Y, BY)
                            if pend is not None:
                                flush(pend)
                                pend = None
                        else:
                            Yb, BYb = Ybr.next()
                            op("dve", lambda e, U=U, Y=Y, Yb=Yb: e.scalar_tensor_tensor(out=Yb[:], in0=U[:, 0:TH], scalar=cw(0), in1=Y[:],
                                                                                       op0=ALU.mult, op1=ALU.add),
                               reads=[BU, BY, Bpp], writes=[BYb])
                            pend = (j, y0[0], y0[1], Yb, BYb)
                flush(pend)
                for dc in range(NC8):
                    wdA, BwdA = load_wp("dA", wdown_d[l], 0, dc, 11, (0, 11))
                    wdB, BwdB = load_wp("dB", wdown_d[l], 0, dc, 11, (11, 22))
                    if dc % 2 == 0:
                        if dc + 2 < NC8:
                            prefetch_wp("dA", wdown_d[l], 0, dc + 2, 11, (0, 11))
                            prefetch_wp("dB", wdown_d[l], 0, dc + 2, 11, (11, 22))
                        elif hf + 1 < NH:
                            for gv in range(2):
                                prefetch_wp("u%d" % gv, wup_d[l], gv * FCH, 0, 8)
                    for ti, tt in enumerate(tts):
                        bank = next_bank()
                        matmul_group(bank, slice(0, 512),
                                     [((wdA[:, k, :] if k < 11 else wdB[:, k - 11, :]), aT[:, k, ti * 512:(ti + 1) * 512]) for k in range(FCH)],
                                     [BwdA, BwdB] + Ba)
                        resid_add(bank, dc, tt)
            S_.barrier()

    def ple(l, sq_i):
        with contextlib.ExitStack() as st:
            work = norm_work(st)
            pT = Ring([sb("pT%d" % i, [128, 2, 512], BF16, st) for i in range(2)])
            Sg = Ring([sb("Sg%d" % i, [128, 512], F32, st) for i in range(2)])
            hn, HN = hn_alloc(st, 2)
            pst = [sb("pst%d" % i, [128, PLE], F32, st) for i in range(4)]
            pstB = [Buf() for _ in range(4)]
            S_.sync_from("sp")
            bank_ring[0] = list(range(8))
            for tt in range(NTT):
                tl = tt % 2
                norm_tile(tt, "norm_ple", l, work, (hn, HN, tl))
                for q in range(4):
                    S_.dma("sp", pst[q][:], p_d[l, sq_i, tt * 512 + q * 128: tt * 512 + (q + 1) * 128, :], pstS[q], writes=[pstB[q]])
                pt, Bpt = pT.next()
                for ec in range(2):
                    bank = next_bank()
                    for q in range(4):
                        op("pe", lambda e, q=q, ec=ec, bank=bank: e.transpose(out=psum[bank][:, q * 128:(q + 1) * 128],
                                                                             in_=pst[q][:, ec * 128:(ec + 1) * 128], identity=ident),
                           reads=[pstB[q], Bcst], writes=[PB[bank]])
                    op("act", lambda e, pt=pt, ec=ec, bank=bank: e.activation(out=pt[:, ec, :], in_=psum[bank][:, :], func=AF.Identity),
                       reads=[PB[bank]], writes=[Bpt])
                for dc in range(NC8):
                    wg, Bwg = load_wp("pg", wpg_d[l], 0, dc, 8)
                    wp, Bwp = load_wp("pp", wpp_d[l], 0, dc, 2)
                    if dc % 2 == 0:
                        ndc = dc + 2 if dc + 2 < NC8 else (0 if tt + 1 < NTT else None)
                        if ndc is not None:
                            prefetch_wp("pg", wpg_d[l], 0, ndc, 8)
                            prefetch_wp("pp", wpp_d[l], 0, ndc, 2)
                    bg = next_bank()
                    matmul_group(bg, slice(0, 512), [(wg[:, k, :], hn[:, k, ts(tl)]) for k in range(8)],
                                 [Bwg] + [HN[k][tl] for k in range(8)])
                    bp = next_bank()
                    matmul_group(bp, slice(0, 512), [(wp[:, k, :], pt[:, k, :]) for k in range(2)], [Bwp, Bpt])
                    sg, Bsg = Sg.next()
                    op("act", lambda e, sg=sg, bg=bg: e.activation(out=sg[:], in_=psum[bg][:, :], func=AF.Sigmoid), reads=[PB[bg]], writes=[Bsg])
                    op("dve", lambda e, sg=sg, bp=bp: e.tensor_tensor(out=sg[:], in0=sg[:], in1=psum[bp][:, :], op=ALU.mult),
                       reads=[Bsg, PB[bp]], writes=[Bsg])
                    op("dve", lambda e, sg=sg, dc=dc: e.tensor_tensor(out=xT[:, dc, ts(tt)], in0=xT[:, dc, ts(tt)], in1=sg[:], op=ALU.add),
                       reads=[Bsg, X[dc][tt]], writes=[X[dc][tt]])
            S_.barrier()

    for sq_i in range(NSEQ):
        load_x(sq_i)
        for l in range(DEPTH):
            if l % 2 == 0:
                attention(l)
            else:
                rglru(l)
            ffn(l)
            ple(l, sq_i)
        final_store(sq_i)
    es.close()
    return nc


def prep_shared(inp, depth):
    pp = PP(depth)
    nr = max(depth // 2, 1)
    tab = np.zeros((128, pp.n), np.float32)

    def put(name, arr):
        tab[:, pp.off[name]:pp.off[name] + arr.shape[1]] = arr

    put("norm_mix", _cols(inp["norm_mix"], 8))
    put("norm_ffn", _cols(inp["norm_ffn"], 8))
    put("norm_ple", _cols(inp["norm_ple"], 8))
    put("norm_final", _cols(inp["norm_final"], 8))
    put("ffn_conv_w", _cols(inp["ffn_conv_w"], 44))
    put("ffn_conv_b", _cols(inp["ffn_conv_b"], 44))
    put("rnn_conv_w", _cols(inp["rnn_conv_w"], 10))
    put("rnn_conv_b", _cols(inp["rnn_conv_b"], 10))
    put("b_gate_a", _cols(inp["rnn_b_gate_a"], 10))
    put("b_gate_x", _cols(inp["rnn_b_gate_x"], 10))
    put("lru", _cols(inp["rnn_lru_param"], 10))
    shared = {
        "pp": tab,
        "cst": make_consts(),
        "wband_a": _band(np.asarray(inp["rnn_w_gate_a"], np.float32)),
        "wband_x": _band(np.asarray(inp["rnn_w_gate_x"], np.float32)),
    }
    for k in ("attn_w_qkv", "attn_w_o", "rnn_w_in", "rnn_w_out", "ffn_w_up", "ffn_w_down", "ple_w_gate", "ple_w_proj"):
        shared[k] = np.ascontiguousarray(np.asarray(inp[k], np.float32))
    return shared


def run(inp, n_cores, depth):
    x = np.asarray(inp["x"], np.float32)
    p = np.asarray(inp["p"], np.float32)
    B, S, _ = x.shape
    assert B % n_cores == 0
    nseq = B // n_cores
    shared = prep_shared(inp, depth)
    nc = build(nseq, S, depth)
    in_maps = []
    for i in range(n_cores):
        m = dict(shared)
        m["x"] = np.ascontiguousarray(x[i * nseq:(i + 1) * nseq])
        m["p"] = np.ascontiguousarray(p[:, i * nseq:(i + 1) * nseq])
        in_maps.append(m)
    res = run_bass_kernel_spmd(nc, in_maps, core_ids=list(range(n_cores)))
    return np.concatenate([np.asarray(r["out"], np.float32) for r in res.results], axis=0)


def kernel(**inputs):
    return run(inputs, N_CORES, 4)
```

```python
import contextlib
import numpy as np
import concourse.bass as bass
import concourse.mybir as mybir
from concourse.bass_utils import run_bass_kernel_spmd

F32 = mybir.dt.float32
BF16 = mybir.dt.bfloat16
AF = mybir.ActivationFunctionType
ALU = mybir.AluOpType

D = 1024
NC8 = 8
HEADS = 16
DH = 64
RW = 1280
RCH = 10
RB = 80
FF = 2816
FCH = 22
PLE = 256
EPS = 1e-6
LRU_C = 8.0
N_CORES = 8


class SemObj:
    def __init__(self, sem, name):
        self.sem = sem
        self.name = name
        self.count = 0


class Buf:
    __slots__ = ("name", "w", "r")

    def __init__(self, name=""):
        self.name = name
        self.w = None
        self.r = {}


class Sched:
    def __init__(self, nc):
        self.nc = nc
        self.eng = {"pe": nc.tensor, "act": nc.scalar, "dve": nc.vector, "pool": nc.gpsimd, "sp": nc.sync}
        self.esem = {k: SemObj(nc.alloc_semaphore("es_" + k), k) for k in self.eng}
        self.waited = {k: {} for k in self.eng}
        self.ndsem = 0
        self.ninst = 0

    def dsem(self):
        self.ndsem += 1
        return SemObj(self.nc.alloc_semaphore("ds%d" % self.ndsem), "ds%d" % self.ndsem)

    def _deps(self, en, reads, writes):
        deps = {}

        def add(tok):
            if tok is None:
                return
            so, v = tok
            if en == "pe" and so is self.esem["pe"]:
                return
            if deps.get(so, 0) < v:
                deps[so] = v

        for b in reads:
            add(b.w)
        for b in writes:
            add(b.w)
            for so, v in b.r.items():
                add((so, v))
        w = self.waited[en]
        e = self.eng[en]
        for so, v in deps.items():
            if w.get(so, 0) >= v:
                continue
            assert so.count >= v, "waiting on un-issued increment %s %d>%d" % (so.name, v, so.count)
            e.wait_ge(so.sem, v)
            self.ninst += 1
            w[so] = v

    def _mark(self, tok, reads, writes):
        so, v = tok
        for b in reads:
            if b.r.get(so, 0) < v:
                b.r[so] = v
        for b in writes:
            b.w = tok
            b.r = {}

    def op(self, en, fn, reads=(), writes=(), inc=True):
        self._deps(en, reads, writes)
        ins = fn(self.eng[en])
        self.ninst += 1
        so = self.esem[en]
        if inc:
            so.count += 1
            ins.then_inc(so.sem, 1)
            tok = (so, so.count)
        else:
            tok = (so, so.count + 1)
        self._mark(tok, reads, writes)
        return ins

    def dma(self, q, out, in_, ds, reads=(), writes=(), **kw):
        self._deps(q, reads, writes)
        ins = self.eng[q].dma_start(out=out, in_=in_, **kw)
        self.ninst += 1
        ds.count += 16
        ins.then_inc(ds.sem, 16)
        self._mark((ds, ds.count), reads, writes)
        return ins

    def wait_all(self, en, bufs):
        self._deps(en, bufs, bufs)

    def sync_from(self, en, others=("pe", "act", "dve")):
        w = self.waited[en]
        for other in others:
            so = self.esem[other]
            if so.count > w.get(so, 0):
                self.eng[en].wait_ge(so.sem, so.count)
                self.ninst += 1
                w[so] = so.count

    def barrier(self, engines=("pe", "act", "dve", "pool")):
        for en in engines:
            w = self.waited[en]
            for other in engines:
                if other == en:
                    continue
                so = self.esem[other]
                if so.count > w.get(so, 0):
                    self.eng[en].wait_ge(so.sem, so.count)
                    self.ninst += 1
                    w[so] = so.count


class Ring:
    def __init__(self, tiles):
        self.tiles = tiles
        self.bufs = [Buf() for _ in tiles]
        self.i = 0

    def next(self):
        k = self.i % len(self.tiles)
        self.i += 1
        return self.tiles[k], self.bufs[k]


def _cols(v, nch):
    lead = int(np.prod(v.shape[:-1])) if v.ndim > 1 else 1
    a = np.asarray(v, np.float32).reshape(lead, nch, 128)
    return np.ascontiguousarray(a.transpose(2, 0, 1).reshape(128, lead * nch))


def band_range(fc):
    b_lo = (128 * fc) // RB
    b_hi = (128 * fc + 127) // RB
    k_lo = (RB * b_lo) // 128
    k_hi = (RB * b_hi + RB - 1) // 128
    return k_lo, min(k_hi, RCH - 1)


def _band(wg):
    L = wg.shape[0]
    dense = np.zeros((L, RW, RW), np.float32)
    for h in range(RW // RB):
        dense[:, h * RB:(h + 1) * RB, h * RB:(h + 1) * RB] = wg[:, h]
    out = np.zeros((L, 128, RCH, 3, 128), np.float32)
    for fc in range(RCH):
        k_lo, k_hi = band_range(fc)
        for kk, kc in enumerate(range(k_lo, k_hi + 1)):
            out[:, :, fc, kk, :] = dense[:, kc * 128:(kc + 1) * 128, fc * 128:(fc + 1) * 128]
    return out


class PP:
    def __init__(self, depth):
        self.off = {}
        self.n = 0
        nr = max(depth // 2, 1)
        for name, cnt in [("norm_mix", depth * 8), ("norm_ffn", depth * 8), ("norm_ple", depth * 8), ("norm_final", 8),
                          ("ffn_conv_w", depth * 3 * 44), ("ffn_conv_b", depth * 44),
                          ("rnn_conv_w", nr * 4 * 10), ("rnn_conv_b", nr * 10), ("b_gate_a", nr * 10),
                          ("b_gate_x", nr * 10), ("lru", nr * 10), ("lruc", nr * 10)]:
            self.off[name] = self.n
            self.n += cnt


def make_consts():
    j = np.arange(128)[:, None]
    s = np.arange(128)[None, :]
    ident = (j == s).astype(np.float32)
    negtri = -(j >= s).astype(np.float32)
    negones = -np.ones((128, 128), np.float32)
    masklt = (j < s).astype(np.float32)
    ones = np.ones((128, 128), np.float32)
    return np.ascontiguousarray(np.concatenate([ident, negtri, negones, masklt, ones], axis=1))


def build(NSEQ, S, DEPTH):
    assert S % 512 == 0
    NTT = S // 512
    NQB = S // 128
    TH = min(S, 1024)
    NH = S // TH
    NTH = TH // 512
    NATT = (DEPTH + 1) // 2
    NRNN = max(DEPTH // 2, 1)
    pp = PP(DEPTH)

    nc = bass.Bass("TRN2", target_bir_lowering=False)

    def dram(name, shape, kind="ExternalInput", dt=F32):
        return nc.dram_tensor(name, list(shape), dt, kind=kind).ap()

    x_d = dram("x", [NSEQ, S, D])
    p_d = dram("p", [DEPTH, NSEQ, S, PLE])
    out_d = dram("out", [NSEQ, S, D], kind="ExternalOutput")
    wqkv_d = dram("attn_w_qkv", [NATT, D, 3 * D])
    wo_d = dram("attn_w_o", [NATT, D, D])
    win_d = dram("rnn_w_in", [NRNN, D, 2 * RW])
    wout_d = dram("rnn_w_out", [NRNN, RW, D])
    wup_d = dram("ffn_w_up", [DEPTH, D, 2 * FF])
    wdown_d = dram("ffn_w_down", [DEPTH, FF, D])
    wpg_d = dram("ple_w_gate", [DEPTH, D, D])
    wpp_d = dram("ple_w_proj", [DEPTH, PLE, D])
    wba_d = dram("wband_a", [NRNN, 128, RCH, 3, 128])
    wbx_d = dram("wband_x", [NRNN, 128, RCH, 3, 128])
    pp_d = dram("pp", [128, pp.n])
    cst_d = dram("cst", [128, 5 * 128])

    S_ = Sched(nc)
    op = S_.op
    es = contextlib.ExitStack()

    uid = [0]

    def sb(name, shape, dt, stack=None):
        uid[0] += 1
        return (stack or es).enter_context(nc.sbuf_tensor("%s_u%d" % (name, uid[0]), list(shape), dt))

    xT = sb("xT", [128, NC8, S], F32)
    ppt = sb("ppt", [128, pp.n], F32)
    cst32 = sb("cst32", [128, 5 * 128], F32)
    cstb = sb("cstb", [128, 5 * 128], BF16)
    ident = cst32[:, 0:128]
    negtri = cstb[:, 128:256]
    negones = cstb[:, 256:384]
    masklt = cstb[:, 384:512]
    onesb = cstb[:, 512:640]
    NW = 4
    WSZ = FCH * 128
    wslots = [sb("w%d" % i, [128, WSZ], BF16) for i in range(NW)]
    wbufs = [Buf("w%d" % i) for i in range(NW)]
    wsems = [S_.dsem() for _ in range(NW)]
    wcnt = [0]
    stgS = [S_.dsem() for _ in range(4)]
    pstS = [S_.dsem() for _ in range(4)]
    carry_f = sb("carry_f", [128, 2 * FCH, 2], F32)
    carry_r = sb("carry_r", [128, RCH, 3], F32)
    carry_h = sb("carry_h", [128, RCH], F32)
    Bcf = [Buf() for _ in range(2 * FCH)]
    Bcr = [Buf() for _ in range(RCH)]
    Bch = [Buf() for _ in range(RCH)]

    psum = [es.enter_context(nc.psum_tensor("ps%d" % i, [128, 512], F32)) for i in range(8)]
    PB = [Buf("ps%d" % i) for i in range(8)]
    bank_ring = [list(range(8))]
    bank_i = [0]

    def next_bank():
        r = bank_ring[0]
        b = r[bank_i[0] % len(r)]
        bank_i[0] += 1
        return b

    X = [[Buf() for _ in range(NTT)] for _ in range(NC8)]
    Bpp = Buf("pp")
    Bcst = Buf("cst")

    def ts(tt):
        return slice(tt * 512, (tt + 1) * 512)

    def ppc(name, idx):
        o = pp.off[name] + idx
        return ppt[:, o:o + 1]

    d0 = S_.dsem()
    S_.dma("sp", ppt[:], pp_d[:, :], d0, writes=[Bpp])
    d1 = S_.dsem()
    S_.dma("sp", cst32[:], cst_d[:, :], d1, writes=[Bcst])
    op("dve", lambda e: e.tensor_copy(out=cstb[:], in_=cst32[:]), reads=[Bcst], writes=[Bcst])
    nl = NRNN * RCH
    lo, lc = pp.off["lru"], pp.off["lruc"]
    op("act", lambda e: e.activation(out=ppt[:, lc:lc + nl], in_=ppt[:, lo:lo + nl], func=AF.Exp, scale=-1.0),
       reads=[Bpp], writes=[Bpp])
    op("act", lambda e: e.activation(out=ppt[:, lc:lc + nl], in_=ppt[:, lc:lc + nl], func=AF.Ln, bias=1.0),
       reads=[Bpp], writes=[Bpp])
    op("dve", lambda e: e.tensor_scalar(out=ppt[:, lc:lc + nl], in0=ppt[:, lc:lc + nl], scalar1=-LRU_C, scalar2=None,
                                        op0=ALU.mult), reads=[Bpp], writes=[Bpp])

    def load_w(view, K, n=128):
        i = wcnt[0] % NW
        wcnt[0] += 1
        t = wslots[i][:, 0:K * n].rearrange("p (k n) -> p k n", k=K)
        S_.dma("pool", t, view, wsems[i], writes=[wbufs[i]])
        return t, wbufs[i]

    pair_cache = {}

    def load_wp(name, w_ap, col0_chunk, idx, K, krange=None):
        if idx % 2 == 0:
            key = (name, col0_chunk + idx)
            if key in stash:
                pair_cache[name] = stash.pop(key)
            else:
                pair_cache[name] = slab_load(w_ap, col0_chunk + idx, K, krange)
        t, B = pair_cache[name]
        return t[:, :, (idx % 2) * 128:(idx % 2 + 1) * 128], B

    stash = {}

    def slab_load(w_ap, chunk, K, krange):
        f0 = chunk * 128
        v = w_ap.rearrange("(k p) f -> p k f", p=128)
        v = v[:, krange[0]:krange[1], f0:f0 + 256] if krange else v[:, :, f0:f0 + 256]
        return load_w(v, K, 256)

    def prefetch_wp(name, w_ap, col0_chunk, idx, K, krange=None):
        key = (name, col0_chunk + idx)
        if key not in stash:
            stash[key] = slab_load(w_ap, col0_chunk + idx, K, krange)

    def wview(w_ap, f0, n=128):
        return w_ap.rearrange("(k p) f -> p k f", p=128)[:, :, f0:f0 + n]

    def matmul_group(bank, cols, pairs, reads, inc_last=True):
        n = len(pairs)
        for k, (l, r) in enumerate(pairs):
            op("pe", lambda e, l=l, r=r, k=k: e.matmul(psum[bank][:, cols], lhsT=l, rhs=r, start=(k == 0), stop=(k == n - 1)),
               reads=reads, writes=[PB[bank]], inc=(inc_last and k == n - 1))

    def norm_tile(tt, gname, gidx, work, dst=None, out_f32=None):
        sq_ring, rs_ring = work
        if dst is not None:
            hn, HN, tl = dst
        bank = next_bank()
        for c in range(NC8):
            sq, Bsq = sq_ring.next()
            op("act", lambda e, c=c, sq=sq: e.activation(out=sq[:], in_=xT[:, c, ts(tt)], func=AF.Square),
               reads=[X[c][tt]], writes=[Bsq])
            op("pe", lambda e, c=c, sq=sq: e.matmul(psum[bank][:, :], lhsT=onesb, rhs=sq[:], start=(c == 0), stop=(c == NC8 - 1)),
               reads=[Bsq, Bcst], writes=[PB[bank]])
        rs, Brs = rs_ring.next()
        op("act", lambda e: e.activation(out=rs[:], in_=psum[bank][:, :], func=AF.Sqrt, scale=1.0 / D, bias=EPS),
           reads=[PB[bank]], writes=[Brs])
        op("dve", lambda e: e.reciprocal(out=rs[:], in_=rs[:]), reads=[Brs], writes=[Brs])
        for c in range(NC8):
            o = hn[:, c, ts(tl)] if out_f32 is None else out_f32[0][:, c, :]
            wb = [HN[c][tl]] if out_f32 is None else [out_f32[1]]
            op("dve", lambda e, c=c, o=o: e.scalar_tensor_tensor(out=o, in0=xT[:, c, ts(tt)], scalar=ppc(gname, gidx * 8 + c),
                                                                 in1=rs[:], op0=ALU.mult, op1=ALU.mult),
               reads=[X[c][tt], Brs, Bpp], writes=wb)

    def resid_add(bank, dc, tt):
        op("dve", lambda e: e.tensor_tensor(out=xT[:, dc, ts(tt)], in0=xT[:, dc, ts(tt)], in1=psum[bank][:, :], op=ALU.add),
           reads=[PB[bank], X[dc][tt]], writes=[X[dc][tt]])

    def hn_alloc(stack, ntl):
        t = sb("hn", [128, NC8, ntl * 512], BF16, stack)
        return t, [[Buf() for _ in range(ntl)] for _ in range(NC8)]

    def norm_work(stack):
        sq_ring = Ring([sb("sq%d" % i, [128, 512], BF16, stack) for i in range(2)])
        rs_ring = Ring([sb("rs%d" % i, [128, 512], F32, stack) for i in range(2)])
        return sq_ring, rs_ring

    def load_x(sq_i):
        bank_ring[0] = list(range(8))
        st = contextlib.ExitStack()
        stg = [sb("stg%d" % i, [128, D], F32, st) for i in range(4)]
        stgB = [Buf() for _ in range(4)]
        S_.sync_from("sp")
        for tt in range(NTT):
            for q in range(4):
                S_.dma("sp", stg[q][:], x_d[sq_i, tt * 512 + q * 128: tt * 512 + (q + 1) * 128, :], stgS[q], writes=[stgB[q]])
            for c in range(NC8):
                bank = next_bank()
                for q in range(4):
                    op("pe", lambda e, q=q, c=c: e.transpose(out=psum[bank][:, q * 128:(q + 1) * 128],
                                                             in_=stg[q][:, c * 128:(c + 1) * 128], identity=ident),
                       reads=[stgB[q], Bcst], writes=[PB[bank]])
                op("act", lambda e, c=c: e.activation(out=xT[:, c, ts(tt)], in_=psum[bank][:, :], func=AF.Identity),
                   reads=[PB[bank]], writes=[X[c][tt]])
        S_.barrier()
        st.close()

    def final_store(sq_i):
        with contextlib.ExitStack() as st:
            work = norm_work(st)
            yf = sb("yf", [128, NC8, 512], F32, st)
            Byf = Buf()
            stg = [sb("stg%d" % i, [128, D], F32, st) for i in range(4)]
            stgB = [Buf() for _ in range(4)]
            bank_ring[0] = list(range(8))
            for tt in range(NTT):
                norm_tile(tt, "norm_final", 0, work, out_f32=(yf, Byf))
                for q in range(4):
                    b0, b1 = next_bank(), next_bank()
                    for c in range(NC8):
                        bank = b0 if c < 4 else b1
                        op("pe", lambda e, c=c, bank=bank: e.transpose(out=psum[bank][:, (c % 4) * 128:(c % 4 + 1) * 128],
                                                                      in_=yf[:, c, q * 128:(q + 1) * 128], identity=ident),
                           reads=[Byf, Bcst], writes=[PB[bank]])
                    op("act", lambda e: e.activation(out=stg[q][:, 0:512], in_=psum[b0][:, :], func=AF.Identity),
                       reads=[PB[b0]], writes=[stgB[q]])
                    op("dve", lambda e: e.tensor_copy(out=stg[q][:, 512:1024], in_=psum[b1][:, :]),
                       reads=[PB[b1]], writes=[stgB[q]])
                    S_.dma("sp", out_d[sq_i, tt * 512 + q * 128: tt * 512 + (q + 1) * 128, :], stg[q][:], stgS[q], reads=[stgB[q]])
            for en in ("sp", "pe", "act", "dve"):
                S_.wait_all(en, stgB)
            S_.barrier()

    def attention(l):
        slot = l // 2
        with contextlib.ExitStack() as st:
            OT = sb("OT", [128, NC8, S], BF16, st)
            BO = [[Buf() for _ in range(NTT)] for _ in range(NC8)]
            QTs = [sb("QT0", [128, S], BF16, st)]
            KTs = [sb("KT0", [128, S], BF16, st)]
            Vbs = [sb("Vb0", [128, NQB, 128], BF16, st)]
            BQs, BKs, BVs = [Buf(), Buf()], [Buf(), Buf()], [Buf(), Buf()]
            NSTR = 4
            Er = [Ring([sb("E%d_%d" % (s, i), [128, 512], BF16, st) for i in range(1)]) for s in range(NSTR)]
            Lr = [Ring([sb("L%d_%d" % (s, i), [128, 512], BF16, st) for i in range(2)]) for s in range(NSTR)]
            Wr = [Ring([sb("W%d_%d" % (s, i), [128, 512], BF16, st) for i in range(2)]) for s in range(NSTR)]
            Lsuf = [sb("Ls%d" % s, [128, 512], BF16, st) for s in range(NSTR)]
            BLs = [Buf() for _ in range(NSTR)]

            hn, HN = hn_alloc(st, NTT)
            bank_ring[0] = list(range(8))
            st2 = contextlib.ExitStack()
            work = norm_work(st2)
            for tt in range(NTT):
                norm_tile(tt, "norm_mix", l, work, (hn, HN, tt))
            S_.barrier()
            st2.close()
            QTs.append(sb("QT1", [128, S], BF16, st))
            KTs.append(sb("KT1", [128, S], BF16, st))
            Vbs.append(sb("Vb1", [128, NQB, 128], BF16, st))
            bank_ring[0] = [6, 7]
            if NTT == 4:
                pairs = [[0, 3], [1, 2]]
            elif NTT == 2:
                pairs = [[0], [1]]
            else:
                pairs = [list(range(NTT))]
            def proj_units(c):
                cb = c % 2
                QT, KT, Vb, BQ, BK, BV = QTs[cb], KTs[cb], Vbs[cb], BQs[cb], BKs[cb], BVs[cb]
                wq, Bwq = load_wp("q", wqkv_d[slot], 0, c, 8)
                wk, Bwk = load_wp("k", wqkv_d[slot], 8, c, 8)
                wv, Bwv = load_wp("v", wqkv_d[slot], 16, c, 8)
                for tt in range(NTT):
                    bank = next_bank()
                    matmul_group(bank, slice(0, 512), [(wq[:, k, :], hn[:, k, ts(tt)]) for k in range(8)],
                                 [Bwq] + [HN[k][tt] for k in range(8)])
                    op("dve", lambda e: e.tensor_scalar(out=QT[:, ts(tt)], in0=psum[bank][:, :], scalar1=DH ** -0.5, scalar2=None,
                                                        op0=ALU.mult), reads=[PB[bank]], writes=[BQ])
                    yield
                    bank = next_bank()
                    matmul_group(bank, slice(0, 512), [(wk[:, k, :], hn[:, k, ts(tt)]) for k in range(8)],
                                 [Bwk] + [HN[k][tt] for k in range(8)])
                    op("dve", lambda e: e.tensor_copy(out=KT[:, ts(tt)], in_=psum[bank][:, :]), reads=[PB[bank]], writes=[BK])
                    yield
                for tt in range(NTT):
                    bank = next_bank()
                    for q in range(4):
                        tb = tt * 4 + q
                        matmul_group(bank, slice(q * 128, (q + 1) * 128),
                                     [(hn[:, k, tb * 128:(tb + 1) * 128], wv[:, k, :]) for k in range(8)],
                                     [Bwv] + [HN[k][tt] for k in range(8)])
                    op("dve", lambda e: e.tensor_copy(out=Vb[:, tt * 4:(tt + 1) * 4, :],
                                                      in_=psum[bank][:, :].rearrange("p (q n) -> p q n", q=4)),
                       reads=[PB[bank]], writes=[BV])
                    yield

            for _ in proj_units(0):
                pass
            for c in range(NC8):
                cb = c % 2
                QT, KT, Vb, BQ, BK, BV = QTs[cb], KTs[cb], Vbs[cb], BQs[cb], BKs[cb], BVs[cb]
                if c + 1 < NC8 and (c + 1) % 2 == 0:
                    prefetch_wp("q", wqkv_d[slot], 0, c + 1, 8)
                    prefetch_wp("k", wqkv_d[slot], 8, c + 1, 8)
                    prefetch_wp("v", wqkv_d[slot], 16, c + 1, 8)
                nxt = proj_units(c + 1) if c + 1 < NC8 else iter(())
                steps = []
                for ps_i, tcs in enumerate(pairs):
                    lst = []
                    for tc in tcs:
                        kbs = list(range(4 * tc + 3, -1, -1))
                        for idx, kb in enumerate(kbs):
                            lst.append((tc, kb, idx, idx == len(kbs) - 1))
                    steps.append(lst)
                nround = max(len(s) for s in steps)
                acts, infos = [], []
                for r in range(nround):
                    act = [(ps_i, hh) for ps_i in range(len(pairs)) if r < len(steps[ps_i]) for hh in range(2)]
                    info = {}
                    for (ps_i, hh) in act:
                        tc, kb, idx, last = steps[ps_i][r]
                        sid = ps_i * 2 + hh
                        off = max(0, kb - 4 * tc)
                        c0 = off * 128
                        diag = kb >= 4 * tc
                        info[(ps_i, hh)] = (tc, kb, idx, last, sid, c0, diag)
                    acts.append(act)
                    infos.append(info)

                def emit_z(r):
                    for key in acts[r]:
                        tc, kb, idx, last, sid, c0, diag = infos[r][key]
                        hh = key[1]
                        hp = slice(64 * hh, 64 * hh + 64)
                        if idx == 0:
                            op("dve", lambda e, sid=sid: e.memset(Lsuf[sid][:], 0.0), writes=[BLs[sid]])
                        op("pe", lambda e, sid=sid, hp=hp, kb=kb, tc=tc, c0=c0: e.matmul(
                            psum[sid][:, c0:512], lhsT=KT[hp, kb * 128:(kb + 1) * 128], rhs=QT[hp, tc * 512 + c0:(tc + 1) * 512],
                            start=True, stop=False, skip_group_check=True), reads=[BK, BQ], writes=[PB[sid]])

                emit_z(0)
                for r in range(nround):
                    act, info = acts[r], infos[r]
                    tiles = {}
                    for key in act:
                        tc, kb, idx, last, sid, c0, diag = info[key]
                        Et, BE = Er[sid].next()
                        op("act", lambda e, Et=Et, sid=sid, c0=c0: e.activation(out=Et[:, c0:512], in_=psum[sid][:, c0:512], func=AF.Exp),
                           reads=[PB[sid]], writes=[BE])
                        tiles[key] = [Et, BE]
                    for key in act:
                        tc, kb, idx, last, sid, c0, diag = info[key]
                        Et, BE = tiles[key]
                        Lt, BL = Lr[sid].next()
                        op("act", lambda e, Et=Et, Lt=Lt, c0=c0: e.activation(out=Lt[:, c0:512], in_=Et[:, c0:512], func=AF.Ln, bias=1.0),
                           reads=[BE], writes=[BL])
                        if diag:
                            op("dve", lambda e, Lt=Lt, c0=c0: e.tensor_tensor(out=Lt[:, c0:c0 + 128], in0=Lt[:, c0:c0 + 128], in1=masklt, op=ALU.mult),
                               reads=[BL, Bcst], writes=[BL])
                        tiles[key] += [Lt, BL]
                    for key in act:
                        tc, kb, idx, last, sid, c0, diag = info[key]
                        Et, BE, Lt, BL = tiles[key]
                        op("pe", lambda e, sid=sid, Lt=Lt, c0=c0, idx=idx: e.matmul(
                            psum[sid][:, c0:512], lhsT=negtri, rhs=Lt[:, c0:512], start=False, stop=(idx == 0), skip_group_check=True),
                           reads=[BL, Bcst], writes=[PB[sid]])
                        if idx > 0:
                            op("pe", lambda e, sid=sid, c0=c0: e.matmul(
                                psum[sid][:, c0:512], lhsT=negones, rhs=Lsuf[sid][:, c0:512], start=False, stop=True, skip_group_check=True),
                               reads=[BLs[sid], Bcst], writes=[PB[sid]])
                    for key in act:
                        tc, kb, idx, last, sid, c0, diag = info[key]
                        Wt, BW = Wr[sid].next()
                        op("act", lambda e, Wt=Wt, sid=sid, c0=c0: e.activation(out=Wt[:, c0:512], in_=psum[sid][:, c0:512], func=AF.Exp),
                           reads=[PB[sid]], writes=[BW])
                        if diag:
                            op("dve", lambda e, Wt=Wt, c0=c0: e.tensor_tensor(out=Wt[:, c0:c0 + 128], in0=Wt[:, c0:c0 + 128], in1=masklt, op=ALU.mult),
                               reads=[BW, Bcst], writes=[BW])
                        tiles[key] += [Wt, BW]
                    if r + 1 < nround:
                        emit_z(r + 1)
                    for key in act:
                        tc, kb, idx, last, sid, c0, diag = info[key]
                        ps_i, hh = key
                        Et, BE, Lt, BL, Wt, BW = tiles[key]
                        pob = 4 + ps_i
                        op("pe", lambda e, pob=pob, hh=hh, kb=kb, Wt=Wt, c0=c0, idx=idx, last=last: e.matmul(
                            psum[pob][64 * hh:64 * hh + 64, c0:512], lhsT=Vb[:, kb, 64 * hh:64 * hh + 64], rhs=Wt[:, c0:512],
                            start=(idx == 0), stop=last, skip_group_check=True), reads=[BW, BV], writes=[PB[pob]])
                        if not last:
                            op("dve", lambda e, sid=sid, Lt=Lt, c0=c0: e.tensor_tensor(out=Lsuf[sid][:, c0:512], in0=Lsuf[sid][:, c0:512],
                                                                                      in1=Lt[:, c0:512], op=ALU.add),
                               reads=[BL, BLs[sid]], writes=[BLs[sid]])
                    if r >= 1:
                        next(nxt, None)
                    for ps_i in range(len(pairs)):
                        if r < len(steps[ps_i]) and steps[ps_i][r][3]:
                            tc = steps[ps_i][r][0]
                            pob = 4 + ps_i
                            op("dve", lambda e, pob=pob, tc=tc: e.tensor_copy(out=OT[:, c, ts(tc)], in_=psum[pob][:, :]),
                               reads=[PB[pob]], writes=[BO[c][tc]])
                for _ in nxt:
                    pass
            bank_ring[0] = list(range(8))
            for dc in range(NC8):
                wo, Bwo = load_wp("o", wo_d[slot], 0, dc, 8)
                if dc % 2 == 0 and dc + 2 < NC8:
                    prefetch_wp("o", wo_d[slot], 0, dc + 2, 8)
                for tt in range(NTT):
                    bank = next_bank()
                    matmul_group(bank, slice(0, 512), [(wo[:, k, :], OT[:, k, ts(tt)]) for k in range(8)],
                                 [Bwo] + [BO[k][tt] for k in range(8)])
                    resid_add(bank, dc, tt)
            S_.barrier()

    def rglru(l):
        slot = l // 2
        with contextlib.ExitStack() as st:
            work = norm_work(st)
            xr = sb("xr", [128, RCH, TH], BF16, st)
            G = sb("G", [128, RCH, TH], BF16, st)
            Bxr = [Buf() for _ in range(RCH)]
            BG = [Buf() for _ in range(RCH)]
            Ur = Ring([sb("Ur%d" % i, [128, TH + 3], F32, st) for i in range(2)])
            Yr = Ring([sb("Yr%d" % i, [128, TH], F32, st) for i in range(2)])
            T1r = Ring([sb("T1_%d" % i, [128, TH], F32, st) for i in range(2)])
            T2r = Ring([sb("T2_%d" % i, [128, TH], F32, st) for i in range(2)])
            T3r = Ring([sb("T3_%d" % i, [128, TH], F32, st) for i in range(2)])
            hn, HN = hn_alloc(st, NTH)
            bank_ring[0] = list(range(8))
            for hf in range(NH):
                h0 = hf * TH
                tts = list(range(hf * NTH, (hf + 1) * NTH))
                for ti, tt in enumerate(tts):
                    norm_tile(tt, "norm_mix", l, work, (hn, HN, ti))
                for j in range(RCH):
                    wr, Bwr = load_wp("r", win_d[slot], RCH, j, 8)
                    wg, Bwg = load_wp("g", win_d[slot], 0, j, 8)
                    if j % 2 == 0 and j + 2 < RCH:
                        prefetch_wp("r", win_d[slot], RCH, j + 2, 8)
                        prefetch_wp("g", win_d[slot], 0, j + 2, 8)
                    U, BU = Ur.next()
                    if hf == 0:
                        op("pool", lambda e, U=U: e.memset(U[:, 0:3], 0.0), writes=[BU])
                    else:
                        op("pool", lambda e, U=U, j=j: e.tensor_copy(out=U[:, 0:3], in_=carry_r[:, j, :]), reads=[Bcr[j]], writes=[BU])
                    for ti, tt in enumerate(tts):
                        bank = next_bank()
                        matmul_group(bank, slice(0, 512), [(wr[:, k, :], hn[:, k, ts(ti)]) for k in range(8)],
                                     [Bwr] + [HN[k][ti] for k in range(8)])
                        op("act", lambda e, U=U, ti=ti: e.activation(out=U[:, 3 + ti * 512:3 + (ti + 1) * 512], in_=psum[bank][:, :], func=AF.Identity),
                           reads=[PB[bank]], writes=[BU])
                    if hf + 1 < NH:
                        op("pool", lambda e, U=U, j=j: e.tensor_copy(out=carry_r[:, j, :], in_=U[:, TH:TH + 3]), reads=[BU], writes=[Bcr[j]])
                    Y, BY = Yr.next()
                    cw = lambda k, j=j: ppc("rnn_conv_w", (slot * 4 + k) * RCH + j)
                    op("act", lambda e, U=U, Y=Y, j=j: e.activation(out=Y[:], in_=U[:, 3:3 + TH], func=AF.Identity,
                                                                   scale=cw(3), bias=ppc("rnn_conv_b", slot * RCH + j)),
                       reads=[BU, Bpp], writes=[BY])
                    for k in (2, 1):
                        op("dve", lambda e, U=U, Y=Y, k=k: e.scalar_tensor_tensor(out=Y[:], in0=U[:, k:k + TH], scalar=cw(k), in1=Y[:],
                                                                                  op0=ALU.mult, op1=ALU.add),
                           reads=[BU, BY, Bpp], writes=[BY])
                    op("dve", lambda e, U=U, Y=Y, j=j: e.scalar_tensor_tensor(out=xr[:, j, :], in0=U[:, 0:TH], scalar=cw(0), in1=Y[:],
                                                                              op0=ALU.mult, op1=ALU.add),
                       reads=[BU, BY, Bpp], writes=[Bxr[j]])
                    for ti, tt in enumerate(tts):
                        bank = next_bank()
                        matmul_group(bank, slice(0, 512), [(wg[:, k, :], hn[:, k, ts(ti)]) for k in range(8)],
                                     [Bwg] + [HN[k][ti] for k in range(8)])
                        op("act", lambda e, j=j, ti=ti: e.activation(out=G[:, j, ti * 512:(ti + 1) * 512], in_=psum[bank][:, :], func=AF.Gelu_apprx_tanh),
                           reads=[PB[bank]], writes=[BG[j]])
                for fc in range(RCH):
                    k_lo, k_hi = band_range(fc)
                    nk = k_hi - k_lo + 1
                    wa, Bwa = load_w(wba_d[slot, :, fc, 0:nk, :], nk)
                    wx, Bwx = load_w(wbx_d[slot, :, fc, 0:nk, :], nk)
                    T1, B1 = T1r.next()
                    T2, B2 = T2r.next()
                    T3, B3 = T3r.next()
                    for (wt_, Bw_, T_, B_, bn) in ((wa, Bwa, T1, B1, "b_gate_a"), (wx, Bwx, T2, B2, "b_gate_x")):
                        for ti in range(NTH):
                            bank = next_bank()
                            matmul_group(bank, slice(0, 512),
                                         [(wt_[:, kk, :], xr[:, k_lo + kk, ti * 512:(ti + 1) * 512]) for kk in range(nk)],
                                         [Bw_] + [Bxr[k_lo + kk] for kk in range(nk)])
                            op("act", lambda e, T_=T_, ti=ti, bn=bn, bank=bank: e.activation(
                                out=T_[:, ti * 512:(ti + 1) * 512], in_=psum[bank][:, :], func=AF.Sigmoid, bias=ppc(bn, slot * RCH + fc)),
                               reads=[PB[bank], Bpp], writes=[B_])
                    op("act", lambda e, T1=T1: e.activation(out=T1[:], in_=T1[:], func=AF.Exp, scale=ppc("lruc", slot * RCH + fc)),
                       reads=[B1, Bpp], writes=[B1])
                    op("act", lambda e, T1=T1, T3=T3: e.activation(out=T3[:], in_=T1[:], func=AF.Square), reads=[B1], writes=[B3])
                    op("act", lambda e, T3=T3: e.activation(out=T3[:], in_=T3[:], func=AF.Sqrt, scale=-1.0, bias=1.0), reads=[B3], writes=[B3])
                    op("dve", lambda e, T2=T2: e.tensor_tensor(out=T2[:], in0=T2[:], in1=xr[:, fc, :], op=ALU.mult), reads=[B2, Bxr[fc]], writes=[B2])
                    op("dve", lambda e, T2=T2, T3=T3: e.tensor_tensor(out=T2[:], in0=T2[:], in1=T3[:], op=ALU.mult), reads=[B2, B3], writes=[B2])
                    init = 0.0 if hf == 0 else carry_h[:, fc:fc + 1]
                    op("dve", lambda e, T1=T1, T2=T2, T3=T3, init=init: e.tensor_tensor_scan(out=T3[:], data0=T1[:], data1=T2[:], initial=init,
                                                                                            op0=ALU.mult, op1=ALU.add),
                       reads=[B1, B2, B3, Bch[fc]], writes=[B3])
                    if hf + 1 < NH:
                        op("dve", lambda e, T3=T3: e.tensor_copy(out=carry_h[:, fc:fc + 1], in_=T3[:, TH - 1:TH]), reads=[B3], writes=[Bch[fc]])
                    op("dve", lambda e, T3=T3: e.tensor_tensor(out=G[:, fc, :], in0=G[:, fc, :], in1=T3[:], op=ALU.mult), reads=[B3, BG[fc]], writes=[BG[fc]])
                for dc in range(NC8):
                    wo, Bwo = load_wp("ro", wout_d[slot], 0, dc, RCH)
                    if dc % 2 == 0 and dc + 2 < NC8:
                        prefetch_wp("ro", wout_d[slot], 0, dc + 2, RCH)
                    for ti, tt in enumerate(tts):
                        bank = next_bank()
                        matmul_group(bank, slice(0, 512), [(wo[:, k, :], G[:, k, ti * 512:(ti + 1) * 512]) for k in range(RCH)],
                                     [Bwo] + BG)
                        resid_add(bank, dc, tt)
            S_.barrier()

    def ffn(l):
        with contextlib.ExitStack() as st:
            work = norm_work(st)
            aT = sb("aT", [128, FCH, TH], BF16, st)
            Ba = [Buf() for _ in range(FCH)]
            Ur = Ring([sb("Uf%d" % i, [128, TH + 2], F32, st) for i in range(4)])
            hn, HN = hn_alloc(st, NTH)
            Yr = Ring([sb("Yf%d" % i, [128, TH], F32, st) for i in range(4)])
            Ybr = Ring([sb("Yb%d" % i, [128, TH], BF16, st) for i in range(2)])
            Gr = Ring([sb("Gf%d" % i, [128, TH], BF16, st) for i in range(2)])
            bank_ring[0] = list(range(8))
            for hf in range(NH):
                tts = list(range(hf * NTH, (hf + 1) * NTH))
                for ti, tt in enumerate(tts):
                    norm_tile(tt, "norm_ffn", l, work, (hn, HN, ti))
                def flush(pend):
                    jj, Y0, BY0, Yb_, BYb_ = pend
                    Gt, BGt = Gr.next()
                    op("act", lambda e: e.activation(out=Gt[:], in_=Y0[:], func=AF.Gelu_apprx_tanh), reads=[BY0], writes=[BGt])
                    op("dve", lambda e: e.tensor_tensor(out=aT[:, jj, :], in0=Gt[:], in1=Yb_[:], op=ALU.mult),
                       reads=[BGt, BYb_], writes=[Ba[jj]])

                pend = None
                for j in range(FCH):
                    cur = [load_wp("u%d" % gv, wup_d[l], gv * FCH, j, 8) for gv in range(2)]
                    if j % 2 == 0:
                        if j + 2 < FCH:
                            for gv in range(2):
                                prefetch_wp("u%d" % gv, wup_d[l], gv * FCH, j + 2, 8)
                        else:
                            prefetch_wp("dA", wdown_d[l], 0, 0, 11, (0, 11))
                            prefetch_wp("dB", wdown_d[l], 0, 0, 11, (11, 22))
                    for gv in range(2):
                        fch = gv * FCH + j
                        w_, Bw_ = cur[gv]
                        U, BU = Ur.next()
                        if hf == 0:
                            op("pool", lambda e, U=U: e.memset(U[:, 0:2], 0.0), writes=[BU])
                        else:
                            op("pool", lambda e, U=U, fch=fch: e.tensor_copy(out=U[:, 0:2], in_=carry_f[:, fch, :]), reads=[Bcf[fch]], writes=[BU])
                        for ti, tt in enumerate(tts):
                            bank = next_bank()
                            matmul_group(bank, slice(0, 512), [(w_[:, k, :], hn[:, k, ts(ti)]) for k in range(8)],
                                         [Bw_] + [HN[k][ti] for k in range(8)])
                            op("act", lambda e, U=U, ti=ti, bank=bank: e.activation(out=U[:, 2 + ti * 512:2 + (ti + 1) * 512], in_=psum[bank][:, :], func=AF.Identity),
                               reads=[PB[bank]], writes=[BU])
                        if hf + 1 < NH:
                            op("pool", lambda e, U=U, fch=fch: e.tensor_copy(out=carry_f[:, fch, :], in_=U[:, TH:TH + 2]), reads=[BU], writes=[Bcf[fch]])
                        Y, BY = Yr.next()
                        cw = lambda k, fch=fch: ppc("ffn_conv_w", (l * 3 + k) * 44 + fch)
                        op("act", lambda e, U=U, Y=Y, fch=fch: e.activation(out=Y[:], in_=U[:, 2:2 + TH], func=AF.Identity,
                                                                           scale=cw(2), bias=ppc("ffn_conv_b", l * 44 + fch)),
                           reads=[BU, Bpp], writes=[BY])
                        op("dve", lambda e, U=U, Y=Y: e.scalar_tensor_tensor(out=Y[:], in0=U[:, 1:1 + TH], scalar=cw(1), in1=Y[:],
                                                                             op0=ALU.mult, op1=ALU.add),
                           reads=[BU, BY, Bpp], writes=[BY])
                        if gv == 0:
                            op("dve", lambda e, U=U, Y=Y: e.scalar_tensor_tensor(out=Y[:], in0=U[:, 0:TH], scalar=cw(0), in1=Y[:],
                                                                                 op0=ALU.mult, op1=ALU.add),
                               reads=[BU, BY, Bpp], writes=[BY])
                            y0 = (Y, BY)
                            if pend is not None:
                                flush(pend)
                                pend = None
                        else:
                            Yb, BYb = Ybr.next()
                            op("dve", lambda e, U=U, Y=Y, Yb=Yb: e.scalar_tensor_tensor(out=Yb[:], in0=U[:, 0:TH], scalar=cw(0), in1=Y[:],
                                                                                       op0=ALU.mult, op1=ALU.add),
                               reads=[BU, BY, Bpp], writes=[BYb])
                            pend = (j, y0[0], y0[1], Yb, BYb)
                flush(pend)
                for dc in range(NC8):
                    wdA, BwdA = load_wp("dA", wdown_d[l], 0, dc, 11, (0, 11))
                    wdB, BwdB = load_wp("dB", wdown_d[l], 0, dc, 11, (11, 22))
                    if dc % 2 == 0:
                        if dc + 2 < NC8:
                            prefetch_wp("dA", wdown_d[l], 0, dc + 2, 11, (0, 11))
                            prefetch_wp("dB", wdown_d[l], 0, dc + 2, 11, (11, 22))
                        elif hf + 1 < NH:
                            for gv in range(2):
                                prefetch_wp("u%d" % gv, wup_d[l], gv * FCH, 0, 8)
                    for ti, tt in enumerate(tts):
                        bank = next_bank()
                        matmul_group(bank, slice(0, 512),
                                     [((wdA[:, k, :] if k < 11 else wdB[:, k - 11, :]), aT[:, k, ti * 512:(ti + 1) * 512]) for k in range(FCH)],
                                     [BwdA, BwdB] + Ba)
                        resid_add(bank, dc, tt)
            S_.barrier()

    def ple(l, sq_i):
        with contextlib.ExitStack() as st:
            work = norm_work(st)
            pT = Ring([sb("pT%d" % i, [128, 2, 512], BF16, st) for i in range(2)])
            Sg = Ring([sb("Sg%d" % i, [128, 512], F32, st) for i in range(2)])
            hn, HN = hn_alloc(st, NTH)
            pst = [sb("pst%d" % i, [128, PLE], F32, st) for i in range(4)]
            pstB = [Buf() for _ in range(4)]
            S_.sync_from("sp")
            bank_ring[0] = list(range(8))
            for hf in range(NH):
                tts = list(range(hf * NTH, (hf + 1) * NTH))
                pts = []
                for ti, tt in enumerate(tts):
                    norm_tile(tt, "norm_ple", l, work, (hn, HN, ti))
                    for q in range(4):
                        S_.dma("sp", pst[q][:], p_d[l, sq_i, tt * 512 + q * 128: tt * 512 + (q + 1) * 128, :], pstS[q], writes=[pstB[q]])
                    pt, Bpt = pT.next()
                    for ec in range(2):
                        bank = next_bank()
                        for q in range(4):
                            op("pe", lambda e, q=q, ec=ec, bank=bank: e.transpose(out=psum[bank][:, q * 128:(q + 1) * 128],
                                                                                 in_=pst[q][:, ec * 128:(ec + 1) * 128], identity=ident),
                               reads=[pstB[q], Bcst], writes=[PB[bank]])
                        op("act", lambda e, pt=pt, ec=ec, bank=bank: e.activation(out=pt[:, ec, :], in_=psum[bank][:, :], func=AF.Identity),
                           reads=[PB[bank]], writes=[Bpt])
                    pts.append((pt, Bpt))
                for dc in range(NC8):
                    wg, Bwg = load_wp("pg", wpg_d[l], 0, dc, 8)
                    wp, Bwp = load_wp("pp", wpp_d[l], 0, dc, 2)
                    if dc % 2 == 0:
                        ndc = dc + 2 if dc + 2 < NC8 else (0 if hf + 1 < NH else None)
                        if ndc is not None:
                            prefetch_wp("pg", wpg_d[l], 0, ndc, 8)
                            prefetch_wp("pp", wpp_d[l], 0, ndc, 2)
                    for ti, tt in enumerate(tts):
                        pt, Bpt = pts[ti]
                        bg = next_bank()
                        matmul_group(bg, slice(0, 512), [(wg[:, k, :], hn[:, k, ts(ti)]) for k in range(8)],
                                     [Bwg] + [HN[k][ti] for k in range(8)])
                        bp = next_bank()
                        matmul_group(bp, slice(0, 512), [(wp[:, k, :], pt[:, k, :]) for k in range(2)], [Bwp, Bpt])
                        sg, Bsg = Sg.next()
                        op("act", lambda e, sg=sg, bg=bg: e.activation(out=sg[:], in_=psum[bg][:, :], func=AF.Sigmoid), reads=[PB[bg]], writes=[Bsg])
                        op("dve", lambda e, sg=sg, bp=bp: e.tensor_tensor(out=sg[:], in0=sg[:], in1=psum[bp][:, :], op=ALU.mult),
                           reads=[Bsg, PB[bp]], writes=[Bsg])
                        op("dve", lambda e, sg=sg, dc=dc, tt=tt: e.tensor_tensor(out=xT[:, dc, ts(tt)], in0=xT[:, dc, ts(tt)], in1=sg[:], op=ALU.add),
                           reads=[Bsg, X[dc][tt]], writes=[X[dc][tt]])
            S_.barrier()

    for sq_i in range(NSEQ):
        load_x(sq_i)
        for l in range(DEPTH):
            if l % 2 == 0:
                attention(l)
            else:
                rglru(l)
            ffn(l)
            ple(l, sq_i)
        final_store(sq_i)
    es.close()
    return nc


def prep_shared(inp, depth):
    pp = PP(depth)
    nr = max(depth // 2, 1)
    tab = np.zeros((128, pp.n), np.float32)

    def put(name, arr):
        tab[:, pp.off[name]:pp.off[name] + arr.shape[1]] = arr

    put("norm_mix", _cols(inp["norm_mix"], 8))
    put("norm_ffn", _cols(inp["norm_ffn"], 8))
    put("norm_ple", _cols(inp["norm_ple"], 8))
    put("norm_final", _cols(inp["norm_final"], 8))
    put("ffn_conv_w", _cols(inp["ffn_conv_w"], 44))
    put("ffn_conv_b", _cols(inp["ffn_conv_b"], 44))
    put("rnn_conv_w", _cols(inp["rnn_conv_w"], 10))
    put("rnn_conv_b", _cols(inp["rnn_conv_b"], 10))
    put("b_gate_a", _cols(inp["rnn_b_gate_a"], 10))
    put("b_gate_x", _cols(inp["rnn_b_gate_x"], 10))
    put("lru", _cols(inp["rnn_lru_param"], 10))
    shared = {
        "pp": tab,
        "cst": make_consts(),
        "wband_a": _band(np.asarray(inp["rnn_w_gate_a"], np.float32)),
        "wband_x": _band(np.asarray(inp["rnn_w_gate_x"], np.float32)),
    }
    for k in ("attn_w_qkv", "attn_w_o", "rnn_w_in", "rnn_w_out", "ffn_w_up", "ffn_w_down", "ple_w_gate", "ple_w_proj"):
        shared[k] = np.ascontiguousarray(np.asarray(inp[k], np.float32))
    return shared


def run(inp, n_cores, depth):
    x = np.asarray(inp["x"], np.float32)
    p = np.asarray(inp["p"], np.float32)
    B, S, _ = x.shape
    assert B % n_cores == 0
    nseq = B // n_cores
    shared = prep_shared(inp, depth)
    nc = build(nseq, S, depth)
    in_maps = []
    for i in range(n_cores):
        m = dict(shared)
        m["x"] = np.ascontiguousarray(x[i * nseq:(i + 1) * nseq])
        m["p"] = np.ascontiguousarray(p[:, i * nseq:(i + 1) * nseq])
        in_maps.append(m)
    res = run_bass_kernel_spmd(nc, in_maps, core_ids=list(range(n_cores)))
    return np.concatenate([np.asarray(r["out"], np.float32) for r in res.results], axis=0)


def kernel(**inputs):
    return run(inputs, N_CORES, 4)
```

```python
import contextlib
import numpy as np
import concourse.bass as bass
import concourse.mybir as mybir
from concourse.bass_utils import run_bass_kernel_spmd

F32 = mybir.dt.float32
BF16 = mybir.dt.bfloat16
AF = mybir.ActivationFunctionType
ALU = mybir.AluOpType

D = 1024
NC8 = 8
HEADS = 16
DH = 64
RW = 1280
RCH = 10
RB = 80
FF = 2816
FCH = 22
PLE = 256
EPS = 1e-6
LRU_C = 8.0
N_CORES = 8


class SemObj:
    def __init__(self, sem, name):
        self.sem = sem
        self.name = name
        self.count = 0


class Buf:
    __slots__ = ("name", "w", "r")

    def __init__(self, name=""):
        self.name = name
        self.w = None
        self.r = {}


class Sched:
    def __init__(self, nc):
        self.nc = nc
        self.eng = {"pe": nc.tensor, "act": nc.scalar, "dve": nc.vector, "pool": nc.gpsimd, "sp": nc.sync}
        self.esem = {k: SemObj(nc.alloc_semaphore("es_" + k), k) for k in self.eng}
        self.waited = {k: {} for k in self.eng}
        self.ndsem = 0
        self.ninst = 0

    def dsem(self):
        self.ndsem += 1
        return SemObj(self.nc.alloc_semaphore("ds%d" % self.ndsem), "ds%d" % self.ndsem)

    def _deps(self, en, reads, writes):
        deps = {}

        def add(tok):
            if tok is None:
                return
            so, v = tok
            if en == "pe" and so is self.esem["pe"]:
                return
            if deps.get(so, 0) < v:
                deps[so] = v

        for b in reads:
            add(b.w)
        for b in writes:
            add(b.w)
            for so, v in b.r.items():
                add((so, v))
        w = self.waited[en]
        e = self.eng[en]
        for so, v in deps.items():
            if w.get(so, 0) >= v:
                continue
            assert so.count >= v, "waiting on un-issued increment %s %d>%d" % (so.name, v, so.count)
            e.wait_ge(so.sem, v)
            self.ninst += 1
            w[so] = v

    def _mark(self, tok, reads, writes):
        so, v = tok
        for b in reads:
            if b.r.get(so, 0) < v:
                b.r[so] = v
        for b in writes:
            b.w = tok
            b.r = {}

    def op(self, en, fn, reads=(), writes=(), inc=True):
        self._deps(en, reads, writes)
        ins = fn(self.eng[en])
        self.ninst += 1
        so = self.esem[en]
        if inc:
            so.count += 1
            ins.then_inc(so.sem, 1)
            tok = (so, so.count)
        else:
            tok = (so, so.count + 1)
        self._mark(tok, reads, writes)
        return ins

    def dma(self, q, out, in_, ds, reads=(), writes=(), **kw):
        self._deps(q, reads, writes)
        ins = self.eng[q].dma_start(out=out, in_=in_, **kw)
        self.ninst += 1
        ds.count += 16
        ins.then_inc(ds.sem, 16)
        self._mark((ds, ds.count), reads, writes)
        return ins

    def wait_all(self, en, bufs):
        self._deps(en, bufs, bufs)

    def sync_from(self, en, others=("pe", "act", "dve")):
        w = self.waited[en]
        for other in others:
            so = self.esem[other]
            if so.count > w.get(so, 0):
                self.eng[en].wait_ge(so.sem, so.count)
                self.ninst += 1
                w[so] = so.count

    def barrier(self, engines=("pe", "act", "dve", "pool")):
        for en in engines:
            w = self.waited[en]
            for other in engines:
                if other == en:
                    continue
                so = self.esem[other]
                if so.count > w.get(so, 0):
                    self.eng[en].wait_ge(so.sem, so.count)
                    self.ninst += 1
                    w[so] = so.count


class Ring:
    def __init__(self, tiles):
        self.tiles = tiles
        self.bufs = [Buf() for _ in tiles]
        self.i = 0

    def next(self):
        k = self.i % len(self.tiles)
        self.i += 1
        return self.tiles[k], self.bufs[k]


def _cols(v, nch):
    lead = int(np.prod(v.shape[:-1])) if v.ndim > 1 else 1
    a = np.asarray(v, np.float32).reshape(lead, nch, 128)
    return np.ascontiguousarray(a.transpose(2, 0, 1).reshape(128, lead * nch))


def band_range(fc):
    b_lo = (128 * fc) // RB
    b_hi = (128 * fc + 127) // RB
    k_lo = (RB * b_lo) // 128
    k_hi = (RB * b_hi + RB - 1) // 128
    return k_lo, min(k_hi, RCH - 1)


def _band(wg):
    L = wg.shape[0]
    dense = np.zeros((L, RW, RW), np.float32)
    for h in range(RW // RB):
        dense[:, h * RB:(h + 1) * RB, h * RB:(h + 1) * RB] = wg[:, h]
    out = np.zeros((L, 128, RCH, 3, 128), np.float32)
    for fc in range(RCH):
        k_lo, k_hi = band_range(fc)
        for kk, kc in enumerate(range(k_lo, k_hi + 1)):
            out[:, :, fc, kk, :] = dense[:, kc * 128:(kc + 1) * 128, fc * 128:(fc + 1) * 128]
    return out


class PP:
    def __init__(self, depth):
        self.off = {}
        self.n = 0
        nr = max(depth // 2, 1)
        for name, cnt in [("norm_mix", depth * 8), ("norm_ffn", depth * 8), ("norm_ple", depth * 8), ("norm_final", 8),
                          ("ffn_conv_w", depth * 3 * 44), ("ffn_conv_b", depth * 44),
                          ("rnn_conv_w", nr * 4 * 10), ("rnn_conv_b", nr * 10), ("b_gate_a", nr * 10),
                          ("b_gate_x", nr * 10), ("lru", nr * 10), ("lruc", nr * 10)]:
            self.off[name] = self.n
            self.n += cnt


def make_consts():
    j = np.arange(128)[:, None]
    s = np.arange(128)[None, :]
    ident = (j == s).astype(np.float32)
    negtri = -(j >= s).astype(np.float32)
    negones = -np.ones((128, 128), np.float32)
    masklt = (j < s).astype(np.float32)
    ones = np.ones((128, 128), np.float32)
    return np.ascontiguousarray(np.concatenate([ident, negtri, negones, masklt, ones], axis=1))


def build(NSEQ, S, DEPTH):
    assert S % 512 == 0
    NTT = S // 512
    NQB = S // 128
    TH = min(S, 1024)
    NH = S // TH
    NTH = TH // 512
    NATT = (DEPTH + 1) // 2
    NRNN = max(DEPTH // 2, 1)
    pp = PP(DEPTH)

    nc = bass.Bass("TRN2", target_bir_lowering=False)

    def dram(name, shape, kind="ExternalInput", dt=F32):
        return nc.dram_tensor(name, list(shape), dt, kind=kind).ap()

    x_d = dram("x", [NSEQ, S, D])
    p_d = dram("p", [DEPTH, NSEQ, S, PLE])
    out_d = dram("out", [NSEQ, S, D], kind="ExternalOutput")
    wqkv_d = dram("attn_w_qkv", [NATT, D, 3 * D])
    wo_d = dram("attn_w_o", [NATT, D, D])
    win_d = dram("rnn_w_in", [NRNN, D, 2 * RW])
    wout_d = dram("rnn_w_out", [NRNN, RW, D])
    wup_d = dram("ffn_w_up", [DEPTH, D, 2 * FF])
    wdown_d = dram("ffn_w_down", [DEPTH, FF, D])
    wpg_d = dram("ple_w_gate", [DEPTH, D, D])
    wpp_d = dram("ple_w_proj", [DEPTH, PLE, D])
    wba_d = dram("wband_a", [NRNN, 128, RCH, 3, 128])
    wbx_d = dram("wband_x", [NRNN, 128, RCH, 3, 128])
    pp_d = dram("pp", [128, pp.n])
    cst_d = dram("cst", [128, 5 * 128])

    S_ = Sched(nc)
    op = S_.op
    es = contextlib.ExitStack()

    uid = [0]

    def sb(name, shape, dt, stack=None):
        uid[0] += 1
        return (stack or es).enter_context(nc.sbuf_tensor("%s_u%d" % (name, uid[0]), list(shape), dt))

    xT = sb("xT", [128, NC8, S], F32)
    ppt = sb("ppt", [128, pp.n], F32)
    cst32 = sb("cst32", [128, 5 * 128], F32)
    cstb = sb("cstb", [128, 5 * 128], BF16)
    ident = cst32[:, 0:128]
    negtri = cstb[:, 128:256]
    negones = cstb[:, 256:384]
    masklt = cstb[:, 384:512]
    onesb = cstb[:, 512:640]
    NW = 4
    WSZ = FCH * 128
    wslots = [sb("w%d" % i, [128, WSZ], BF16) for i in range(NW)]
    wbufs = [Buf("w%d" % i) for i in range(NW)]
    wsems = [S_.dsem() for _ in range(NW)]
    wcnt = [0]
    stgS = [S_.dsem() for _ in range(4)]
    pstS = [S_.dsem() for _ in range(4)]
    carry_f = sb("carry_f", [128, 2 * FCH, 2], F32)
    carry_r = sb("carry_r", [128, RCH, 3], F32)
    carry_h = sb("carry_h", [128, RCH], F32)
    Bcf = [Buf() for _ in range(2 * FCH)]
    Bcr = [Buf() for _ in range(RCH)]
    Bch = [Buf() for _ in range(RCH)]

    psum = [es.enter_context(nc.psum_tensor("ps%d" % i, [128, 512], F32)) for i in range(8)]
    PB = [Buf("ps%d" % i) for i in range(8)]
    bank_ring = [list(range(8))]
    bank_i = [0]

    def next_bank():
        r = bank_ring[0]
        b = r[bank_i[0] % len(r)]
        bank_i[0] += 1
        return b

    X = [[Buf() for _ in range(NTT)] for _ in range(NC8)]
    Bpp = Buf("pp")
    Bcst = Buf("cst")

    def ts(tt):
        return slice(tt * 512, (tt + 1) * 512)

    def ppc(name, idx):
        o = pp.off[name] + idx
        return ppt[:, o:o + 1]

    d0 = S_.dsem()
    S_.dma("sp", ppt[:], pp_d[:, :], d0, writes=[Bpp])
    d1 = S_.dsem()
    S_.dma("sp", cst32[:], cst_d[:, :], d1, writes=[Bcst])
    op("dve", lambda e: e.tensor_copy(out=cstb[:], in_=cst32[:]), reads=[Bcst], writes=[Bcst])
    nl = NRNN * RCH
    lo, lc = pp.off["lru"], pp.off["lruc"]
    op("act", lambda e: e.activation(out=ppt[:, lc:lc + nl], in_=ppt[:, lo:lo + nl], func=AF.Exp, scale=-1.0),
       reads=[Bpp], writes=[Bpp])
    op("act", lambda e: e.activation(out=ppt[:, lc:lc + nl], in_=ppt[:, lc:lc + nl], func=AF.Ln, bias=1.0),
       reads=[Bpp], writes=[Bpp])
    op("dve", lambda e: e.tensor_scalar(out=ppt[:, lc:lc + nl], in0=ppt[:, lc:lc + nl], scalar1=-LRU_C, scalar2=None,
                                        op0=ALU.mult), reads=[Bpp], writes=[Bpp])

    def load_w(view, K, n=128):
        i = wcnt[0] % NW
        wcnt[0] += 1
        t = wslots[i][:, 0:K * n].rearrange("p (k n) -> p k n", k=K)
        S_.dma("pool", t, view, wsems[i], writes=[wbufs[i]])
        return t, wbufs[i]

    pair_cache = {}

    def load_wp(name, w_ap, col0_chunk, idx, K, krange=None):
        if idx % 2 == 0:
            key = (name, col0_chunk + idx)
            if key in stash:
                pair_cache[name] = stash.pop(key)
            else:
                pair_cache[name] = slab_load(w_ap, col0_chunk + idx, K, krange)
        t, B = pair_cache[name]
        return t[:, :, (idx % 2) * 128:(idx % 2 + 1) * 128], B

    stash = {}

    def slab_load(w_ap, chunk, K, krange):
        f0 = chunk * 128
        v = w_ap.rearrange("(k p) f -> p k f", p=128)
        v = v[:, krange[0]:krange[1], f0:f0 + 256] if krange else v[:, :, f0:f0 + 256]
        return load_w(v, K, 256)

    def prefetch_wp(name, w_ap, col0_chunk, idx, K, krange=None):
        key = (name, col0_chunk + idx)
        if key not in stash:
            stash[key] = slab_load(w_ap, col0_chunk + idx, K, krange)

    def wview(w_ap, f0, n=128):
        return w_ap.rearrange("(k p) f -> p k f", p=128)[:, :, f0:f0 + n]

    def matmul_group(bank, cols, pairs, reads, inc_last=True):
        n = len(pairs)
        for k, (l, r) in enumerate(pairs):
            op("pe", lambda e, l=l, r=r, k=k: e.matmul(psum[bank][:, cols], lhsT=l, rhs=r, start=(k == 0), stop=(k == n - 1)),
               reads=reads, writes=[PB[bank]], inc=(inc_last and k == n - 1))

    def norm_tile(tt, gname, gidx, work, dst=None, out_f32=None):
        sq_ring, rs_ring = work
        if dst is not None:
            hn, HN, tl = dst
        bank = next_bank()
        for c in range(NC8):
            sq, Bsq = sq_ring.next()
            op("act", lambda e, c=c, sq=sq: e.activation(out=sq[:], in_=xT[:, c, ts(tt)], func=AF.Square),
               reads=[X[c][tt]], writes=[Bsq])
            op("pe", lambda e, c=c, sq=sq: e.matmul(psum[bank][:, :], lhsT=onesb, rhs=sq[:], start=(c == 0), stop=(c == NC8 - 1)),
               reads=[Bsq, Bcst], writes=[PB[bank]])
        rs, Brs = rs_ring.next()
        op("act", lambda e: e.activation(out=rs[:], in_=psum[bank][:, :], func=AF.Ln, scale=1.0 / D, bias=EPS),
           reads=[PB[bank]], writes=[Brs])
        op("act", lambda e: e.activation(out=rs[:], in_=rs[:], func=AF.Exp, scale=-0.5), reads=[Brs], writes=[Brs])
        for c in range(NC8):
            o = hn[:, c, ts(tl)] if out_f32 is None else out_f32[0][:, c, :]
            wb = [HN[c][tl]] if out_f32 is None else [out_f32[1]]
            op("dve", lambda e, c=c, o=o: e.scalar_tensor_tensor(out=o, in0=xT[:, c, ts(tt)], scalar=ppc(gname, gidx * 8 + c),
                                                                 in1=rs[:], op0=ALU.mult, op1=ALU.mult),
               reads=[X[c][tt], Brs, Bpp], writes=wb)

    def resid_add(bank, dc, tt):
        op("dve", lambda e: e.tensor_tensor(out=xT[:, dc, ts(tt)], in0=xT[:, dc, ts(tt)], in1=psum[bank][:, :], op=ALU.add),
           reads=[PB[bank], X[dc][tt]], writes=[X[dc][tt]])

    def hn_alloc(stack, ntl):
        t = sb("hn", [128, NC8, ntl * 512], BF16, stack)
        return t, [[Buf() for _ in range(ntl)] for _ in range(NC8)]

    def norm_work(stack):
        sq_ring = Ring([sb("sq%d" % i, [128, 512], BF16, stack) for i in range(2)])
        rs_ring = Ring([sb("rs%d" % i, [128, 512], F32, stack) for i in range(2)])
        return sq_ring, rs_ring

    def load_x(sq_i):
        bank_ring[0] = list(range(8))
        st = contextlib.ExitStack()
        stg = [sb("stg%d" % i, [128, D], F32, st) for i in range(4)]
        stgB = [Buf() for _ in range(4)]
        S_.sync_from("sp")
        for tt in range(NTT):
            for q in range(4):
                S_.dma("sp", stg[q][:], x_d[sq_i, tt * 512 + q * 128: tt * 512 + (q + 1) * 128, :], stgS[q], writes=[stgB[q]])
            for c in range(NC8):
                bank = next_bank()
                for q in range(4):
                    op("pe", lambda e, q=q, c=c: e.transpose(out=psum[bank][:, q * 128:(q + 1) * 128],
                                                             in_=stg[q][:, c * 128:(c + 1) * 128], identity=ident),
                       reads=[stgB[q], Bcst], writes=[PB[bank]])
                op("act", lambda e, c=c: e.activation(out=xT[:, c, ts(tt)], in_=psum[bank][:, :], func=AF.Identity),
                   reads=[PB[bank]], writes=[X[c][tt]])
        S_.barrier()
        st.close()

    def final_store(sq_i):
        with contextlib.ExitStack() as st:
            work = norm_work(st)
            yf = sb("yf", [128, NC8, 512], F32, st)
            Byf = Buf()
            stg = [sb("stg%d" % i, [128, D], F32, st) for i in range(4)]
            stgB = [Buf() for _ in range(4)]
            bank_ring[0] = list(range(8))
            for tt in range(NTT):
                norm_tile(tt, "norm_final", 0, work, out_f32=(yf, Byf))
                for q in range(4):
                    b0, b1 = next_bank(), next_bank()
                    for c in range(NC8):
                        bank = b0 if c < 4 else b1
                        op("pe", lambda e, c=c, bank=bank: e.transpose(out=psum[bank][:, (c % 4) * 128:(c % 4 + 1) * 128],
                                                                      in_=yf[:, c, q * 128:(q + 1) * 128], identity=ident),
                           reads=[Byf, Bcst], writes=[PB[bank]])
                    op("act", lambda e: e.activation(out=stg[q][:, 0:512], in_=psum[b0][:, :], func=AF.Identity),
                       reads=[PB[b0]], writes=[stgB[q]])
                    op("dve", lambda e: e.tensor_copy(out=stg[q][:, 512:1024], in_=psum[b1][:, :]),
                       reads=[PB[b1]], writes=[stgB[q]])
                    S_.dma("sp", out_d[sq_i, tt * 512 + q * 128: tt * 512 + (q + 1) * 128, :], stg[q][:], stgS[q], reads=[stgB[q]])
            for en in ("sp", "pe", "act", "dve"):
                S_.wait_all(en, stgB)
            S_.barrier()

    def attention(l):
        slot = l // 2
        with contextlib.ExitStack() as st:
            OT = sb("OT", [128, NC8, S], BF16, st)
            BO = [[Buf() for _ in range(NTT)] for _ in range(NC8)]
            QTs = [sb("QT0", [128, S], BF16, st)]
            KTs = [sb("KT0", [128, S], BF16, st)]
            Vbs = [sb("Vb0", [128, NQB, 128], BF16, st)]
            BQs, BKs, BVs = [Buf(), Buf()], [Buf(), Buf()], [Buf(), Buf()]
            NSTR = 4
            Er = [Ring([sb("E%d_%d" % (s, i), [128, 512], BF16, st) for i in range(1)]) for s in range(NSTR)]
            Lr = [Ring([sb("L%d_%d" % (s, i), [128, 512], BF16, st) for i in range(2)]) for s in range(NSTR)]
            Wr = [Ring([sb("W%d_%d" % (s, i), [128, 512], BF16, st) for i in range(2)]) for s in range(NSTR)]
            Lsuf = [sb("Ls%d" % s, [128, 512], BF16, st) for s in range(NSTR)]
            BLs = [Buf() for _ in range(NSTR)]

            hn, HN = hn_alloc(st, NTT)
            bank_ring[0] = list(range(8))
            st2 = contextlib.ExitStack()
            work = norm_work(st2)
            for tt in range(NTT):
                norm_tile(tt, "norm_mix", l, work, (hn, HN, tt))
            S_.barrier()
            st2.close()
            QTs.append(sb("QT1", [128, S], BF16, st))
            KTs.append(sb("KT1", [128, S], BF16, st))
            Vbs.append(sb("Vb1", [128, NQB, 128], BF16, st))
            bank_ring[0] = [6, 7]
            if NTT == 4:
                pairs = [[0, 3], [1, 2]]
            elif NTT == 2:
                pairs = [[0], [1]]
            else:
                pairs = [list(range(NTT))]
            def proj_units(c):
                cb = c % 2
                QT, KT, Vb, BQ, BK, BV = QTs[cb], KTs[cb], Vbs[cb], BQs[cb], BKs[cb], BVs[cb]
                wq, Bwq = load_wp("q", wqkv_d[slot], 0, c, 8)
                wk, Bwk = load_wp("k", wqkv_d[slot], 8, c, 8)
                wv, Bwv = load_wp("v", wqkv_d[slot], 16, c, 8)
                for tt in range(NTT):
                    bank = next_bank()
                    matmul_group(bank, slice(0, 512), [(wq[:, k, :], hn[:, k, ts(tt)]) for k in range(8)],
                                 [Bwq] + [HN[k][tt] for k in range(8)])
                    op("dve", lambda e: e.tensor_scalar(out=QT[:, ts(tt)], in0=psum[bank][:, :], scalar1=DH ** -0.5, scalar2=None,
                                                        op0=ALU.mult), reads=[PB[bank]], writes=[BQ])
                    yield
                    bank = next_bank()
                    matmul_group(bank, slice(0, 512), [(wk[:, k, :], hn[:, k, ts(tt)]) for k in range(8)],
                                 [Bwk] + [HN[k][tt] for k in range(8)])
                    op("dve", lambda e: e.tensor_copy(out=KT[:, ts(tt)], in_=psum[bank][:, :]), reads=[PB[bank]], writes=[BK])
                    yield
                for tt in range(NTT):
                    bank = next_bank()
                    for q in range(4):
                        tb = tt * 4 + q
                        matmul_group(bank, slice(q * 128, (q + 1) * 128),
                                     [(hn[:, k, tb * 128:(tb + 1) * 128], wv[:, k, :]) for k in range(8)],
                                     [Bwv] + [HN[k][tt] for k in range(8)])
                    op("dve", lambda e: e.tensor_copy(out=Vb[:, tt * 4:(tt + 1) * 4, :],
                                                      in_=psum[bank][:, :].rearrange("p (q n) -> p q n", q=4)),
                       reads=[PB[bank]], writes=[BV])
                    yield

            for _ in proj_units(0):
                pass
            for c in range(NC8):
                cb = c % 2
                QT, KT, Vb, BQ, BK, BV = QTs[cb], KTs[cb], Vbs[cb], BQs[cb], BKs[cb], BVs[cb]
                if c + 1 < NC8 and (c + 1) % 2 == 0:
                    prefetch_wp("q", wqkv_d[slot], 0, c + 1, 8)
                    prefetch_wp("k", wqkv_d[slot], 8, c + 1, 8)
                    prefetch_wp("v", wqkv_d[slot], 16, c + 1, 8)
                nxt = proj_units(c + 1) if c + 1 < NC8 else iter(())
                steps = []
                for ps_i, tcs in enumerate(pairs):
                    lst = []
                    for tc in tcs:
                        kbs = list(range(4 * tc + 3, -1, -1))
                        for idx, kb in enumerate(kbs):
                            lst.append((tc, kb, idx, idx == len(kbs) - 1))
                    steps.append(lst)
                nround = max(len(s) for s in steps)
                acts, infos = [], []
                for r in range(nround):
                    act = [(ps_i, hh) for ps_i in range(len(pairs)) if r < len(steps[ps_i]) for hh in range(2)]
                    info = {}
                    for (ps_i, hh) in act:
                        tc, kb, idx, last = steps[ps_i][r]
                        sid = ps_i * 2 + hh
                        off = max(0, kb - 4 * tc)
                        c0 = off * 128
                        diag = kb >= 4 * tc
                        info[(ps_i, hh)] = (tc, kb, idx, last, sid, c0, diag)
                    acts.append(act)
                    infos.append(info)

                def emit_z(r):
                    for key in acts[r]:
                        tc, kb, idx, last, sid, c0, diag = infos[r][key]
                        hh = key[1]
                        hp = slice(64 * hh, 64 * hh + 64)
                        if idx == 0:
                            op("dve", lambda e, sid=sid: e.memset(Lsuf[sid][:], 0.0), writes=[BLs[sid]])
                        op("pe", lambda e, sid=sid, hp=hp, kb=kb, tc=tc, c0=c0: e.matmul(
                            psum[sid][:, c0:512], lhsT=KT[hp, kb * 128:(kb + 1) * 128], rhs=QT[hp, tc * 512 + c0:(tc + 1) * 512],
                            start=True, stop=False, skip_group_check=True), reads=[BK, BQ], writes=[PB[sid]])

                emit_z(0)
                for r in range(nround):
                    act, info = acts[r], infos[r]
                    tiles = {}
                    for key in act:
                        tc, kb, idx, last, sid, c0, diag = info[key]
                        Et, BE = Er[sid].next()
                        op("act", lambda e, Et=Et, sid=sid, c0=c0: e.activation(out=Et[:, c0:512], in_=psum[sid][:, c0:512], func=AF.Exp),
                           reads=[PB[sid]], writes=[BE])
                        tiles[key] = [Et, BE]
                    for key in act:
                        tc, kb, idx, last, sid, c0, diag = info[key]
                        Et, BE = tiles[key]
                        Lt, BL = Lr[sid].next()
                        op("act", lambda e, Et=Et, Lt=Lt, c0=c0: e.activation(out=Lt[:, c0:512], in_=Et[:, c0:512], func=AF.Ln, bias=1.0),
                           reads=[BE], writes=[BL])
                        if diag:
                            op("dve", lambda e, Lt=Lt, c0=c0: e.tensor_tensor(out=Lt[:, c0:c0 + 128], in0=Lt[:, c0:c0 + 128], in1=masklt, op=ALU.mult),
                               reads=[BL, Bcst], writes=[BL])
                        tiles[key] += [Lt, BL]
                    for key in act:
                        tc, kb, idx, last, sid, c0, diag = info[key]
                        Et, BE, Lt, BL = tiles[key]
                        op("pe", lambda e, sid=sid, Lt=Lt, c0=c0, idx=idx: e.matmul(
                            psum[sid][:, c0:512], lhsT=negtri, rhs=Lt[:, c0:512], start=False, stop=(idx == 0), skip_group_check=True),
                           reads=[BL, Bcst], writes=[PB[sid]])
                        if idx > 0:
                            op("pe", lambda e, sid=sid, c0=c0: e.matmul(
                                psum[sid][:, c0:512], lhsT=negones, rhs=Lsuf[sid][:, c0:512], start=False, stop=True, skip_group_check=True),
                               reads=[BLs[sid], Bcst], writes=[PB[sid]])
                    for key in act:
                        tc, kb, idx, last, sid, c0, diag = info[key]
                        Wt, BW = Wr[sid].next()
                        op("act", lambda e, Wt=Wt, sid=sid, c0=c0: e.activation(out=Wt[:, c0:512], in_=psum[sid][:, c0:512], func=AF.Exp),
                           reads=[PB[sid]], writes=[BW])
                        if diag:
                            op("dve", lambda e, Wt=Wt, c0=c0: e.tensor_tensor(out=Wt[:, c0:c0 + 128], in0=Wt[:, c0:c0 + 128], in1=masklt, op=ALU.mult),
                               reads=[BW, Bcst], writes=[BW])
                        tiles[key] += [Wt, BW]
                    if r + 1 < nround:
                        emit_z(r + 1)
                    for key in act:
                        tc, kb, idx, last, sid, c0, diag = info[key]
                        ps_i, hh = key
                        Et, BE, Lt, BL, Wt, BW = tiles[key]
                        pob = 4 + ps_i
                        op("pe", lambda e, pob=pob, hh=hh, kb=kb, Wt=Wt, c0=c0, idx=idx, last=last: e.matmul(
                            psum[pob][64 * hh:64 * hh + 64, c0:512], lhsT=Vb[:, kb, 64 * hh:64 * hh + 64], rhs=Wt[:, c0:512],
                            start=(idx == 0), stop=last, skip_group_check=True), reads=[BW, BV], writes=[PB[pob]])
                        if not last:
                            op("dve", lambda e, sid=sid, Lt=Lt, c0=c0: e.tensor_tensor(out=Lsuf[sid][:, c0:512], in0=Lsuf[sid][:, c0:512],
                                                                                      in1=Lt[:, c0:512], op=ALU.add),
                               reads=[BL, BLs[sid]], writes=[BLs[sid]])
                    if r >= 1:
                        next(nxt, None)
                    for ps_i in range(len(pairs)):
                        if r < len(steps[ps_i]) and steps[ps_i][r][3]:
                            tc = steps[ps_i][r][0]
                            pob = 4 + ps_i
                            op("dve", lambda e, pob=pob, tc=tc: e.tensor_copy(out=OT[:, c, ts(tc)], in_=psum[pob][:, :]),
                               reads=[PB[pob]], writes=[BO[c][tc]])
                for _ in nxt:
                    pass
            bank_ring[0] = list(range(8))
            for dc in range(NC8):
                wo, Bwo = load_wp("o", wo_d[slot], 0, dc, 8)
                if dc % 2 == 0 and dc + 2 < NC8:
                    prefetch_wp("o", wo_d[slot], 0, dc + 2, 8)
                for tt in range(NTT):
                    bank = next_bank()
                    matmul_group(bank, slice(0, 512), [(wo[:, k, :], OT[:, k, ts(tt)]) for k in range(8)],
                                 [Bwo] + [BO[k][tt] for k in range(8)])
                    resid_add(bank, dc, tt)
            S_.barrier()

    def rglru(l):
        slot = l // 2
        with contextlib.ExitStack() as st:
            work = norm_work(st)
            xr = sb("xr", [128, RCH, TH], BF16, st)
            G = sb("G", [128, RCH, TH], BF16, st)
            Bxr = [Buf() for _ in range(RCH)]
            BG = [Buf() for _ in range(RCH)]
            Ur = Ring([sb("Ur%d" % i, [128, TH + 3], F32, st) for i in range(2)])
            Yr = Ring([sb("Yr%d" % i, [128, TH], F32, st) for i in range(2)])
            T1r = Ring([sb("T1_%d" % i, [128, TH], F32, st) for i in range(2)])
            T2r = Ring([sb("T2_%d" % i, [128, TH], F32, st) for i in range(2)])
            T3r = Ring([sb("T3_%d" % i, [128, TH], F32, st) for i in range(2)])
            hn, HN = hn_alloc(st, NTH)
            bank_ring[0] = list(range(8))
            for hf in range(NH):
                h0 = hf * TH
                tts = list(range(hf * NTH, (hf + 1) * NTH))
                for ti, tt in enumerate(tts):
                    norm_tile(tt, "norm_mix", l, work, (hn, HN, ti))
                for j in range(RCH):
                    wr, Bwr = load_wp("r", win_d[slot], RCH, j, 8)
                    wg, Bwg = load_wp("g", win_d[slot], 0, j, 8)
                    if j % 2 == 0 and j + 2 < RCH:
                        prefetch_wp("r", win_d[slot], RCH, j + 2, 8)
                        prefetch_wp("g", win_d[slot], 0, j + 2, 8)
                    U, BU = Ur.next()
                    if hf == 0:
                        op("pool", lambda e, U=U: e.memset(U[:, 0:3], 0.0), writes=[BU])
                    else:
                        op("pool", lambda e, U=U, j=j: e.tensor_copy(out=U[:, 0:3], in_=carry_r[:, j, :]), reads=[Bcr[j]], writes=[BU])
                    for ti, tt in enumerate(tts):
                        bank = next_bank()
                        matmul_group(bank, slice(0, 512), [(wr[:, k, :], hn[:, k, ts(ti)]) for k in range(8)],
                                     [Bwr] + [HN[k][ti] for k in range(8)])
                        op("act", lambda e, U=U, ti=ti: e.activation(out=U[:, 3 + ti * 512:3 + (ti + 1) * 512], in_=psum[bank][:, :], func=AF.Identity),
                           reads=[PB[bank]], writes=[BU])
                    if hf + 1 < NH:
                        op("pool", lambda e, U=U, j=j: e.tensor_copy(out=carry_r[:, j, :], in_=U[:, TH:TH + 3]), reads=[BU], writes=[Bcr[j]])
                    Y, BY = Yr.next()
                    cw = lambda k, j=j: ppc("rnn_conv_w", (slot * 4 + k) * RCH + j)
                    op("act", lambda e, U=U, Y=Y, j=j: e.activation(out=Y[:], in_=U[:, 3:3 + TH], func=AF.Identity,
                                                                   scale=cw(3), bias=ppc("rnn_conv_b", slot * RCH + j)),
                       reads=[BU, Bpp], writes=[BY])
                    for k in (2, 1):
                        op("dve", lambda e, U=U, Y=Y, k=k: e.scalar_tensor_tensor(out=Y[:], in0=U[:, k:k + TH], scalar=cw(k), in1=Y[:],
                                                                                  op0=ALU.mult, op1=ALU.add),
                           reads=[BU, BY, Bpp], writes=[BY])
                    op("dve", lambda e, U=U, Y=Y, j=j: e.scalar_tensor_tensor(out=xr[:, j, :], in0=U[:, 0:TH], scalar=cw(0), in1=Y[:],
                                                                              op0=ALU.mult, op1=ALU.add),
                       reads=[BU, BY, Bpp], writes=[Bxr[j]])
                    for ti, tt in enumerate(tts):
                        bank = next_bank()
                        matmul_group(bank, slice(0, 512), [(wg[:, k, :], hn[:, k, ts(ti)]) for k in range(8)],
                                     [Bwg] + [HN[k][ti] for k in range(8)])
                        op("act", lambda e, j=j, ti=ti: e.activation(out=G[:, j, ti * 512:(ti + 1) * 512], in_=psum[bank][:, :], func=AF.Gelu_apprx_tanh),
                           reads=[PB[bank]], writes=[BG[j]])
                for fc in range(RCH):
                    k_lo, k_hi = band_range(fc)
                    nk = k_hi - k_lo + 1
                    wa, Bwa = load_w(wba_d[slot, :, fc, 0:nk, :], nk)
                    wx, Bwx = load_w(wbx_d[slot, :, fc, 0:nk, :], nk)
                    T1, B1 = T1r.next()
                    T2, B2 = T2r.next()
                    T3, B3 = T3r.next()
                    for (wt_, Bw_, T_, B_, bn) in ((wa, Bwa, T1, B1, "b_gate_a"), (wx, Bwx, T2, B2, "b_gate_x")):
                        for ti in range(NTH):
                            bank = next_bank()
                            matmul_group(bank, slice(0, 512),
                                         [(wt_[:, kk, :], xr[:, k_lo + kk, ti * 512:(ti + 1) * 512]) for kk in range(nk)],
                                         [Bw_] + [Bxr[k_lo + kk] for kk in range(nk)])
                            op("act", lambda e, T_=T_, ti=ti, bn=bn, bank=bank: e.activation(
                                out=T_[:, ti * 512:(ti + 1) * 512], in_=psum[bank][:, :], func=AF.Sigmoid, bias=ppc(bn, slot * RCH + fc)),
                               reads=[PB[bank], Bpp], writes=[B_])
                    op("act", lambda e, T1=T1: e.activation(out=T1[:], in_=T1[:], func=AF.Exp, scale=ppc("lruc", slot * RCH + fc)),
                       reads=[B1, Bpp], writes=[B1])
                    op("act", lambda e, T1=T1, T3=T3: e.activation(out=T3[:], in_=T1[:], func=AF.Square), reads=[B1], writes=[B3])
                    op("act", lambda e, T3=T3: e.activation(out=T3[:], in_=T3[:], func=AF.Sqrt, scale=-1.0, bias=1.0), reads=[B3], writes=[B3])
                    op("dve", lambda e, T2=T2: e.tensor_tensor(out=T2[:], in0=T2[:], in1=xr[:, fc, :], op=ALU.mult), reads=[B2, Bxr[fc]], writes=[B2])
                    op("dve", lambda e, T2=T2, T3=T3: e.tensor_tensor(out=T2[:], in0=T2[:], in1=T3[:], op=ALU.mult), reads=[B2, B3], writes=[B2])
                    init = 0.0 if hf == 0 else carry_h[:, fc:fc + 1]
                    op("dve", lambda e, T1=T1, T2=T2, T3=T3, init=init: e.tensor_tensor_scan(out=T3[:], data0=T1[:], data1=T2[:], initial=init,
                                                                                            op0=ALU.mult, op1=ALU.add),
                       reads=[B1, B2, B3, Bch[fc]], writes=[B3])
                    if hf + 1 < NH:
                        op("dve", lambda e, T3=T3: e.tensor_copy(out=carry_h[:, fc:fc + 1], in_=T3[:, TH - 1:TH]), reads=[B3], writes=[Bch[fc]])
                    op("dve", lambda e, T3=T3: e.tensor_tensor(out=G[:, fc, :], in0=G[:, fc, :], in1=T3[:], op=ALU.mult), reads=[B3, BG[fc]], writes=[BG[fc]])
                for dc in range(NC8):
                    wo, Bwo = load_wp("ro", wout_d[slot], 0, dc, RCH)
                    if dc % 2 == 0 and dc + 2 < NC8:
                        prefetch_wp("ro", wout_d[slot], 0, dc + 2, RCH)
                    for ti, tt in enumerate(tts):
                        bank = next_bank()
                        matmul_group(bank, slice(0, 512), [(wo[:, k, :], G[:, k, ti * 512:(ti + 1) * 512]) for k in range(RCH)],
                                     [Bwo] + BG)
                        resid_add(bank, dc, tt)
            S_.barrier()

    def ffn(l):
        with contextlib.ExitStack() as st:
            work = norm_work(st)
            aT = sb("aT", [128, FCH, TH], BF16, st)
            Ba = [Buf() for _ in range(FCH)]
            Ur = Ring([sb("Uf%d" % i, [128, TH + 2], F32, st) for i in range(4)])
            hn, HN = hn_alloc(st, NTH)
            Yr = Ring([sb("Yf%d" % i, [128, TH], F32, st) for i in range(4)])
            Ybr = Ring([sb("Yb%d" % i, [128, TH], BF16, st) for i in range(2)])
            Gr = Ring([sb("Gf%d" % i, [128, TH], BF16, st) for i in range(2)])
            bank_ring[0] = list(range(8))
            for hf in range(NH):
                tts = list(range(hf * NTH, (hf + 1) * NTH))
                if hf == 0:
                    for ti, tt in enumerate(tts):
                        norm_tile(tt, "norm_ffn", l, work, (hn, HN, ti))
                def flush(pend):
                    jj, Y0, BY0, Yb_, BYb_ = pend
                    Gt, BGt = Gr.next()
                    op("act", lambda e: e.activation(out=Gt[:], in_=Y0[:], func=AF.Gelu_apprx_tanh), reads=[BY0], writes=[BGt])
                    op("dve", lambda e: e.tensor_tensor(out=aT[:, jj, :], in0=Gt[:], in1=Yb_[:], op=ALU.mult),
                       reads=[BGt, BYb_], writes=[Ba[jj]])

                pend = None
                for j in range(FCH):
                    cur = [load_wp("u%d" % gv, wup_d[l], gv * FCH, j, 8) for gv in range(2)]
                    if j % 2 == 0:
                        if j + 2 < FCH:
                            for gv in range(2):
                                prefetch_wp("u%d" % gv, wup_d[l], gv * FCH, j + 2, 8)
                        else:
                            prefetch_wp("dA", wdown_d[l], 0, 0, 11, (0, 11))
                            prefetch_wp("dB", wdown_d[l], 0, 0, 11, (11, 22))
                    for gv in range(2):
                        fch = gv * FCH + j
                        w_, Bw_ = cur[gv]
                        U, BU = Ur.next()
                        if hf == 0:
                            op("pool", lambda e, U=U: e.memset(U[:, 0:2], 0.0), writes=[BU])
                        else:
                            op("pool", lambda e, U=U, fch=fch: e.tensor_copy(out=U[:, 0:2], in_=carry_f[:, fch, :]), reads=[Bcf[fch]], writes=[BU])
                        for ti, tt in enumerate(tts):
                            bank = next_bank()
                            matmul_group(bank, slice(0, 512), [(w_[:, k, :], hn[:, k, ts(ti)]) for k in range(8)],
                                         [Bw_] + [HN[k][ti] for k in range(8)])
                            op("act", lambda e, U=U, ti=ti, bank=bank: e.activation(out=U[:, 2 + ti * 512:2 + (ti + 1) * 512], in_=psum[bank][:, :], func=AF.Identity),
                               reads=[PB[bank]], writes=[BU])
                        if hf + 1 < NH:
                            op("pool", lambda e, U=U, fch=fch: e.tensor_copy(out=carry_f[:, fch, :], in_=U[:, TH:TH + 2]), reads=[BU], writes=[Bcf[fch]])
                        Y, BY = Yr.next()
                        cw = lambda k, fch=fch: ppc("ffn_conv_w", (l * 3 + k) * 44 + fch)
                        op("act", lambda e, U=U, Y=Y, fch=fch: e.activation(out=Y[:], in_=U[:, 2:2 + TH], func=AF.Identity,
                                                                           scale=cw(2), bias=ppc("ffn_conv_b", l * 44 + fch)),
                           reads=[BU, Bpp], writes=[BY])
                        op("dve", lambda e, U=U, Y=Y: e.scalar_tensor_tensor(out=Y[:], in0=U[:, 1:1 + TH], scalar=cw(1), in1=Y[:],
                                                                             op0=ALU.mult, op1=ALU.add),
                           reads=[BU, BY, Bpp], writes=[BY])
                        if gv == 0:
                            op("dve", lambda e, U=U, Y=Y: e.scalar_tensor_tensor(out=Y[:], in0=U[:, 0:TH], scalar=cw(0), in1=Y[:],
                                                                                 op0=ALU.mult, op1=ALU.add),
                               reads=[BU, BY, Bpp], writes=[BY])
                            y0 = (Y, BY)
                            if pend is not None:
                                flush(pend)
                                pend = None
                        else:
                            Yb, BYb = Ybr.next()
                            op("dve", lambda e, U=U, Y=Y, Yb=Yb: e.scalar_tensor_tensor(out=Yb[:], in0=U[:, 0:TH], scalar=cw(0), in1=Y[:],
                                                                                       op0=ALU.mult, op1=ALU.add),
                               reads=[BU, BY, Bpp], writes=[BYb])
                            pend = (j, y0[0], y0[1], Yb, BYb)
                flush(pend)
                for dc in range(NC8):
                    wdA, BwdA = load_wp("dA", wdown_d[l], 0, dc, 11, (0, 11))
                    wdB, BwdB = load_wp("dB", wdown_d[l], 0, dc, 11, (11, 22))
                    if dc % 2 == 0:
                        if dc + 2 < NC8:
                            prefetch_wp("dA", wdown_d[l], 0, dc + 2, 11, (0, 11))
                            prefetch_wp("dB", wdown_d[l], 0, dc + 2, 11, (11, 22))
                        elif hf + 1 < NH:
                            for gv in range(2):
                                prefetch_wp("u%d" % gv, wup_d[l], gv * FCH, 0, 8)
                    for ti, tt in enumerate(tts):
                        bank = next_bank()
                        matmul_group(bank, slice(0, 512),
                                     [((wdA[:, k, :] if k < 11 else wdB[:, k - 11, :]), aT[:, k, ti * 512:(ti + 1) * 512]) for k in range(FCH)],
                                     [BwdA, BwdB] + Ba)
                        resid_add(bank, dc, tt)
                    if dc == 1 and hf + 1 < NH:
                        for ti2 in range(NTH):
                            norm_tile((hf + 1) * NTH + ti2, "norm_ffn", l, work, (hn, HN, ti2))
            S_.barrier()

    def ple(l, sq_i):
        with contextlib.ExitStack() as st:
            work = norm_work(st)
            pT = Ring([sb("pT%d" % i, [128, 2, 512], BF16, st) for i in range(2)])
            Sg = Ring([sb("Sg%d" % i, [128, 512], F32, st) for i in range(2)])
            hn, HN = hn_alloc(st, NTH)
            pst = [sb("pst%d" % i, [128, PLE], F32, st) for i in range(4)]
            pstB = [Buf() for _ in range(4)]
            S_.sync_from("sp")
            bank_ring[0] = list(range(8))
            for hf in range(NH):
                tts = list(range(hf * NTH, (hf + 1) * NTH))
                pts = []
                for ti, tt in enumerate(tts):
                    norm_tile(tt, "norm_ple", l, work, (hn, HN, ti))
                    for q in range(4):
                        S_.dma("sp", pst[q][:], p_d[l, sq_i, tt * 512 + q * 128: tt * 512 + (q + 1) * 128, :], pstS[q], writes=[pstB[q]])
                    pt, Bpt = pT.next()
                    for ec in range(2):
                        bank = next_bank()
                        for q in range(4):
                            op("pe", lambda e, q=q, ec=ec, bank=bank: e.transpose(out=psum[bank][:, q * 128:(q + 1) * 128],
                                                                                 in_=pst[q][:, ec * 128:(ec + 1) * 128], identity=ident),
                               reads=[pstB[q], Bcst], writes=[PB[bank]])
                        op("act", lambda e, pt=pt, ec=ec, bank=bank: e.activation(out=pt[:, ec, :], in_=psum[bank][:, :], func=AF.Identity),
                           reads=[PB[bank]], writes=[Bpt])
                    pts.append((pt, Bpt))
                for dc in range(NC8):
                    wg, Bwg = load_wp("pg", wpg_d[l], 0, dc, 8)
                    wp, Bwp = load_wp("pp", wpp_d[l], 0, dc, 2)
                    if dc % 2 == 0:
                        ndc = dc + 2 if dc + 2 < NC8 else (0 if hf + 1 < NH else None)
                        if ndc is not None:
                            prefetch_wp("pg", wpg_d[l], 0, ndc, 8)
                            prefetch_wp("pp", wpp_d[l], 0, ndc, 2)
                    for ti, tt in enumerate(tts):
                        pt, Bpt = pts[ti]
                        bg = next_bank()
                        matmul_group(bg, slice(0, 512), [(wg[:, k, :], hn[:, k, ts(ti)]) for k in range(8)],
                                     [Bwg] + [HN[k][ti] for k in range(8)])
                        bp = next_bank()
                        matmul_group(bp, slice(0, 512), [(wp[:, k, :], pt[:, k, :]) for k in range(2)], [Bwp, Bpt])
                        sg, Bsg = Sg.next()
                        op("act", lambda e, sg=sg, bg=bg: e.activation(out=sg[:], in_=psum[bg][:, :], func=AF.Sigmoid), reads=[PB[bg]], writes=[Bsg])
                        op("dve", lambda e, sg=sg, bp=bp: e.tensor_tensor(out=sg[:], in0=sg[:], in1=psum[bp][:, :], op=ALU.mult),
                           reads=[Bsg, PB[bp]], writes=[Bsg])
                        op("dve", lambda e, sg=sg, dc=dc, tt=tt: e.tensor_tensor(out=xT[:, dc, ts(tt)], in0=xT[:, dc, ts(tt)], in1=sg[:], op=ALU.add),
                           reads=[Bsg, X[dc][tt]], writes=[X[dc][tt]])
            S_.barrier()

    for sq_i in range(NSEQ):
        load_x(sq_i)
        for l in range(DEPTH):
            if l % 2 == 0:
                attention(l)
            else:
                rglru(l)
            ffn(l)
            ple(l, sq_i)
        final_store(sq_i)
    es.close()
    return nc


def prep_shared(inp, depth):
    pp = PP(depth)
    nr = max(depth // 2, 1)
    tab = np.zeros((128, pp.n), np.float32)

    def put(name, arr):
        tab[:, pp.off[name]:pp.off[name] + arr.shape[1]] = arr

    put("norm_mix", _cols(inp["norm_mix"], 8))
    put("norm_ffn", _cols(inp["norm_ffn"], 8))
    put("norm_ple", _cols(inp["norm_ple"], 8))
    put("norm_final", _cols(inp["norm_final"], 8))
    put("ffn_conv_w", _cols(inp["ffn_conv_w"], 44))
    put("ffn_conv_b", _cols(inp["ffn_conv_b"], 44))
    put("rnn_conv_w", _cols(inp["rnn_conv_w"], 10))
    put("rnn_conv_b", _cols(inp["rnn_conv_b"], 10))
    put("b_gate_a", _cols(inp["rnn_b_gate_a"], 10))
    put("b_gate_x", _cols(inp["rnn_b_gate_x"], 10))
    put("lru", _cols(inp["rnn_lru_param"], 10))
    shared = {
        "pp": tab,
        "cst": make_consts(),
        "wband_a": _band(np.asarray(inp["rnn_w_gate_a"], np.float32)),
        "wband_x": _band(np.asarray(inp["rnn_w_gate_x"], np.float32)),
    }
    for k in ("attn_w_qkv", "attn_w_o", "rnn_w_in", "rnn_w_out", "ffn_w_up", "ffn_w_down", "ple_w_gate", "ple_w_proj"):
        shared[k] = np.ascontiguousarray(np.asarray(inp[k], np.float32))
    return shared


def run(inp, n_cores, depth):
    x = np.asarray(inp["x"], np.float32)
    p = np.asarray(inp["p"], np.float32)
    B, S, _ = x.shape
    assert B % n_cores == 0
    nseq = B // n_cores
    shared = prep_shared(inp, depth)
    nc = build(nseq, S, depth)
    in_maps = []
    for i in range(n_cores):
        m = dict(shared)
        m["x"] = np.ascontiguousarray(x[i * nseq:(i + 1) * nseq])
        m["p"] = np.ascontiguousarray(p[:, i * nseq:(i + 1) * nseq])
        in_maps.append(m)
    res = run_bass_kernel_spmd(nc, in_maps, core_ids=list(range(n_cores)))
    return np.concatenate([np.asarray(r["out"], np.float32) for r in res.results], axis=0)


def kernel(**inputs):
    return run(inputs, N_CORES, 4)
```
